# Optimizing a Trainium2 kernel written in Bass

```python
import jax
import jax.numpy as jnp
from jax import lax
import numpy as np

D_MODEL = 1024
BATCH = 2
SEQ = 8192
DEPTH = 4
DEC_BATCH = 128
DEC_SEQ = 4
PAST_LEN = 2048
PAGE_SIZE = 128

N_A = DEPTH // 2
N_B = DEPTH - N_A
PLE_DIM = 256
D_FF = ((8 * D_MODEL // 3 + 127) // 128) * 128
RET_HEADS = 4
RET_DK = D_MODEL // RET_HEADS
RET_DV = 2 * RET_DK
RET_QK = RET_HEADS * RET_DK
RET_VDIM = RET_HEADS * RET_DV
RET_CHUNK = 128
ROPE_BASE = 10000.0
SB_HEADS = 16
SB_HD = D_MODEL // SB_HEADS
SB_BIAS_INIT = -8.0
Q_BLOCK = 128
EPS = 1e-6

kernel_name = 'yoco_retnet_stickbreaking_step'


def rmsnorm(x, g):
    xf = x.astype(jnp.float32)
    y = xf * lax.rsqrt(jnp.mean(xf * xf, axis=-1, keepdims=True) + EPS)
    return (y * g.astype(jnp.float32)).astype(x.dtype)


def swiglu(h, w_gate, w_up, w_down):
    return (jax.nn.silu(h @ w_gate) * (h @ w_up)) @ w_down


def rope(x, pos):
    half = x.shape[-1] // 2
    inv_freq = ROPE_BASE ** (-jnp.arange(half, dtype=jnp.float32) / half)
    ang = pos.astype(jnp.float32)[:, None] * inv_freq[None, :]
    cos = jnp.cos(ang)[None, :, None, :]
    sin = jnp.sin(ang)[None, :, None, :]
    xf = x.astype(jnp.float32)
    x1, x2 = xf[..., :half], xf[..., half:]
    return jnp.concatenate([x1 * cos - x2 * sin, x2 * cos + x1 * sin], axis=-1)


def retention_log_decay():
    return jnp.log(1.0 - 2.0 ** (-5.0 - jnp.arange(RET_HEADS, dtype=jnp.float32)))


def retention_chunk(S, q, k, v, lg):
    C = q.shape[1]
    idx = jnp.arange(C, dtype=jnp.float32)
    diff = idx[:, None] - idx[None, :]
    decay = jnp.where(diff[None] >= 0, jnp.exp(jnp.maximum(diff, 0.0)[None] * lg[:, None, None]), 0.0)
    s = jnp.einsum('bihd,bjhd->bhij', q, k) * decay[None]
    o = jnp.einsum('bhij,bjhv->bihv', s, v)
    q_dec = q * jnp.exp((idx + 1.0)[:, None] * lg[None, :])[None, :, :, None]
    o = o + jnp.einsum('bihd,bhdv->bihv', q_dec, S)
    k_dec = k * jnp.exp((C - 1.0 - idx)[:, None] * lg[None, :])[None, :, :, None]
    S_new = jnp.exp(C * lg)[None, :, None, None] * S + jnp.einsum('bjhd,bjhv->bhdv', k_dec, v)
    return o, S_new


def retention(q, k, v, S0, lg):
    B, T = q.shape[0], q.shape[1]
    if T <= RET_CHUNK or T % RET_CHUNK != 0:
        return retention_chunk(S0, q, k, v, lg)
    n = T // RET_CHUNK

    def to_chunks(a):
        return jnp.moveaxis(a.reshape(B, n, RET_CHUNK, *a.shape[2:]), 1, 0)

    def step(S, inp):
        o, S = retention_chunk(S, inp[0], inp[1], inp[2], lg)
        return S, o

    S_fin, o = lax.scan(step, S0, (to_chunks(q), to_chunks(k), to_chunks(v)))
    o = jnp.moveaxis(o, 0, 1).reshape(B, T, RET_HEADS, RET_DV)
    return o, S_fin


def retention_mixer(h, pos, S0, w_in, w_out, g_gn):
    B, T, _ = h.shape
    proj = h @ w_in
    q, k, v, g = jnp.split(proj, [RET_QK, 2 * RET_QK, 2 * RET_QK + RET_VDIM], axis=-1)
    q = rope(q.reshape(B, T, RET_HEADS, RET_DK), pos)
    k = rope(k.reshape(B, T, RET_HEADS, RET_DK), pos) * (RET_DK ** -0.5)
    v = v.reshape(B, T, RET_HEADS, RET_DV).astype(jnp.float32)
    o, S = retention(q, k, v, S0.astype(jnp.float32), retention_log_decay())
    mu = jnp.mean(o, axis=-1, keepdims=True)
    var = jnp.mean(jnp.square(o - mu), axis=-1, keepdims=True)
    o = (o - mu) * lax.rsqrt(var + EPS) * g_gn.astype(jnp.float32).reshape(RET_HEADS, RET_DV)
    o = (jax.nn.silu(g.astype(jnp.float32)) * o.reshape(B, T, RET_VDIM)).astype(h.dtype)
    return o @ w_out, S


def sb_block(q, k, v, q_pos, k_pos, bias):
    z = jnp.einsum('bqhd,bkhd->bhqk', q, k).astype(jnp.float32) * (SB_HD ** -0.5)
    z = z + bias.astype(jnp.float32)[None, :, None, None]
    mask = (k_pos[None, :] < q_pos[:, None])[None, None]
    log_1m = jnp.where(mask, jax.nn.log_sigmoid(-z), 0.0)
    later = lax.cumsum(log_1m, axis=3, reverse=True) - log_1m
    A = jnp.where(mask, jnp.exp(jax.nn.log_sigmoid(z) + later), 0.0)
    return jnp.einsum('bhqk,bkhd->bqhd', A.astype(v.dtype), v)


def stick_breaking(q, k, v, q_pos, k_pos, bias):
    B, T = q.shape[0], q.shape[1]
    if T <= Q_BLOCK or T % Q_BLOCK != 0:
        return sb_block(q, k, v, q_pos, k_pos, bias)
    n = T // Q_BLOCK
    qb = jnp.moveaxis(q.reshape(B, n, Q_BLOCK, SB_HEADS, SB_HD), 1, 0)
    pb = q_pos.reshape(n, Q_BLOCK)
    o = lax.map(lambda a: sb_block(a[0], k, v, a[1], k_pos, bias), (qb, pb))
    return jnp.moveaxis(o, 0, 1).reshape(B, T, SB_HEADS, SB_HD)


def sb_mixer(h, pos, keys, vals, key_pos, w_q, w_o, bias):
    B, T, _ = h.shape
    q = (h @ w_q).reshape(B, T, SB_HEADS, SB_HD)
    o = stick_breaking(q, keys, vals, pos, key_pos, bias)
    return o.reshape(B, T, SB_HEADS * SB_HD) @ w_o


def trunk(x, p, pos, ret_state, past_k, past_v, past_pos,
          g_norm, w_ff1_gate, w_ff1_up, w_ff1_down, w_ff2_gate, w_ff2_up, w_ff2_down,
          w_ple_up, w_ple_gate, g_ple, w_ret_in, w_ret_out, g_ret_gn,
          g_kv, w_kv, w_sb_q, w_sb_out, b_sb, g_final):
    B, T, _ = x.shape
    new_states = []
    k_new = v_new = None
    keys = vals = key_pos = None
    for i in range(DEPTH):
        x = x + 0.5 * swiglu(rmsnorm(x, g_norm[i, 0]), w_ff1_gate[i], w_ff1_up[i], w_ff1_down[i])
        h = rmsnorm(x, g_norm[i, 1])
        if i < N_A:
            o, S = retention_mixer(h, pos, ret_state[i], w_ret_in[i], w_ret_out[i], g_ret_gn[i])
            new_states.append(S)
        else:
            o = sb_mixer(h, pos, keys, vals, key_pos, w_sb_q[i - N_A], w_sb_out[i - N_A], b_sb[i - N_A])
        x = x + o
        x = x + 0.5 * swiglu(rmsnorm(x, g_norm[i, 2]), w_ff2_gate[i], w_ff2_up[i], w_ff2_down[i])
        x = x + (p[i] @ w_ple_up[i]) * jax.nn.sigmoid(rmsnorm(x, g_ple[i]) @ w_ple_gate[i])
        if i == N_A - 1:
            kv = rmsnorm(x, g_kv) @ w_kv
            k_new, v_new = jnp.split(kv, 2, axis=-1)
            k_new = k_new.reshape(B, T, SB_HEADS, SB_HD)
            v_new = v_new.reshape(B, T, SB_HEADS, SB_HD)
            if past_k is None:
                keys, vals, key_pos = k_new, v_new, pos
            else:
                keys = jnp.concatenate([past_k.astype(k_new.dtype), k_new], axis=1)
                vals = jnp.concatenate([past_v.astype(v_new.dtype), v_new], axis=1)
                key_pos = jnp.concatenate([past_pos, pos], axis=0)
    y = rmsnorm(x, g_final)
    return y, jnp.stack(new_states), k_new, v_new


def setup_inputs(seed: int = 0) -> dict:
    key = jax.random.key(seed)
    ks = jax.random.split(key, 32)
    n_pages = PAST_LEN // PAGE_SIZE
    n_used = DEC_BATCH * n_pages
    n_pool = n_used + max(1, n_used // 4)

    def nrm(k, shape, scale):
        return jax.random.normal(k, shape, jnp.float32) * scale

    page_table = jax.random.permutation(ks[7], n_pool)[:n_used].reshape(DEC_BATCH, n_pages).astype(jnp.int32)
    ret_in_width = 2 * RET_QK + 2 * RET_VDIM
    sb_w = SB_HEADS * SB_HD
    return {
        'x_prompt': nrm(ks[0], (BATCH, SEQ, D_MODEL), 1.0),
        'x_sample': nrm(ks[1], (DEC_BATCH, DEC_SEQ, D_MODEL), 1.0),
        'p_prompt': nrm(ks[2], (DEPTH, BATCH, SEQ, PLE_DIM), 1.0),
        'p_sample': nrm(ks[3], (DEPTH, DEC_BATCH, DEC_SEQ, PLE_DIM), 1.0),
        'state_ret': nrm(ks[4], (N_A, DEC_BATCH, RET_HEADS, RET_DK, RET_DV), 0.5),
        'cache_k': nrm(ks[5], (n_pool, PAGE_SIZE, SB_HEADS, SB_HD), 1.0),
        'cache_v': nrm(ks[6], (n_pool, PAGE_SIZE, SB_HEADS, SB_HD), 1.0),
        'page_table': page_table,
        'g_norm': 1.0 + nrm(ks[8], (DEPTH, 3, D_MODEL), 0.02),
        'w_ff1_gate': nrm(ks[9], (DEPTH, D_MODEL, D_FF), D_MODEL ** -0.5),
        'w_ff1_up': nrm(ks[10], (DEPTH, D_MODEL, D_FF), D_MODEL ** -0.5),
        'w_ff1_down': nrm(ks[11], (DEPTH, D_FF, D_MODEL), D_FF ** -0.5),
        'w_ff2_gate': nrm(ks[12], (DEPTH, D_MODEL, D_FF), D_MODEL ** -0.5),
        'w_ff2_up': nrm(ks[13], (DEPTH, D_MODEL, D_FF), D_MODEL ** -0.5),
        'w_ff2_down': nrm(ks[14], (DEPTH, D_FF, D_MODEL), D_FF ** -0.5),
        'w_ple_up': nrm(ks[15], (DEPTH, PLE_DIM, D_MODEL), PLE_DIM ** -0.5),
        'w_ple_gate': nrm(ks[16], (DEPTH, D_MODEL, D_MODEL), D_MODEL ** -0.5),
        'g_ple': 1.0 + nrm(ks[17], (DEPTH, D_MODEL), 0.02),
        'w_ret_in': nrm(ks[18], (N_A, D_MODEL, ret_in_width), D_MODEL ** -0.5),
        'w_ret_out': nrm(ks[19], (N_A, RET_VDIM, D_MODEL), RET_VDIM ** -0.5),
        'g_ret_gn': 1.0 + nrm(ks[20], (N_A, RET_VDIM), 0.02),
        'g_kv': 1.0 + nrm(ks[21], (D_MODEL,), 0.02),
        'w_kv': nrm(ks[22], (D_MODEL, 2 * sb_w), D_MODEL ** -0.5),
        'w_sb_q': nrm(ks[23], (N_B, D_MODEL, sb_w), D_MODEL ** -0.5),
        'w_sb_out': nrm(ks[24], (N_B, sb_w, D_MODEL), sb_w ** -0.5),
        'b_sb': SB_BIAS_INIT + nrm(ks[26], (N_B, SB_HEADS), 0.5),
        'g_final': 1.0 + nrm(ks[25], (D_MODEL,), 0.02),
    }


def reference(x_prompt, x_sample, p_prompt, p_sample, state_ret, cache_k, cache_v, page_table,
              g_norm, w_ff1_gate, w_ff1_up, w_ff1_down, w_ff2_gate, w_ff2_up, w_ff2_down,
              w_ple_up, w_ple_gate, g_ple, w_ret_in, w_ret_out, g_ret_gn,
              g_kv, w_kv, w_sb_q, w_sb_out, b_sb, g_final):
    weights = (g_norm, w_ff1_gate, w_ff1_up, w_ff1_down, w_ff2_gate, w_ff2_up, w_ff2_down,
               w_ple_up, w_ple_gate, g_ple, w_ret_in, w_ret_out, g_ret_gn,
               g_kv, w_kv, w_sb_q, w_sb_out, b_sb, g_final)
    n_prompt, t_prompt = x_prompt.shape[0], x_prompt.shape[1]
    ret_zero = jnp.zeros((N_A, n_prompt, RET_HEADS, RET_DK, RET_DV), jnp.float32)
    pos_p = jnp.arange(t_prompt, dtype=jnp.int32)
    y_p, S_p, k_p, v_p = trunk(x_prompt, p_prompt, pos_p, ret_zero, None, None, None, *weights)
    n_dec, t_dec = x_sample.shape[0], x_sample.shape[1]
    n_pages = page_table.shape[1]
    past_len = n_pages * PAGE_SIZE
    past_k = cache_k[page_table].reshape(n_dec, past_len, SB_HEADS, SB_HD)
    past_v = cache_v[page_table].reshape(n_dec, past_len, SB_HEADS, SB_HD)
    past_pos = jnp.arange(past_len, dtype=jnp.int32)
    pos_s = past_len + jnp.arange(t_dec, dtype=jnp.int32)
    y_s, S_s, k_s, v_s = trunk(x_sample, p_sample, pos_s, state_ret, past_k, past_v, past_pos, *weights)
    return (y_p, y_s, S_p.astype(x_prompt.dtype), S_s.astype(state_ret.dtype), k_p, v_p, k_s, v_s)
```

```python
import math
import numpy as np
import ml_dtypes
import concourse.bass as bass
import concourse.mybir as mybir
from concourse.bass_utils import run_bass_kernel_spmd

F32 = mybir.dt.float32
BF16 = mybir.dt.bfloat16
I32 = mybir.dt.int32
AF = mybir.ActivationFunctionType
ALU = mybir.AluOpType
NEG = -30000.0
ENGS = ("pe", "act", "dve", "pool", "sp")
DTSZ = {F32: 4, BF16: 2, I32: 4}


class Cfg:
    def __init__(s, **kw):
        s.NC = 8; s.CPS = 4; s.D = 1024; s.DFF = 2816; s.PLE = 256; s.SEG = 2048
        s.SPC = 16; s.DSEQ = 4; s.NPG = 16; s.PAGE = 128; s.NPOOL = 2560
        s.DEPTH = 4; s.NA = 2; s.HD = 64; s.DK = 256; s.EPS = 1e-6; s.ROPE_BASE = 10000.0
        s.SB_BIAS = -8.0
        s.__dict__.update(kw)
        s.KC = s.D // 128; s.JC = s.DFF // 128; s.PC = s.PLE // 128
        s.RH = s.D // s.DK; s.DV = 2 * s.DK; s.RQK = s.RH * s.DK; s.RV = s.RH * s.DV; s.VC = s.RV // 128
        s.SBH = s.D // s.HD; s.TS = s.SPC * s.DSEQ; s.T = s.SEG + s.TS
        s.BLK = min(512, s.SEG); s.NB = s.SEG // s.BLK
        s.RBLK = min(256, s.SEG); s.NRB = s.SEG // s.RBLK
        s.BATCH = s.NC // s.CPS; s.PAST = s.NPG * s.PAGE
        s.NKB = s.SEG // 128
        assert s.DK == 256 and s.DV == 512


class Buf:
    __slots__ = ("name", "w", "r")

    def __init__(s, name):
        s.name = name; s.w = None; s.r = []


class Prog:
    def __init__(s, nc):
        s.nc = nc
        s.ops = {e: [] for e in ENGS}
        s.cnt = {e: 0 for e in ENGS}
        s.known = {e: {} for e in ENGS}
        s.pending = {e: {} for e in ENGS}
        s.dcnt = {}
        s.kinc = {}
        s.sems = {}
        s.nbuf = 0

    def buf(s, name):
        return Buf(name)

    def sem(s, key):
        if key not in s.sems:
            s.sems[key] = s.nc.alloc_semaphore("s%d" % len(s.sems))
        return s.sems[key]

    def _val(s, prod):
        if prod[0] == "c":
            return ("E", prod[1]), prod[2] + 1
        return ("D", prod[1]), s.dcnt[prod[1]] * prod[2]

    def add(s, eng, fn, r=(), w=(), dma=None, inc=16):
        need = dict(s.pending[eng]); s.pending[eng] = {}
        prods = []
        for b in r:
            if b.w is not None: prods.append(b.w)
        for b in w:
            if b.w is not None: prods.append(b.w)
            prods.extend(b.r)
        for p in prods:
            if p[0] == "c" and p[1] == "pe" and eng == "pe" and dma is None:
                continue
            k, v = s._val(p)
            if need.get(k, 0) < v: need[k] = v
        waits = []
        kn = s.known[eng]
        for k, v in need.items():
            if kn.get(k, 0) < v:
                kn[k] = v; waits.append((k, v))
        if dma is None:
            me = ("c", eng, s.cnt[eng]); s.cnt[eng] += 1
            sig = (("E", eng), 1)
        else:
            s.dcnt[dma] = s.dcnt.get(dma, 0) + 1
            s.kinc[dma] = inc
            me = ("d", dma, inc)
            sig = (("D", dma), inc)
        s.ops[eng].append((waits, fn, sig))
        for b in r: b.r.append(me)
        for b in w:
            b.w = me; b.r = []
        return me

    def barrier(s):
        snap = {}
        for e in ENGS:
            if s.cnt[e]: snap[("E", e)] = s.cnt[e]
        for k, c in s.dcnt.items():
            snap[("D", k)] = c * s.kinc.get(k, 16)
        for e in ENGS:
            for k, v in snap.items():
                if k == ("E", e) and e == "pe": continue
                if s.pending[e].get(k, 0) < v: s.pending[e][k] = v

    def emit(s):
        nc = s.nc
        s.barrier()
        final = s.pending
        sems = {k: s.sem(k) for k in set(k for e in ENGS for (w, _, sg) in s.ops[e] for k in [sg[0]] + [x[0] for x in w])}
        for e in ENGS:
            for k in final[e]: sems.setdefault(k, s.sem(k))

        def replay(eng, name):
            for (waits, fn, sig) in s.ops[name]:
                for (k, v) in waits: eng.wait_ge(sems[k], v)
                ins = fn(eng)
                ins.then_inc(sems[sig[0]], sig[1])
            kn = s.known[name]
            for k, v in final[name].items():
                if kn.get(k, 0) < v: eng.wait_ge(sems[k], v)

        with nc.Block() as block:
            @block.tensor
            def _(e): replay(e, "pe")

            @block.scalar
            def _(e): replay(e, "act")

            @block.vector
            def _(e): replay(e, "dve")

            @block.gpsimd
            def _(e): replay(e, "pool")

            @block.sync
            def _(e): replay(e, "sp")


class SB:
    BASE = 16512 + 96
    LIMIT = 16512 + 212000

    def __init__(s, nc):
        s.nc = nc; s.off = SB.BASE; s.n = 0

    def alloc(s, name, shape, dt):
        sz = DTSZ[dt]
        for d in shape[1:]: sz *= d
        sz = (sz + 63) // 64 * 64
        assert s.off + sz <= SB.LIMIT, ("SBUF overflow", name, s.off, sz)
        s.n += 1
        t = s.nc.alloc_sbuf_tensor_at("%s_%d" % (name, s.n), list(shape), dt, offset=s.off)
        s.off += sz
        return t

    def mark(s): return s.off

    def reset(s, m): s.off = m


def host_constants(cfg, core):
    c = cfg
    seg = core % c.CPS
    cons = {}
    cons["ident_f"] = np.eye(128, dtype=np.float32)
    ib = np.zeros((128, 4, 128), dtype=np.float32)
    ib[:, 0] = np.eye(128)
    ib[:, 1] = 1.0
    kk = np.arange(128)
    ib[:, 2] = -(kk[:, None] >= kk[None, :]).astype(np.float32)
    ib[:, 3] = -1.0
    cons["cb"] = ib.astype(ml_dtypes.bfloat16)
    half = 128
    inv = (c.ROPE_BASE ** (-np.arange(half, dtype=np.float32) / half)).astype(np.float32)
    pos = np.concatenate([seg * c.SEG + np.arange(c.SEG), c.PAST + (np.arange(c.TS) % c.DSEQ)]).astype(np.float32)
    ang = (pos[None, :] * inv[:, None]).astype(np.float32)
    cons["rope"] = np.stack([np.cos(ang), np.sin(ang)]).astype(np.float32)
    lg = np.log(1.0 - 2.0 ** (-5.0 - np.arange(c.RH, dtype=np.float64)))
    j = np.arange(128)
    sc = c.DK ** -0.5
    decT = np.zeros((128, c.RH, 128), np.float32)
    for h in range(c.RH):
        decT[:, h, :] = np.where(j[:, None] <= j[None, :], np.exp(-(j[:, None] + 1.0) * lg[h]) * sc, 0.0)
    cons["decT"] = decT
    js = np.arange(c.TS); jl = js % c.DSEQ; bs = js // c.DSEQ
    decTs = np.zeros((128, c.RH, c.TS), np.float32)
    for h in range(c.RH):
        m = (bs[:, None] == bs[None, :]) & (jl[:, None] <= jl[None, :])
        decTs[:c.TS, h, :] = np.where(m, np.exp(-(jl[:, None] + 1.0) * lg[h]) * sc, 0.0)
    cons["decTs"] = decTs
    RH = c.RH
    rt = np.zeros((128, 8 * RH + c.CPS * RH + c.SPC + 8), np.float32)
    o = 0
    cons_off = {}
    def put(name, arr):
        nonlocal o
        n = arr.shape[1]; rt[:, o:o + n] = arr; cons_off[name] = o; o += n
    put("ci", np.exp((j[:, None] + 1.0) * lg[None, :]))
    put("ci2", np.exp(2.0 * (j[:, None] + 1.0) * lg[None, :]))
    put("kdec", np.exp((127.0 - j[:, None]) * lg[None, :]) * sc)
    put("g128", np.tile(np.exp(128.0 * lg)[None, :], (128, 1)))
    jl128 = np.arange(128) % c.DSEQ
    put("cis", np.exp((jl128[:, None] + 1.0) * lg[None, :]))
    put("cis2", np.exp(2.0 * (jl128[:, None] + 1.0) * lg[None, :]))
    put("kdecs", np.exp((c.DSEQ - 1.0 - jl128[:, None]) * lg[None, :]) * sc)
    put("g4", np.tile(np.exp(float(c.DSEQ) * lg)[None, :], (128, 1)))
    coef = np.zeros((c.CPS, RH))
    for r in range(c.CPS):
        if r < seg:
            coef[r] = np.exp(float(c.SEG) * (seg - 1 - r) * lg)
    put("coef", np.tile(coef.reshape(1, -1), (128, 1)))
    rowm = np.zeros((128, c.SPC))
    for b in range(c.SPC):
        rowm[b * c.DSEQ:(b + 1) * c.DSEQ, b] = 1.0
    put("rowm", rowm)
    put("eps", np.full((128, 1), c.EPS))
    put("one", np.full((128, 1), 1.0))
    segm = np.zeros((128, c.CPS + 1))
    for si in range(c.CPS):
        r = c.CPS - 1 - si
        segm[:, 1 + si] = 0.0 if r < seg else NEG
    put("segm", segm)
    cons["rt"] = rt[:, :o].copy()
    cons["_off"] = cons_off
    nr = c.BLK // 128
    qi = np.arange(c.BLK)
    mn = np.zeros((128, nr, c.BLK), np.float32)
    for r in range(nr):
        mn[:, r, :] = np.where((r * 128 + j[:, None]) >= qi[None, :], NEG, 0.0)
    cons["mneg"] = mn.astype(ml_dtypes.bfloat16)
    sm = np.full((128, c.SPC, c.SBH, c.DSEQ), NEG, np.float32)
    for b in range(c.SPC):
        for jj in range(c.DSEQ):
            for ii in range(c.DSEQ):
                if jj < ii:
                    sm[b * c.DSEQ + jj, b, :, ii] = 0.0
    cons["smask"] = sm.reshape(128, c.SPC, c.SBH * c.DSEQ)
    colm = np.zeros((128, c.SPC, c.TS), np.float32)
    for b in range(c.SPC):
        colm[:, b, b * c.DSEQ:(b + 1) * c.DSEQ] = 1.0
    cons["colm"] = colm.astype(ml_dtypes.bfloat16)
    return cons


CONST_NAMES = ("ident_f", "cb", "rope", "decT", "decTs", "rt", "mneg", "smask", "colm")


def build(cfg, stop_after=None):
    c = cfg
    nc = bass.Bass("TRN2", target_bir_lowering=False)
    P = Prog(nc)
    sb = SB(nc)
    fl = c.__dict__.get("flags", {})
    KC, JC, PC, RH, VC, T, SEG, TS, D = c.KC, c.JC, c.PC, c.RH, c.VC, c.T, c.SEG, c.TS, c.D
    cons0 = host_constants(c, 0)
    OFF = cons0["_off"]

    def din(name, shape, dt=F32):
        return nc.dram_tensor(name, list(shape), dt, kind="ExternalInput")

    def dout(name, shape, dt=F32):
        return nc.dram_tensor(name, list(shape), dt, kind="ExternalOutput")

    def dscr(name, shape, dt=BF16):
        return nc.dram_tensor(name, list(shape), dt)

    x_in = din("x", [T, D])
    p_in = din("p", [c.DEPTH, T, c.PLE])
    st_in = din("state_ret", [c.NA, c.SPC, RH, c.DK, c.DV])
    ck_in = din("cache_k", [c.NPOOL * c.PAGE, D])
    cv_in = din("cache_v", [c.NPOOL * c.PAGE, D])
    pt_in = din("page_table", [c.SPC, c.NPG], I32)
    gv_in = din("gvec", [128, (c.DEPTH * 4 + 2) * KC])
    ggn_in = din("g_ret_gn", [c.NA, c.RV])
    bsb_in = din("b_sb", [c.DEPTH - c.NA, c.SBH])
    W = {}
    W["ff_gate"] = [din("w_ff1_gate", [c.DEPTH, D, c.DFF]), din("w_ff2_gate", [c.DEPTH, D, c.DFF])]
    W["ff_up"] = [din("w_ff1_up", [c.DEPTH, D, c.DFF]), din("w_ff2_up", [c.DEPTH, D, c.DFF])]
    W["ff_down"] = [din("w_ff1_down", [c.DEPTH, c.DFF, D]), din("w_ff2_down", [c.DEPTH, c.DFF, D])]
    W["ple_up"] = din("w_ple_up", [c.DEPTH, c.PLE, D])
    W["ple_gate"] = din("w_ple_gate", [c.DEPTH, D, D])
    W["ret_in"] = din("w_ret_in", [c.NA, D, 2 * c.RQK + 2 * c.RV])
    W["ret_out"] = din("w_ret_out", [c.NA, c.RV, D])
    W["kv"] = din("w_kv", [D, 2 * D])
    W["sb_q"] = din("w_sb_q", [c.DEPTH - c.NA, D, D])
    W["sb_out"] = din("w_sb_out", [c.DEPTH - c.NA, D, D])
    cin = {}
    for nm in CONST_NAMES:
        a = cons0[nm]
        cin[nm] = din("c_" + nm, a.shape, BF16 if a.dtype == ml_dtypes.bfloat16 else F32)

    y_out = dout("y", [T, D])
    sp_out = dout("s_prompt", [c.NA, RH, c.DK, c.DV])
    ss_out = dout("s_sample", [c.NA, c.SPC, RH, c.DK, c.DV])
    k_out = dout("k_new", [T, D])
    v_out = dout("v_new", [T, D])

    xT = sb.alloc("xT", [128, KC, T], F32)
    ident_f = sb.alloc("ident_f", [128, 128], F32)
    cb = sb.alloc("cb", [128, 4, 128], BF16)
    ident_b, ones_b, ntri_b, nones_b = cb[:, 0, :], cb[:, 1, :], cb[:, 2, :], cb[:, 3, :]
    rt = sb.alloc("rt", list(cons0["rt"].shape), F32)
    gv = sb.alloc("gv", [128, (c.DEPTH * 4 + 2) * KC], F32)
    NW = 2
    JW = 256 if JC % 2 == 0 else 128
    OW = 128
    WSLOT = max(KC * 512, 2 * KC * JW, JC * OW, VC * OW)
    wring = [sb.alloc("wr%d" % i, [128, WSLOT], BF16) for i in range(NW)]
    wbufs = [P.buf("wr%d" % i) for i in range(NW)]
    wstate = {"i": 0}
    kTs = sb.alloc("kTs", [128, KC, TS], BF16)
    vs_tok = sb.alloc("vs_tok", [128, D], BF16)
    B_kTs = P.buf("kTs"); B_vs = P.buf("vs")
    persist_mark = sb.mark()

    pbank = [nc.alloc_psum_tensor("pb%d" % i, [128, 512], F32) for i in range(7)]
    pbf = nc.alloc_psum_tensor("pbf", [128, 1024], BF16)
    Bp = [P.buf("pb%d" % i) for i in range(7)]
    Bpbf = P.buf("pbf")

    B_const = P.buf("const")
    B_x = [[P.buf("x%d_%d" % (b, fc)) for fc in range(KC)] for b in range(c.NB + 1)]

    def rtc(name, col, n=1, rows=128):
        o = OFF[name] + col
        return rt[0:rows, o:o + n]

    blocks = [(b * c.BLK, c.BLK) for b in range(c.NB)] + [(SEG, TS)]
    NBT = len(blocks)

    def xbufs(bi):
        return B_x[bi]

    class WS:
        pass

    wscr = {}

    wkey = {"k": "ws0"}

    def mk_w(name, src2d, K, N, CW):
        ws = WS()
        ws.K = K; ws.N = N; ws.CW = CW; ws.Kc = K // 128; ws.NG = N // CW
        ws.t = dscr("ws_" + name, [ws.NG, 128, ws.Kc, CW])
        ws.buf = P.buf("ws_" + name)
        srcv = src2d.rearrange("(kc p) n -> p kc n", p=128)
        for g in range(ws.NG):
            P.add("pool", (lambda e, g=g, ws=ws, srcv=srcv: e.dma_start(out=ws.t[g], in_=srcv[:, :, g * CW:(g + 1) * CW])),
                  r=(), w=(ws.buf,), dma=wkey["k"])
        wscr[name] = ws
        return ws

    def cast_layer_weights(l):
        wkey["k"] = "ws%d" % l
        for wh in range(2):
            mk_w("g%d_%d" % (l, wh), W["ff_gate"][wh][l], D, c.DFF, JW)
            mk_w("u%d_%d" % (l, wh), W["ff_up"][wh][l], D, c.DFF, JW)
            mk_w("d%d_%d" % (l, wh), W["ff_down"][wh][l], c.DFF, D, OW)
            if wh == 0:
                if l < c.NA:
                    mk_w("ri%d" % l, W["ret_in"][l], D, 2 * c.RQK + 2 * c.RV, 512)
                    mk_w("ro%d" % l, W["ret_out"][l], c.RV, D, OW)
                else:
                    mk_w("sq%d" % l, W["sb_q"][l - c.NA], D, D, min(512, D))
                    mk_w("so%d" % l, W["sb_out"][l - c.NA], D, D, OW)
        mk_w("pu%d" % l, W["ple_up"][l], c.PLE, D, min(512, D))
        mk_w("pg%d" % l, W["ple_gate"][l], D, D, min(512, D))
        if l == c.NA - 1:
            mk_w("kv", W["kv"], D, 2 * D, min(512, D))

    p_scr = dscr("p_scr", [c.DEPTH, T, c.PLE])
    B_pscr = P.buf("p_scr")

    def load_w(ws, g):
        i = wstate["i"] % NW; wstate["i"] += 1
        view = wring[i][:, 0:ws.Kc * ws.CW].rearrange("p (k c) -> p k c", c=ws.CW)
        P.add("sp", (lambda e, view=view, ws=ws, g=g: e.dma_start(out=view, in_=ws.t[g])),
              r=(ws.buf,), w=(wbufs[i],), dma="wr%d" % i)
        return view, wbufs[i]

    def load_w2(wa, wb_, g):
        i = wstate["i"] % NW; wstate["i"] += 1
        sz = wa.Kc * wa.CW
        va = wring[i][:, 0:sz].rearrange("p (k c) -> p k c", c=wa.CW)
        vb = wring[i][:, sz:2 * sz].rearrange("p (k c) -> p k c", c=wa.CW)
        P.add("sp", (lambda e: e.dma_start(out=va, in_=wa.t[g])), r=(wa.buf,), w=(wbufs[i],), dma="wr%d" % i)
        P.add("sp", (lambda e: e.dma_start(out=vb, in_=wb_.t[g])), r=(wb_.buf,), w=(wbufs[i],), dma="wr%d" % i)
        return va, vb, wbufs[i]

    def pe_group(fl, r, w):
        def fn(e, fl=fl):
            ins = None
            for f in fl: ins = f(e)
            return ins
        P.add("pe", fn, r=r, w=w)

    def MM(out, lhsT, rhs, start, stop):
        return lambda e: e.matmul(out=out, lhsT=lhsT, rhs=rhs, start=start, stop=stop)

    def TR(out, in_, ident):
        return lambda e: e.transpose(out=out, in_=in_, identity=ident)

    def ACT(out, in_, func, r, w, bias=None, scale=None):
        kw = {}
        if bias is not None: kw["bias"] = bias
        if scale is not None: kw["scale"] = scale
        P.add("act", (lambda e: e.activation(out=out, in_=in_, func=func, **kw)), r=r, w=w)

    def TT(eng, out, in0, in1, op, r, w):
        P.add(eng, (lambda e: e.tensor_tensor(out=out, in0=in0, in1=in1, op=op)), r=r, w=w)

    def TS_(eng, out, in0, s1, s2, op0, op1, r, w):
        if s2 is None:
            P.add(eng, (lambda e: e.tensor_scalar(out=out, in0=in0, scalar1=s1, scalar2=None, op0=op0)), r=r, w=w)
        else:
            P.add(eng, (lambda e: e.tensor_scalar(out=out, in0=in0, scalar1=s1, scalar2=s2, op0=op0, op1=op1)), r=r, w=w)

    def STT(eng, out, in0, scalar, in1, op0, op1, r, w):
        P.add(eng, (lambda e: e.scalar_tensor_tensor(out=out, in0=in0, scalar=scalar, in1=in1, op0=op0, op1=op1)), r=r, w=w)

    def CP(eng, out, in_, r, w):
        if eng == "act":
            P.add(eng, (lambda e: e.copy(out=out, in_=in_)), r=r, w=w)
        else:
            P.add(eng, (lambda e: e.tensor_copy(out=out, in_=in_)), r=r, w=w)

    def DMA(eng, out, in_, r, w, key):
        P.add(eng, (lambda e: e.dma_start(out=out, in_=in_)), r=r, w=w, dma=key)

    def DMAT(out, in_, r, w, key):
        P.add("sp", (lambda e: e.dma_start_transpose(out=out, in_=in_)), r=r, w=w, dma=key)

    pstate = {"i": 0}

    def nbank():
        i = pstate["i"] % 7; pstate["i"] += 1
        return pbank[i], Bp[i]

    cast_layer_weights(0)
    for l in range(c.DEPTH):
        P.add("pool", (lambda e, l=l: e.dma_start(out=p_scr[l], in_=p_in[l])), r=(), w=(B_pscr,), dma="p_scr")
    DMA("sp", ident_f[:, :], cin["ident_f"][:, :], (), (B_const,), "const")
    DMA("sp", cb[:, :, :], cin["cb"][:, :, :], (), (B_const,), "const")
    DMA("sp", rt[:, :], cin["rt"][:, :], (), (B_const,), "const")
    DMA("sp", gv[:, :], gv_in[:, :], (), (B_const,), "const")

    m0 = sb.mark()
    xst = [sb.alloc("xst%d" % i, [128, D], F32) for i in range(2)]
    Bxst = [P.buf("xst%d" % i) for i in range(2)]
    ti = 0
    for bi, (t0, n) in enumerate(blocks):
        for tt in range(0, n, 128):
            rows = min(128, n - tt)
            s = ti % 2; ti += 1
            DMA("sp", xst[s][0:rows, :], x_in[t0 + tt:t0 + tt + rows, :], (), (Bxst[s],), "xst%d" % s)
            for f0 in range(0, KC, 4):
                nf = min(4, KC - f0)
                bank, bb = nbank()
                pe_group([TR(bank[:, k * 128:k * 128 + rows], xst[s][0:rows, (f0 + k) * 128:(f0 + k + 1) * 128], ident_f[0:rows, 0:rows])
                          for k in range(nf)], r=(Bxst[s], B_const), w=(bb,))
                CP("dve" if (f0 // 4) % 2 == 0 else "act",
                   xT[:, f0:f0 + nf, t0 + tt:t0 + tt + rows],
                   bank[:, 0:nf * 128].rearrange("p (k t) -> p k t", t=128)[:, :, 0:rows],
                   (bb,), tuple(B_x[bi][f0:f0 + nf]))
    sb.reset(m0)
    P.barrier()

    def rmsnorm(bi, t0, n, gcol, hT, hbuf, work, out_f32=False):
        sq, Bsq, lnv, Blnv, rstd, Brstd = work
        bank, bb = nbank()
        for fc in range(KC):
            s = fc % 2
            if fc % 2 == 0:
                ACT(sq[s][:, 0:n], xT[:, fc, t0:t0 + n], AF.Square, (B_x[bi][fc],), (Bsq[s],))
            else:
                TT("pool", sq[s][:, 0:n], xT[:, fc, t0:t0 + n], xT[:, fc, t0:t0 + n], ALU.mult, (B_x[bi][fc],), (Bsq[s],))
            pe_group([MM(bank[:, 0:n], ones_b, sq[s][:, 0:n], fc == 0, fc == KC - 1)], r=(Bsq[s], B_const), w=(bb,))
        ACT(lnv[:, 0:n], bank[:, 0:n], AF.Ln, (bb, B_const), (Blnv,), bias=rtc("eps", 0), scale=1.0 / D)
        ACT(rstd[:, 0:n], lnv[:, 0:n], AF.Exp, (Blnv,), (Brstd,), scale=-0.5)
        for fc in range(KC):
            STT("dve", hT[:, fc, 0:n], xT[:, fc, t0:t0 + n], gv[:, gcol + fc:gcol + fc + 1], rstd[:, 0:n],
                ALU.mult, ALU.mult, (B_x[bi][fc], Brstd, B_const), (hbuf,))

    def norm_work():
        sq = [sb.alloc("sq%d" % i, [128, 512], BF16) for i in range(2)]
        Bsq = [P.buf("sq%d" % i) for i in range(2)]
        lnv = sb.alloc("lnv", [128, 512], F32); rstd = sb.alloc("rstd", [128, 512], F32)
        return (sq, Bsq, lnv, P.buf("lnv"), rstd, P.buf("rstd"))

    def gcol_norm(l, i): return (l * 3 + i) * KC
    def gcol_ple(l): return (c.DEPTH * 3 + l) * KC
    GCOL_KV = (c.DEPTH * 4) * KC
    GCOL_FIN = (c.DEPTH * 4 + 1) * KC

    def ffn_groups():
        groups = []; cur = []; tot = 0
        for bi, (t0, n) in enumerate(blocks):
            if tot + n > 1100 and cur:
                groups.append(cur); cur = []; tot = 0
            cur.append((bi, t0, n)); tot += n
        groups.append(cur)
        return groups

    def ffn(l, wh):
        m = sb.mark()
        wk = norm_work()
        groups = ffn_groups()
        gmax = max(sum(n for (_, _, n) in g) for g in groups)
        hT = sb.alloc("hT", [128, KC, gmax], BF16)
        act = sb.alloc("act", [128, JC, gmax], BF16)
        sg = [sb.alloc("sg%d" % i, [128, 512], F32) for i in range(2)]
        Bsg = [P.buf("sg%d" % i) for i in range(2)]
        wg, wu, wd = wscr["g%d_%d" % (l, wh)], wscr["u%d_%d" % (l, wh)], wscr["d%d_%d" % (l, wh)]
        for grp in groups:
            Bh = [P.buf("hT%d" % k) for k in range(len(grp))]
            Bact = [[P.buf("act") for _ in range(JC)] for _ in grp]
            offs = []; o = 0
            for k, (bi, t0, n) in enumerate(grp):
                offs.append(o)
                rmsnorm(bi, t0, n, gcol_norm(l, 0 if wh == 0 else 2), hT[:, :, o:o + n], Bh[k], wk)
                o += n
            si = 0
            for jg in range(wg.NG):
                gt, ut, gb = load_w2(wg, wu, jg)
                ub = gb
                for jj in range(JW // 128):
                    j = jg * (JW // 128) + jj
                    for k, (bi, t0, n) in enumerate(grp):
                        o = offs[k]
                        b1, bb1 = nbank(); b2, bb2 = nbank()
                        pe_group([MM(b1[:, 0:n], gt[:, kc, jj * 128:(jj + 1) * 128], hT[:, kc, o:o + n], kc == 0, kc == KC - 1) for kc in range(KC)],
                                 r=(gb, Bh[k]), w=(bb1,))
                        pe_group([MM(b2[:, 0:n], ut[:, kc, jj * 128:(jj + 1) * 128], hT[:, kc, o:o + n], kc == 0, kc == KC - 1) for kc in range(KC)],
                                 r=(ub, Bh[k]), w=(bb2,))
                        s = si % 2; si += 1
                        ACT(sg[s][:, 0:n], b1[:, 0:n], AF.Silu, (bb1,), (Bsg[s],))
                        TT("dve", act[:, j, o:o + n], sg[s][:, 0:n], b2[:, 0:n], ALU.mult, (Bsg[s], bb2), (Bact[k][j],))
            for og in range(wd.NG):
                dt_, db = load_w(wd, og)
                for oo in range(OW // 128):
                    oc = og * (OW // 128) + oo
                    for k, (bi, t0, n) in enumerate(grp):
                        o = offs[k]
                        b1, bb1 = nbank()
                        pe_group([MM(b1[:, 0:n], dt_[:, j, oo * 128:(oo + 1) * 128], act[:, j, o:o + n], j == 0, j == JC - 1) for j in range(JC)],
                                 r=(db,) + tuple(Bact[k]), w=(bb1,))
                        STT("dve", xT[:, oc, t0:t0 + n], b1[:, 0:n], 0.5, xT[:, oc, t0:t0 + n], ALU.mult, ALU.add,
                            (bb1, B_x[bi][oc]), (B_x[bi][oc],))
        sb.reset(m)
        P.barrier()

    def ple(l):
        m = sb.mark()
        wk = norm_work()
        hT = sb.alloc("hT", [128, KC, 512], BF16); Bh = P.buf("hT")
        pT = sb.alloc("pT", [128, PC, 512], BF16); BpT = P.buf("pT")
        sgt = sb.alloc("sgt", [128, 512], F32); Bsgt = P.buf("sgt")
        upt = sb.alloc("upt", [128, 512], F32); Bupt = P.buf("upt")
        wu, wg = wscr["pu%d" % l], wscr["pg%d" % l]
        for bi, (t0, n) in enumerate(blocks):
            rmsnorm(bi, t0, n, gcol_ple(l), hT, Bh, wk)
            for pc in range(PC):
                DMAT(pT[:, pc, 0:n], p_scr[l, t0:t0 + n, pc * 128:(pc + 1) * 128], (B_pscr,), (BpT,), "pT")
            CWp = wu.CW
            for g in range(wu.NG):
                ut, ub = load_w(wu, g)
                gt, gb = load_w(wg, g)
                for oo in range(CWp // 128):
                    oc = g * (CWp // 128) + oo
                    b1, bb1 = nbank(); b2, bb2 = nbank()
                    pe_group([MM(b1[:, 0:n], ut[:, pc, oo * 128:(oo + 1) * 128], pT[:, pc, 0:n], pc == 0, pc == PC - 1) for pc in range(PC)],
                             r=(ub, BpT), w=(bb1,))
                    pe_group([MM(b2[:, 0:n], gt[:, kc, oo * 128:(oo + 1) * 128], hT[:, kc, 0:n], kc == 0, kc == KC - 1) for kc in range(KC)],
                             r=(gb, Bh), w=(bb2,))
                    ACT(sgt[:, 0:n], b2[:, 0:n], AF.Sigmoid, (bb2,), (Bsgt,))
                    TT("dve", upt[:, 0:n], sgt[:, 0:n], b1[:, 0:n], ALU.mult, (Bsgt, bb1), (Bupt,))
                    TT("dve", xT[:, oc, t0:t0 + n], xT[:, oc, t0:t0 + n], upt[:, 0:n], ALU.add, (Bupt, B_x[bi][oc]), (B_x[bi][oc],))
        sb.reset(m)
        P.barrier()


    groups_rep = [[sq * c.CPS + r for r in range(c.CPS)] for sq in range(c.BATCH)]

    def retention(l):
        m = sb.mark()
        wk = norm_work()
        RB = c.RBLK
        wi, wo = wscr["ri%d" % l], wscr["ro%d" % l]
        NQG = c.RQK // 512
        H2 = 2 * RH
        hT = sb.alloc("r_hT", [128, KC, RB], BF16); Bh = P.buf("r_hT")
        qkT = sb.alloc("r_qkT", [128, 2 * H2, RB], BF16); Bqk = [P.buf("r_qk%d" % i) for i in range(2 * H2)]
        cos = sb.alloc("r_cos", [128, RB], F32); sin = sb.alloc("r_sin", [128, RB], F32)
        Bcos = P.buf("cos"); Bsin = P.buf("sin")
        tmp = [sb.alloc("r_t%d" % i, [128, RB], F32) for i in range(4)]; Bt = [P.buf("r_t%d" % i) for i in range(4)]
        NCH = RB // 128
        v_tok = sb.alloc("r_v", [128, NCH, c.RV], BF16); Bv = [[P.buf("r_v") for _ in range(RH)] for _ in range(NCH)]
        sg_tok = sb.alloc("r_sg", [128, NCH, c.RV], BF16); Bsgt = [[P.buf("r_sg") for _ in range(RH)] for _ in range(NCH)]
        k_tok = sb.alloc("r_kt", [128, c.RQK], BF16); Bkt = P.buf("r_kt")
        sT = sb.alloc("r_sT", [128, RH, 128], BF16); BsT = P.buf("r_sT")
        on = [sb.alloc("r_on%d" % i, [128, 512], F32) for i in range(2)]; Bon = [P.buf("r_on%d" % i) for i in range(2)]
        og = sb.alloc("r_og", [128, c.RV], BF16); Bog = P.buf("r_og")
        ogT = sb.alloc("r_ogT", [128, VC, RB], BF16); BogT = P.buf("r_ogT")
        Sx = [sb.alloc("r_S%d" % i, [128, H2, 512], F32) for i in range(2)]
        BSx = [[P.buf("r_S%d_%d" % (i, hd)) for hd in range(H2)] for i in range(2)]
        S_bf = sb.alloc("r_Sbf", [128, H2, 512], BF16); BSbf = [P.buf("r_Sbf%d" % hd) for hd in range(H2)]
        ggn = sb.alloc("r_ggn", [128, c.RV], F32); Bggn = P.buf("r_ggn")
        decT = sb.alloc("r_decT", [128, RH, 128], F32); decTs = sb.alloc("r_decTs", [128, RH, TS], F32); Bdec = P.buf("r_dec")
        colm = sb.alloc("r_colm", [128, c.SPC, TS], BF16)
        stats = sb.alloc("r_stats", [128, 6], F32); mv = sb.alloc("r_mv", [128, 2], F32); sm = sb.alloc("r_sm", [128, 8], F32)
        Bst = P.buf("r_stats"); Bmv = P.buf("r_mv"); Bsm = P.buf("r_sm")
        qb = [sb.alloc("r_qb%d" % i, [128, H2, TS], BF16) for i in range(2)]; Bqb = [P.buf("r_qb%d" % i) for i in range(2)]
        kb = sb.alloc("r_kb", [128, c.RQK], BF16); Bkb = P.buf("r_kb")

        DMA("sp", ggn[:, :], ggn_in[l:l + 1, :].broadcast_to([128, c.RV]), (), (Bggn,), "r_c")
        DMA("sp", decT[:, :, :], cin["decT"][:, :, :], (), (Bdec,), "r_c")
        DMA("sp", decTs[:, :, :], cin["decTs"][:, :, :], (), (Bdec,), "r_c")
        DMA("sp", colm[:, :, :], cin["colm"][:, :, :], (), (Bdec,), "r_c")
        S = Sx[0]; BS = BSx[0]
        bank_o = [pbank[h] for h in range(RH)]; Bbo = [Bp[h] for h in range(RH)]
        bank_s, Bbs = pbank[4], Bp[4]
        ubanks = [(pbank[5], Bp[5]), (pbank[6], Bp[6])]
        ust = {"i": 0}
        HH = H2 // 2
        agin = [dscr("ag_s_in%d_%d" % (l, ch), [HH * 128, 512], F32) for ch in range(2)]
        agout = [dscr("ag_s_out%d_%d" % (l, ch), [c.CPS * HH * 128, 512], F32) for ch in range(2)]
        Bagin = P.buf("agin"); Bagout = P.buf("agout")

        def ret_block(t0, n, C, sample, p2):
            bi = c.NB if sample else t0 // c.BLK
            nch = n // C
            rmsnorm(bi, t0, n, gcol_norm(l, 1), hT, Bh, wk)
            DMA("sp", cos[:, 0:n], cin["rope"][0, :, t0:t0 + n], (), (Bcos,), "r_cos")
            DMA("sp", sin[:, 0:n], cin["rope"][1, :, t0:t0 + n], (), (Bsin,), "r_sin")
            glist = ([("q", g) for g in range(NQG)] if p2 else []) + [("k", NQG + g) for g in range(NQG)]
            for (kind, g) in glist:
                wt, wb = load_w(wi, g)
                for hh in range(2):
                    h = (g % NQG) * 2 + hh
                    bA, bbA = nbank(); bB, bbB = nbank()
                    pe_group([MM(bA[:, 0:n], wt[:, kc, (2 * hh) * 128:(2 * hh + 1) * 128], hT[:, kc, 0:n], kc == 0, kc == KC - 1) for kc in range(KC)],
                             r=(wb, Bh), w=(bbA,))
                    pe_group([MM(bB[:, 0:n], wt[:, kc, (2 * hh + 1) * 128:(2 * hh + 2) * 128], hT[:, kc, 0:n], kc == 0, kc == KC - 1) for kc in range(KC)],
                             r=(wb, Bh), w=(bbB,))
                    base = (0 if kind == "q" else H2) + 2 * h
                    TT("dve", tmp[0][:, 0:n], bA[:, 0:n], cos[:, 0:n], ALU.mult, (bbA, Bcos), (Bt[0],))
                    TT("dve", tmp[1][:, 0:n], bB[:, 0:n], sin[:, 0:n], ALU.mult, (bbB, Bsin), (Bt[1],))
                    TT("pool", qkT[:, base, 0:n], tmp[0][:, 0:n], tmp[1][:, 0:n], ALU.subtract, (Bt[0], Bt[1]), (Bqk[base],))
                    TT("dve", tmp[2][:, 0:n], bB[:, 0:n], cos[:, 0:n], ALU.mult, (bbB, Bcos), (Bt[2],))
                    TT("dve", tmp[3][:, 0:n], bA[:, 0:n], sin[:, 0:n], ALU.mult, (bbA, Bsin), (Bt[3],))
                    TT("pool", qkT[:, base + 1, 0:n], tmp[2][:, 0:n], tmp[3][:, 0:n], ALU.add, (Bt[2], Bt[3]), (Bqk[base + 1],))
            for kind in (("v", "g") if p2 else ("v",)):
                for h in range(RH):
                    wt, wb = load_w(wi, 2 * NQG + (0 if kind == "v" else RH) + h)
                    for ci in range(nch):
                        c0 = ci * C
                        bk, bbk = nbank()
                        pe_group([MM(bk[0:C, :], hT[:, kc, c0:c0 + C], wt[:, kc, :], kc == 0, kc == KC - 1) for kc in range(KC)],
                                 r=(wb, Bh), w=(bbk,))
                        if kind == "v":
                            CP("act", v_tok[0:C, ci, h * 512:(h + 1) * 512], bk[0:C, :], (bbk,), (Bv[ci][h],))
                        else:
                            ACT(sg_tok[0:C, ci, h * 512:(h + 1) * 512], bk[0:C, :], AF.Silu, (bbk,), (Bsgt[ci][h],))
            dT = decTs if sample else decT
            tci, tci2, tkd = ("cis", "cis2", "kdecs") if sample else ("ci", "ci2", "kdec")
            for ci in range(nch):
                c0 = ci * C
                pe_group([TR(pbf[0:C, hc * 128:(hc + 1) * 128], qkT[:, H2 + hc, c0:c0 + C], ident_b) for hc in range(H2)],
                         r=tuple(Bqk[H2:2 * H2]) + (B_const,), w=(Bpbf,))
                for h in range(RH):
                    TS_("dve", k_tok[0:C, h * 256:(h + 1) * 256], pbf[0:C, h * 256:(h + 1) * 256], rtc(tkd, h, 1, C), None, ALU.mult, None,
                        (Bpbf, B_const), (Bkt,))
                if p2:
                    pe_group([MM(bank_s[0:C, h * C:(h + 1) * C], qkT[:, H2 + 2 * h + hf, c0:c0 + C], qkT[:, 2 * h + hf, c0:c0 + C], hf == 0, hf == 1)
                              for h in range(RH) for hf in range(2)], r=tuple(Bqk), w=(Bbs,))
                    TT("dve", sT[0:C, :, 0:C], bank_s[0:C, 0:RH * C].rearrange("p (h i) -> p h i", i=C), dT[0:C, :, 0:C], ALU.mult,
                       (Bbs, Bdec), (BsT,))
                    for h in range(RH):
                        fl_ = [MM(bank_o[h][0:C, :], sT[0:C, h, 0:C], v_tok[0:C, ci, h * 512:(h + 1) * 512], True, False)]
                        if not sample:
                            fl_ += [MM(bank_o[h][0:C, :], qkT[:, 2 * h + dc, c0:c0 + C], S_bf[:, 2 * h + dc, :], False, dc == 1) for dc in range(2)]
                        pe_group(fl_, r=(BsT, Bv[ci][h], Bqk[2 * h], Bqk[2 * h + 1], BSbf[2 * h], BSbf[2 * h + 1]), w=(Bbo[h],))
                if sample:
                    for b in range(c.SPC):
                        sx = Sx[b % 2]; bsx = BSx[b % 2]
                        DMA("sp", sx[:, :, :], st_in[l, b].rearrange("h (dc p) v -> p (h dc) v", p=128), (), tuple(bsx), "r_S%d" % (b % 2))
                        for hd in range(H2):
                            CP("act" if hd % 2 == 0 else "pool", S_bf[:, hd, :], sx[:, hd, :], (bsx[hd],), (BSbf[hd],))
                        TT("dve", qb[b % 2][:, :, :], qkT[:, 0:H2, 0:TS], colm[:, b:b + 1, :].broadcast_to([128, H2, TS]), ALU.mult,
                           tuple(Bqk[0:H2]) + (Bdec,), (Bqb[b % 2],))
                        for h in range(RH):
                            pe_group([MM(bank_o[h][0:C, :], qb[b % 2][:, 2 * h + dc, :], S_bf[:, 2 * h + dc, :], False, (b == c.SPC - 1 and dc == 1)) for dc in range(2)],
                                     r=(Bqb[b % 2], BSbf[2 * h], BSbf[2 * h + 1]), w=(Bbo[h],))
                        TS_("dve", kb[0:C, :], k_tok[0:C, :], rtc("rowm", b, 1, C), None, ALU.mult, None, (Bkt, B_const), (Bkb,))
                        for h in range(RH):
                            for dc in range(2):
                                hd = 2 * h + dc
                                bu, bbu = ubanks[ust["i"] % 2]; ust["i"] += 1
                                pe_group([MM(bu[:, :], kb[0:C, hd * 128:(hd + 1) * 128], v_tok[0:C, ci, h * 512:(h + 1) * 512], True, True)],
                                         r=(Bkb, Bv[ci][h]), w=(bbu,))
                                STT("dve", sx[:, hd, :], sx[:, hd, :], rtc("g4", h), bu[:, :], ALU.mult, ALU.add, (bbu, bsx[hd], B_const, BSbf[hd]), (bsx[hd],))
                        DMA("sp", ss_out[l, b].rearrange("h (dc p) v -> p (h dc) v", p=128), sx[:, :, :], tuple(bsx), (), "r_So%d" % (b % 2))
                if p2:
                    for h in range(RH):
                        P.add("dve", (lambda e, h=h: e.bn_stats(out=stats[0:C, :], in_=bank_o[h][0:C, :])), r=(Bbo[h],), w=(Bst,))
                        P.add("dve", (lambda e: e.bn_aggr(out=mv[0:C, :], in_=stats[0:C, :])), r=(Bst,), w=(Bmv,))
                        TS_("dve", sm[0:C, 0:1], mv[0:C, 1:2], rtc(tci2, h, 1, C), rtc("eps", 0, 1, C), ALU.mult, ALU.add, (Bmv, B_const), (Bsm,))
                        ACT(sm[0:C, 1:2], sm[0:C, 0:1], AF.Ln, (Bsm,), (Bsm,))
                        ACT(sm[0:C, 2:3], sm[0:C, 1:2], AF.Exp, (Bsm,), (Bsm,), scale=-0.5)
                        TT("dve", sm[0:C, 3:4], sm[0:C, 2:3], rtc(tci, h, 1, C), ALU.mult, (Bsm, B_const), (Bsm,))
                        STT("dve", sm[0:C, 4:5], mv[0:C, 0:1], -1.0, sm[0:C, 3:4], ALU.mult, ALU.mult, (Bmv, Bsm), (Bsm,))
                        ACT(on[0][0:C, :], bank_o[h][0:C, :], AF.Identity, (Bbo[h], Bsm), (Bon[0],), bias=sm[0:C, 4:5], scale=sm[0:C, 3:4])
                        TT("pool", on[1][0:C, :], on[0][0:C, :], ggn[0:C, h * 512:(h + 1) * 512], ALU.mult, (Bon[0], Bggn), (Bon[1],))
                        TT("dve", og[0:C, h * 512:(h + 1) * 512], on[1][0:C, :], sg_tok[0:C, ci, h * 512:(h + 1) * 512], ALU.mult,
                           (Bon[1], Bsgt[ci][h]), (Bog,))
                    for v0 in range(0, VC, 8):
                        nv = min(8, VC - v0)
                        pe_group([TR(pbf[:, k * C:(k + 1) * C], og[0:C, (v0 + k) * 128:(v0 + k + 1) * 128], ident_b[0:C, 0:C]) for k in range(nv)],
                                 r=(Bog, B_const), w=(Bpbf,))
                        CP("act", ogT[:, v0:v0 + nv, c0:c0 + C], pbf[:, 0:nv * C].rearrange("p (k t) -> p k t", t=C), (Bpbf,), (BogT,))
                if not sample:
                    for h in range(RH):
                        for dc in range(2):
                            hd = 2 * h + dc
                            bu, bbu = ubanks[ust["i"] % 2]; ust["i"] += 1
                            pe_group([MM(bu[:, :], k_tok[0:C, hd * 128:(hd + 1) * 128], v_tok[0:C, ci, h * 512:(h + 1) * 512], True, True)],
                                     r=(Bkt, Bv[ci][h]), w=(bbu,))
                            STT("dve", S[:, hd, :], S[:, hd, :], rtc("g128", h), bu[:, :], ALU.mult, ALU.add, (bbu, BS[hd], B_const, BSbf[hd]), (BS[hd],))
                            if p2:
                                CP("act" if dc == 0 else "pool", S_bf[:, hd, :], S[:, hd, :], (BS[hd],), (BSbf[hd],))
            if p2:
                for g in range(wo.NG):
                    wt, wb = load_w(wo, g)
                    for oo in range(OW // 128):
                        oc = g * (OW // 128) + oo
                        bk, bbk = nbank()
                        pe_group([MM(bk[:, 0:n], wt[:, vc, oo * 128:(oo + 1) * 128], ogT[:, vc, 0:n], vc == 0, vc == VC - 1) for vc in range(VC)],
                                 r=(wb, BogT), w=(bbk,))
                        TT("dve", xT[:, oc, t0:t0 + n], xT[:, oc, t0:t0 + n], bk[:, 0:n], ALU.add, (bbk, B_x[bi][oc]), (B_x[bi][oc],))

        for hd in range(H2):
            P.add("dve", (lambda e, hd=hd: e.memset(S[:, hd, :], 0.0)), r=(), w=(BS[hd],))
        for rb in range(c.NRB):
            ret_block(rb * RB, RB, 128, False, False)
        for ch in range(2):
            DMA("sp", agin[ch].ap().rearrange("(hd p) v -> p hd v", p=128), S[:, ch * HH:(ch + 1) * HH, :], tuple(BS), (Bagin,), "r_ag")
        for ch in range(2):
            P.add("pool", (lambda e, ch=ch: e.collective_compute("AllGather", ALU.bypass, replica_groups=groups_rep,
                                                                  ins=[agin[ch].ap().opt()], outs=[agout[ch].ap().opt()])),
                  r=(Bagin,), w=(Bagout,), dma="cc_s%d_%d" % (l, ch), inc=1)
        for hd in range(H2):
            P.add("dve", (lambda e, hd=hd: e.memset(S[:, hd, :], 0.0)), r=(), w=(BS[hd],))
        for r_ in range(c.CPS):
            for ch in range(2):
                DMA("sp", Sx[1][:, ch * HH:(ch + 1) * HH, :], agout[ch].ap()[r_ * HH * 128:(r_ + 1) * HH * 128, :].rearrange("(hd p) v -> p hd v", p=128),
                    (Bagout,), tuple(BSx[1]), "r_S1")
            for hd in range(H2):
                STT("dve", S[:, hd, :], Sx[1][:, hd, :], rtc("coef", r_ * RH + hd // 2), S[:, hd, :], ALU.mult, ALU.add,
                    (BSx[1][hd], BS[hd], B_const), (BS[hd],))
        for hd in range(H2):
            CP("act" if hd % 2 == 0 else "pool", S_bf[:, hd, :], S[:, hd, :], (BS[hd],), (BSbf[hd],))
        for rb in range(c.NRB):
            ret_block(rb * RB, RB, 128, False, True)
        DMA("sp", sp_out[l].rearrange("h (dc p) v -> p (h dc) v", p=128), S[:, :, :], tuple(BS), (), "r_So0")
        ret_block(SEG, TS, TS, True, True)
        sb.reset(m)
        P.barrier()


    NCH = D // 256
    ag_kT_in = [dscr("ag_kT_in%d" % ch, [256, SEG]) for ch in range(NCH)]
    ag_kT_out = [dscr("ag_kT_out%d" % ch, [c.CPS * 256, SEG]) for ch in range(NCH)]
    agvi = [dscr("agvi%d" % ch, [SEG, 256]) for ch in range(NCH)]
    agvo = [dscr("agvo%d" % ch, [c.CPS * SEG, 256]) for ch in range(NCH)]
    B_agk_in = P.buf("agk_in"); B_agk_out = P.buf("agk_out"); B_agv_in = P.buf("agv_in"); B_agv_out = P.buf("agv_out")

    def kv_phase():
        m = sb.mark()
        wk = norm_work()
        wkv = wscr["kv"]; CW = wkv.CW; NGK = D // CW
        hT = sb.alloc("kv_hT", [128, KC, 512], BF16); Bh = P.buf("kv_hT")
        kst = [sb.alloc("kv_kst%d" % i, [128, 512], BF16) for i in range(2)]; Bkst = [P.buf("kst%d" % i) for i in range(2)]
        st32 = [sb.alloc("kv_st%d" % i, [128, 512], F32) for i in range(2)]; Bst32 = [P.buf("st32%d" % i) for i in range(2)]
        vst = [sb.alloc("kv_vst%d" % i, [128, 512], BF16) for i in range(2)]; Bvst = [P.buf("vst%d" % i) for i in range(2)]
        i1 = 0; i2 = 0
        for bi, (t0, n) in enumerate(blocks):
            sample = bi == c.NB
            rmsnorm(bi, t0, n, GCOL_KV, hT, Bh, wk)
            for g in range(NGK if fl.get("kva", True) else 0):
                wt, wb = load_w(wkv, g)
                for oo in range(CW // 128):
                    fc = g * (CW // 128) + oo
                    bk, bbk = nbank()
                    pe_group([MM(bk[:, 0:n], wt[:, kc, oo * 128:(oo + 1) * 128], hT[:, kc, 0:n], kc == 0, kc == KC - 1) for kc in range(KC)],
                             r=(wb, Bh), w=(bbk,))
                    if sample:
                        CP("act", kTs[:, fc, 0:n], bk[:, 0:n], (bbk,), (B_kTs,))
                    else:
                        s_ = i1 % 2; i1 += 1
                        CP("act", kst[s_][:, 0:n], bk[:, 0:n], (bbk,), (Bkst[s_],))
                        DMA("sp", ag_kT_in[fc // 2][(fc % 2) * 128:(fc % 2 + 1) * 128, t0:t0 + n], kst[s_][:, 0:n], (Bkst[s_],), (B_agk_in,), "kst%d" % s_)
            for g in range(2 * NGK if fl.get("kvb", True) else 0):
                wt, wb = load_w(wkv, g)
                isv = g >= NGK
                col = (g % NGK) * CW
                for tt in range(0, n, 128):
                    rows = min(128, n - tt)
                    bk, bbk = nbank()
                    pe_group([MM(bk[0:rows, 0:CW], hT[:, kc, tt:tt + rows], wt[:, kc, :], kc == 0, kc == KC - 1) for kc in range(KC)],
                             r=(wb, Bh), w=(bbk,))
                    s_ = i2 % 2; i2 += 1
                    CP("dve", st32[s_][0:rows, 0:CW], bk[0:rows, 0:CW], (bbk,), (Bst32[s_],))
                    dst = v_out if isv else k_out
                    if fl.get("kvo", True):
                        DMA("sp", dst[t0 + tt:t0 + tt + rows, col:col + CW], st32[s_][0:rows, 0:CW], (Bst32[s_],), (), "st32%d" % s_)
                    if isv and fl.get("kvv", True):
                        if sample:
                            CP("act", vs_tok[0:rows, col:col + CW], st32[s_][0:rows, 0:CW], (Bst32[s_],), (B_vs,))
                        else:
                            CP("act", vst[s_][0:rows, 0:CW], st32[s_][0:rows, 0:CW], (Bst32[s_],), (Bvst[s_],))
                            for cc_ in range(0, CW, 256):
                                DMA("sp", agvi[(col + cc_) // 256][t0 + tt:t0 + tt + rows, :], vst[s_][0:rows, cc_:cc_ + 256], (Bvst[s_],), (B_agv_in,), "vst%d" % s_)
        if c.__dict__.get("flags", {}).get("kvcc", True):
            for ch in range(NCH):
                P.add("pool", (lambda e, ch=ch: e.collective_compute("AllGather", ALU.bypass, replica_groups=groups_rep,
                                                                      ins=[ag_kT_in[ch].ap().opt()], outs=[ag_kT_out[ch].ap().opt()])),
                      r=(B_agk_in,), w=(B_agk_out,), dma="cc_k%d" % ch, inc=1)
                P.add("pool", (lambda e, ch=ch: e.collective_compute("AllGather", ALU.bypass, replica_groups=groups_rep,
                                                                      ins=[agvi[ch].ap().opt()], outs=[agvo[ch].ap().opt()])),
                      r=(B_agv_in,), w=(B_agv_out,), dma="cc_v%d" % ch, inc=1)
        sb.reset(m)
        P.barrier()

    def sb_layer(l):
        li = l - c.NA
        m = sb.mark()
        wq, wo = wscr["sq%d" % l], wscr["so%d" % l]
        SBH, DSEQ, NPG, SPC, NKB, CPS = c.SBH, c.DSEQ, c.NPG, c.SPC, c.NKB, c.CPS
        HQ = SBH * DSEQ
        QT = sb.alloc("QT", [128, KC, T], BF16)
        BQ = [[[P.buf("Q") for _ in range(2)] for _ in range(KC)] for _ in range(NBT)]
        BQs = [P.buf("Qs%d" % b) for b in range(SPC)]
        m1 = sb.mark()
        wk = norm_work()
        hT = sb.alloc("sb_hT", [128, KC, 512], BF16); Bh = P.buf("sb_hT")
        for bi, (t0, n) in enumerate(blocks):
            rmsnorm(bi, t0, n, gcol_norm(l, 1), hT, Bh, wk)
            for g in range(wq.NG):
                wt, wb = load_w(wq, g)
                for oo in range(wq.CW // 128):
                    fc = g * (wq.CW // 128) + oo
                    bk, bbk = nbank()
                    pe_group([MM(bk[:, 0:n], wt[:, kc, oo * 128:(oo + 1) * 128], hT[:, kc, 0:n], kc == 0, kc == KC - 1) for kc in range(KC)],
                             r=(wb, Bh), w=(bbk,))
                    wl = tuple(BQ[bi][fc]) + (tuple(BQs) if bi == c.NB else ())
                    ACT(QT[:, fc, t0:t0 + n], bk[:, 0:n], AF.Identity, (bbk,), wl, scale=float(c.HD) ** -0.5)
        sb.reset(m1)
        P.barrier()
        NSL = CPS + 1
        KT = sb.alloc("KT", [128, NSL, SEG], BF16); BKT = P.buf("KT")
        Vt = sb.alloc("Vt", [128, NSL * NKB, 128], BF16); BVt = P.buf("Vt")
        mneg = sb.alloc("mneg", [128, c.BLK // 128, c.BLK], BF16); Bmn = P.buf("mneg")
        e32 = [sb.alloc("e32_%d" % i, [128, 512], F32) for i in range(2)]; Be32 = [P.buf("e32") for i in range(2)]
        spt = [sb.alloc("sp_%d" % i, [128, 512], BF16) for i in range(2)]; Bspt = [P.buf("sp") for i in range(2)]
        sps32 = sb.alloc("sps32", [128, 512], F32); Bsps32 = P.buf("sps32")
        spsb = [sb.alloc("spsb_%d" % i, [128, 512], BF16) for i in range(2)]; Bspsb = [P.buf("spsb") for i in range(2)]
        aex = [sb.alloc("aex_%d" % i, [128, 512], BF16) for i in range(2)]; Baex = [P.buf("aex") for i in range(2)]
        bsb = sb.alloc("bsb", [128, SBH], F32); Bbsb = P.buf("bsb")
        biasP = sb.alloc("biasP", [128, NSL, SBH], F32)
        biasS = sb.alloc("biasS", [128, SBH, DSEQ], F32)
        newb = sb.alloc("newb", [128, SPC, HQ], F32); Bnewb = P.buf("newb")
        Ksb = [sb.alloc("Ksb%d" % i, [128, D], BF16) for i in range(2)]; BKsb = [P.buf("Ksb") for i in range(2)]
        Vsb = [sb.alloc("Vsb%d" % i, [128, D], BF16) for i in range(2)]; BVsb = [P.buf("Vsb") for i in range(2)]
        KTp = [sb.alloc("KTp%d" % i, [128, KC, 128], BF16) for i in range(2)]; BKTp = [P.buf("KTp") for i in range(2)]
        zs = sb.alloc("zs", [128, HQ], F32); Bzs = P.buf("zs")
        es = sb.alloc("es", [128, HQ], F32); Bes = P.buf("es")
        sps = sb.alloc("sps", [128, HQ], BF16); Bsps = P.buf("sps")
        ss32 = sb.alloc("ss32", [128, HQ], F32); Bss32 = P.buf("ss32")
        ssb = [sb.alloc("ssb%d" % i, [128, HQ], BF16) for i in range(2)]; Bssb = [P.buf("ssb") for i in range(2)]
        ts_ = sb.alloc("ts", [128, HQ], F32); Bts = P.buf("ts")
        axs = sb.alloc("axs", [128, HQ], BF16); Baxs = P.buf("axs")
        zer = sb.alloc("zer", [128, 2 * KC * DSEQ], BF16); Bzer = P.buf("zer")
        P.add("dve", (lambda e: e.memset(zer[:, :], 0.0)), r=(), w=(Bzer,))
        ptile = sb.alloc("ptile", [128, SPC * NPG], I32); iot = sb.alloc("iot", [128, 1], I32); idx = sb.alloc("idx", [128, SPC * NPG], I32)
        Bidx = P.buf("idx"); Bpt = P.buf("ptile"); Biot = P.buf("iot")

        DMA("sp", mneg[:, :, :], cin["mneg"][:, :, :], (), (Bmn,), "sb_c")
        DMA("sp", bsb[:, :], bsb_in[li:li + 1, :].broadcast_to([128, SBH]), (), (Bbsb,), "sb_c")
        DMA("sp", newb[:, :, :], cin["smask"][:, :, :], (), (Bnewb,), "sb_c")
        DMA("sp", ptile[:, :], pt_in.ap().rearrange("b j -> (b j)").unsqueeze(0).broadcast_to([128, SPC * NPG]), (), (Bpt,), "sb_c")
        P.add("pool", (lambda e: e.iota(iot[:, :], pattern=[[0, 1]], base=0, channel_multiplier=1)), r=(), w=(Biot,))
        TS_("dve", idx[:, :], ptile[:, :], 128, iot[:, 0:1], ALU.mult, ALU.add, (Bpt, Biot), (Bidx,))
        for sl in range(NSL):
            TS_("dve", biasP[:, sl, :], bsb[:, :], rtc("segm", sl), None, ALU.add, None, (Bbsb, B_const), (Bbsb,))
        for q_ in range(DSEQ):
            CP("dve", biasS[:, :, q_], bsb[:, :], (Bbsb,), (Bbsb,))
        for b in range(SPC):
            TT("dve", newb[:, b, :], newb[:, b, :], biasS[:, :, :].rearrange("p h q -> p (h q)"), ALU.add, (Bnewb, Bbsb), (Bnewb,))

        QTz = sb.alloc("QTz", [128, KC, 2, TS], BF16); BQTz = P.buf("QTz")
        P.add("dve", (lambda e: e.memset(QTz[:, :, :, :], 0.0)), r=(), w=(BQTz,))
        CP("dve", QTz[0:64, :, 0, :], QT[0:64, :, SEG:SEG + TS], tuple(BQs), (BQTz,))
        CP("dve", QTz[64:128, :, 1, :], QT[64:128, :, SEG:SEG + TS], tuple(BQs), (BQTz,))
        zbanks = [(pbank[0], Bp[0]), (pbank[1], Bp[1])]
        abanks = [(pbank[2], Bp[2]), (pbank[3], Bp[3])]
        obanks = [(pbank[4], Bp[4]), (pbank[5], Bp[5])]
        cnt = {"t": 0, "o": 0, "g": 0}
        for fc in range(KC if fl.get("sbp", True) else 0):
            fo_ = (fc % 2) * 128
            DMA("sp", KT[:, 0, :], ag_kT_in[fc // 2][fo_:fo_ + 128, :], (B_agk_in,), (BKT,), "KT")
            DMA("sp", Vt[:, 0:NKB, :], agvi[fc // 2][:, fo_:fo_ + 128].rearrange("(kb p) c -> p kb c", p=128), (B_agv_in,), (BVt,), "Vt")
            for si in range(CPS):
                r_ = CPS - 1 - si
                DMA("sp", KT[:, 1 + si, :], ag_kT_out[fc // 2][r_ * 256 + fo_:r_ * 256 + fo_ + 128, :], (B_agk_out,), (BKT,), "KT")
                DMA("sp", Vt[:, (1 + si) * NKB:(2 + si) * NKB, :],
                    agvo[fc // 2][r_ * SEG:(r_ + 1) * SEG, fo_:fo_ + 128].rearrange("(kb p) c -> p kb c", p=128), (B_agv_out,), (BVt,), "Vt")
            for e_ in range(2):
                h = 2 * fc + e_; pb = 64 * e_
                for bi in range(c.NB):
                    t0, n = blocks[bi]
                    Qh = QT[pb:pb + 64, fc, t0:t0 + n]
                    tiles = [(0, kb, (kb - t0 // 128) if kb * 128 >= t0 else None) for kb in reversed(range((t0 + n) // 128))]
                    tiles += [(1 + si, kb, None) for si in range(CPS) for kb in reversed(range(NKB))]
                    ob, bob = obanks[cnt["o"] % 2]; cnt["o"] += 1
                    for ti_, (slot, kb, diag) in enumerate(tiles):
                        first = ti_ == 0; last = ti_ == len(tiles) - 1
                        j = cnt["t"] % 2; cnt["t"] += 1
                        zb, bzb = zbanks[j]; ab, bab = abanks[j]
                        Kh = KT[pb:pb + 64, slot, kb * 128:(kb + 1) * 128]
                        bias = biasP[:, slot, h:h + 1]
                        rd = (BKT, BQ[bi][fc][e_]) + ((Bmn, B_const) if diag is not None else ())
                        fz = [MM(zb[:, 0:n], Kh, Qh, True, diag is None)]
                        if diag is not None: fz.append(MM(zb[:, 0:n], ident_b, mneg[:, diag, 0:n], False, True))
                        pe_group(fz, r=rd, w=(bzb,))
                        ACT(e32[j][:, 0:n], zb[:, 0:n], AF.Exp, (bzb, Bbsb), (Be32[j],), bias=bias)
                        ACT(spt[j][:, 0:n], e32[j][:, 0:n], AF.Ln, (Be32[j], B_const), (Bspt[j],), bias=rtc("one", 0))
                        fa = [MM(ab[:, 0:n], Kh, Qh, True, False)]
                        if diag is not None: fa.append(MM(ab[:, 0:n], ident_b, mneg[:, diag, 0:n], False, False))
                        fa.append(MM(ab[:, 0:n], ntri_b, spt[j][:, 0:n], False, first))
                        rr = rd + (Bspt[j], B_const)
                        if not first:
                            fa.append(MM(ab[:, 0:n], nones_b, spsb[(j + 1) % 2][:, 0:n], False, True))
                            rr = rr + (Bspsb[(j + 1) % 2],)
                        pe_group(fa, r=rr, w=(bab,))
                        if not last:
                            if first:
                                CP("dve", sps32[:, 0:n], spt[j][:, 0:n], (Bspt[j],), (Bsps32,))
                            else:
                                TT("dve", sps32[:, 0:n], sps32[:, 0:n], spt[j][:, 0:n], ALU.add, (Bspt[j], Bsps32), (Bsps32,))
                            CP("dve", spsb[j][:, 0:n], sps32[:, 0:n], (Bsps32,), (Bspsb[j],))
                        ACT(aex[j][:, 0:n], ab[:, 0:n], AF.Exp, (bab, Bbsb), (Baex[j],), bias=bias)
                        pe_group([MM(ob[:, 0:n], Vt[:, slot * NKB + kb, :], aex[j][:, 0:n], first, last)], r=(BVt, Baex[j]), w=(bob,))
                    CP("dve", QT[pb:pb + 64, fc, t0:t0 + n], ob[pb:pb + 64, 0:n], (bob,), (BQ[bi][fc][e_],))
        ts0 = SEG
        for b in range(SPC if fl.get("sbs", True) else 0):
            qc0 = ts0 + b * DSEQ
            ob, bob = obanks[cnt["o"] % 2]; cnt["o"] += 1
            P.add("dve", (lambda e: e.memset(ss32[:, :], 0.0)), r=(), w=(Bss32,))
            ntile = NPG + 1 if fl.get("sbs1", True) else 1
            for ti_ in range(ntile):
                first = ti_ == 0; last = ti_ == ntile - 1
                j = cnt["t"] % 2; cnt["t"] += 1
                zb, bzb = zbanks[j]; ab, bab = abanks[j]
                if first:
                    rows = TS
                    def Kof(fc_): return kTs[:, fc_, 0:TS]
                    Vsrc = vs_tok; rk = (B_kTs,); rv = (B_vs,)
                    btile = newb[0:rows, b, :]
                else:
                    pg = NPG - ti_
                    g_ = cnt["g"] % 2; cnt["g"] += 1
                    rows = 128
                    col = b * NPG + pg
                    P.add("pool", (lambda e, g_=g_, col=col: e.indirect_dma_start(
                        out=Ksb[g_][:, :], out_offset=None, in_=ck_in[:, :],
                        in_offset=bass.IndirectOffsetOnAxis(ap=idx[:, col:col + 1], axis=0))), r=(Bidx,), w=(BKsb[g_],), dma="Ksb%d" % g_)
                    P.add("pool", (lambda e, g_=g_, col=col: e.indirect_dma_start(
                        out=Vsb[g_][:, :], out_offset=None, in_=cv_in[:, :],
                        in_offset=bass.IndirectOffsetOnAxis(ap=idx[:, col:col + 1], axis=0))), r=(Bidx,), w=(BVsb[g_],), dma="Vsb%d" % g_)
                    pe_group([TR(pbf[:, k * 128:(k + 1) * 128], Ksb[g_][:, k * 128:(k + 1) * 128], ident_b) for k in range(KC)],
                             r=(BKsb[g_], B_const), w=(Bpbf,))
                    CP("dve", KTp[g_][:, :, :], pbf[:, 0:KC * 128].rearrange("p (k t) -> p k t", t=128), (Bpbf,), (BKTp[g_],))
                    def Kof(fc_, g_=g_): return KTp[g_][:, fc_, :]
                    Vsrc = Vsb[g_]; rk = (BKTp[g_],); rv = (BVsb[g_],)
                    btile = biasS[:, :, :].rearrange("p h q -> p (h q)")
                pe_group([MM(zb[0:rows, fc_ * 2 * DSEQ:(fc_ + 1) * 2 * DSEQ], Kof(fc_), QTz[:, fc_, :, b * DSEQ:(b + 1) * DSEQ], True, True)
                          for fc_ in range(KC)], r=rk + (BQTz,), w=(bzb,))
                TT("dve", zs[0:rows, :], zb[0:rows, 0:HQ], btile, ALU.add, (bzb, Bbsb, Bnewb), (Bzs,))
                ACT(es[0:rows, :], zs[0:rows, :], AF.Exp, (Bzs,), (Bes,))
                ACT(sps[0:rows, :], es[0:rows, :], AF.Ln, (Bes, B_const), (Bsps,), bias=rtc("one", 0, 1, rows))
                if not fl.get("sba", True): continue
                fa = [MM(ab[0:rows, 0:HQ], ntri_b[0:rows, 0:rows], sps[0:rows, :], True, False)]
                rr = rk + (BQTz, Bsps, B_const)
                if not first:
                    fa.append(MM(ab[0:rows, 0:HQ], nones_b[:, 0:rows], ssb[(j + 1) % 2][:, :], False, False))
                    rr = rr + (Bssb[(j + 1) % 2],)
                fa += [MM(ab[0:rows, fc_ * 2 * DSEQ:(fc_ + 1) * 2 * DSEQ], Kof(fc_), QTz[:, fc_, :, b * DSEQ:(b + 1) * DSEQ], False, fc_ == KC - 1)
                       for fc_ in range(KC)]
                pe_group(fa, r=rr, w=(bab,))
                if not last:
                    TT("dve", ss32[0:rows, :], ss32[0:rows, :], sps[0:rows, :], ALU.add, (Bsps, Bss32), (Bss32,))
                    CP("dve", ssb[j][:, :], ss32[:, :], (Bss32,), (Bssb[j],))
                TT("dve", ts_[0:rows, :], ab[0:rows, 0:HQ], btile, ALU.add, (bab, Bbsb, Bnewb), (Bts,))
                ACT(axs[0:rows, :], ts_[0:rows, :], AF.Exp, (Bts,), (Baxs,))
                if not fl.get("sbo", True): continue
                fo = [MM(ob[:, 0:2 * KC * DSEQ], ident_b, zer[:, 0:2 * KC * DSEQ], True, False)] if first else []
                fo += [MM(ob[:, (hh % 2) * KC * DSEQ + (hh // 2) * DSEQ:(hh % 2) * KC * DSEQ + (hh // 2 + 1) * DSEQ],
                          Vsrc[0:rows, (hh // 2) * 128:(hh // 2 + 1) * 128], axs[0:rows, hh * DSEQ:(hh + 1) * DSEQ], False, last and hh == SBH - 1) for hh in range(SBH)]
                pe_group(fo, r=rv + (Baxs, Bzer, B_const), w=(bob,))
            for e_ in range(2 if (fl.get("sbo", True) and fl.get("sba", True)) else 0):
                CP("dve", QT[64 * e_:64 * e_ + 64, :, qc0:qc0 + DSEQ],
                   ob[64 * e_:64 * e_ + 64, e_ * KC * DSEQ:(e_ + 1) * KC * DSEQ].rearrange("p (k q) -> p k q", q=DSEQ), (bob,), (BQs[b],))
        for bi, (t0, n) in enumerate(blocks):
            for g in range(wo.NG):
                wt, wb = load_w(wo, g)
                for oo in range(wo.CW // 128):
                    oc = g * (wo.CW // 128) + oo
                    bk, bbk = nbank()
                    rq = tuple(BQ[bi][kc][e_] for kc in range(KC) for e_ in range(2)) + (tuple(BQs) if bi == c.NB else ())
                    pe_group([MM(bk[:, 0:n], wt[:, kc, oo * 128:(oo + 1) * 128], QT[:, kc, t0:t0 + n], kc == 0, kc == KC - 1) for kc in range(KC)],
                             r=(wb,) + rq, w=(bbk,))
                    TT("dve", xT[:, oc, t0:t0 + n], xT[:, oc, t0:t0 + n], bk[:, 0:n], ALU.add, (bbk, B_x[bi][oc]), (B_x[bi][oc],))
        sb.reset(m)
        P.barrier()

    def final():
        m = sb.mark()
        wk = norm_work()
        yT = sb.alloc("yT", [128, KC, 512], F32); ByT = P.buf("yT")
        yst = [sb.alloc("yst%d" % i, [128, D], F32) for i in range(2)]
        Byst = [P.buf("yst%d" % i) for i in range(2)]
        ti = 0
        for bi, (t0, n) in enumerate(blocks):
            rmsnorm(bi, t0, n, GCOL_FIN, yT, ByT, wk)
            for tt in range(0, n, 128):
                rows = min(128, n - tt)
                s = ti % 2; ti += 1
                for f0 in range(0, KC, 4):
                    nf = min(4, KC - f0)
                    bank, bb = nbank()
                    pe_group([TR(bank[0:rows, k * 128:(k + 1) * 128], yT[:, f0 + k, tt:tt + rows], ident_f[:, :]) for k in range(nf)],
                             r=(ByT, B_const), w=(bb,))
                    CP("dve" if (f0 // 4) % 2 == 0 else "act", yst[s][0:rows, f0 * 128:(f0 + nf) * 128], bank[0:rows, 0:nf * 128], (bb,), (Byst[s],))
                DMA("sp", y_out[t0 + tt:t0 + tt + rows, :], yst[s][0:rows, :], (Byst[s],), (), "yst%d" % s)
        sb.reset(m)

    for l in range(c.DEPTH):
        if l + 1 < c.DEPTH:
            cast_layer_weights(l + 1)
        ffn(l, 0)
        if l < c.NA:
            if fl.get("ret", True): retention(l)
        else:
            if fl.get("sb", True): sb_layer(l)
        ffn(l, 1)
        ple(l)
        if l == c.NA - 1 and fl.get("kv", True):
            kv_phase()
    final()
    P.emit()
    return nc


def prep_inputs(cfg, inputs):
    c = cfg
    g_norm = np.asarray(inputs["g_norm"], np.float32)
    vecs = [g_norm[l, i] for l in range(c.DEPTH) for i in range(3)] + [np.asarray(inputs["g_ple"], np.float32)[l] for l in range(c.DEPTH)]
    vecs += [np.asarray(inputs["g_kv"], np.float32), np.asarray(inputs["g_final"], np.float32)]
    gvec = np.concatenate([v.reshape(c.KC, 128).T for v in vecs], axis=1).astype(np.float32)
    gvec = np.ascontiguousarray(gvec)
    shared = {"gvec": gvec}
    for k in ("w_ff1_gate", "w_ff1_up", "w_ff1_down", "w_ff2_gate", "w_ff2_up", "w_ff2_down", "w_ple_up", "w_ple_gate",
              "w_ret_in", "w_ret_out", "g_ret_gn", "w_kv", "w_sb_q", "w_sb_out", "b_sb"):
        shared[k] = np.ascontiguousarray(np.asarray(inputs[k], np.float32))
    shared["cache_k"] = np.asarray(inputs["cache_k"], np.float32).reshape(c.NPOOL * c.PAGE, c.D)
    shared["cache_v"] = np.asarray(inputs["cache_v"], np.float32).reshape(c.NPOOL * c.PAGE, c.D)
    xp = np.asarray(inputs["x_prompt"], np.float32); xs = np.asarray(inputs["x_sample"], np.float32)
    pp = np.asarray(inputs["p_prompt"], np.float32); ps = np.asarray(inputs["p_sample"], np.float32)
    st = np.asarray(inputs["state_ret"], np.float32)
    pt = np.asarray(inputs["page_table"], np.int32)
    maps = []
    for core in range(c.NC):
        seq, seg = core // c.CPS, core % c.CPS
        m = dict(shared)
        sl = slice(seg * c.SEG, (seg + 1) * c.SEG); bs = slice(core * c.SPC, (core + 1) * c.SPC)
        m["x"] = np.concatenate([xp[seq, sl], xs[bs].reshape(c.TS, c.D)], axis=0)
        m["p"] = np.concatenate([pp[:, seq, sl], ps[:, bs].reshape(c.DEPTH, c.TS, c.PLE)], axis=1)
        m["state_ret"] = np.ascontiguousarray(st[:, bs])
        m["page_table"] = np.ascontiguousarray(pt[bs])
        cons = host_constants(c, core)
        for nm in CONST_NAMES:
            m["c_" + nm] = cons[nm]
        maps.append(m)
    return maps


def assemble(cfg, results):
    c = cfg
    y_p = np.zeros((c.BATCH, c.CPS * c.SEG, c.D), np.float32)
    y_s = np.zeros((c.NC * c.SPC, c.DSEQ, c.D), np.float32)
    S_p = np.zeros((c.NA, c.BATCH, c.RH, c.DK, c.DV), np.float32)
    S_s = np.zeros((c.NA, c.NC * c.SPC, c.RH, c.DK, c.DV), np.float32)
    k_p = np.zeros((c.BATCH, c.CPS * c.SEG, c.SBH, c.HD), np.float32); v_p = np.zeros_like(k_p)
    k_s = np.zeros((c.NC * c.SPC, c.DSEQ, c.SBH, c.HD), np.float32); v_s = np.zeros_like(k_s)
    for core in range(c.NC):
        r = results[core]
        seq, seg = core // c.CPS, core % c.CPS
        sl = slice(seg * c.SEG, (seg + 1) * c.SEG); bs = slice(core * c.SPC, (core + 1) * c.SPC)
        y = np.asarray(r["y"]); kn = np.asarray(r["k_new"]); vn = np.asarray(r["v_new"])
        y_p[seq, sl] = y[:c.SEG]; y_s[bs] = y[c.SEG:].reshape(c.SPC, c.DSEQ, c.D)
        k_p[seq, sl] = kn[:c.SEG].reshape(c.SEG, c.SBH, c.HD); v_p[seq, sl] = vn[:c.SEG].reshape(c.SEG, c.SBH, c.HD)
        k_s[bs] = kn[c.SEG:].reshape(c.SPC, c.DSEQ, c.SBH, c.HD); v_s[bs] = vn[c.SEG:].reshape(c.SPC, c.DSEQ, c.SBH, c.HD)
        S_s[:, bs] = np.asarray(r["s_sample"])
        if seg == c.CPS - 1:
            S_p[:, seq] = np.asarray(r["s_prompt"])
    return (y_p, y_s, S_p, S_s, k_p, v_p, k_s, v_s)


_NC_CACHE = {}


def kernel(**inputs):
    cfg = Cfg()
    if "nc" not in _NC_CACHE:
        _NC_CACHE["nc"] = build(cfg)
    nc = _NC_CACHE["nc"]
    maps = prep_inputs(cfg, inputs)
    res = run_bass_kernel_spmd(nc, maps, core_ids=list(range(cfg.NC)))
    return assemble(cfg, res.results)
```

```python
import math
import numpy as np
import ml_dtypes
import concourse.bass as bass
import concourse.mybir as mybir
from concourse.bass_utils import run_bass_kernel_spmd

F32 = mybir.dt.float32
BF16 = mybir.dt.bfloat16
I32 = mybir.dt.int32
AF = mybir.ActivationFunctionType
ALU = mybir.AluOpType
NEG = -30000.0
ENGS = ("pe", "act", "dve", "pool", "sp")
DTSZ = {F32: 4, BF16: 2, I32: 4}


class Cfg:
    def __init__(s, **kw):
        s.NC = 8; s.CPS = 4; s.D = 1024; s.DFF = 2816; s.PLE = 256; s.SEG = 2048
        s.SPC = 16; s.DSEQ = 4; s.NPG = 16; s.PAGE = 128; s.NPOOL = 2560
        s.DEPTH = 4; s.NA = 2; s.HD = 64; s.DK = 256; s.EPS = 1e-6; s.ROPE_BASE = 10000.0
        s.SB_BIAS = -8.0
        s.__dict__.update(kw)
        s.KC = s.D // 128; s.JC = s.DFF // 128; s.PC = s.PLE // 128
        s.RH = s.D // s.DK; s.DV = 2 * s.DK; s.RQK = s.RH * s.DK; s.RV = s.RH * s.DV; s.VC = s.RV // 128
        s.SBH = s.D // s.HD; s.TS = s.SPC * s.DSEQ; s.T = s.SEG + s.TS
        s.BLK = min(512, s.SEG); s.NB = s.SEG // s.BLK
        s.RBLK = min(256, s.SEG); s.NRB = s.SEG // s.RBLK
        s.BATCH = s.NC // s.CPS; s.PAST = s.NPG * s.PAGE
        s.NKB = s.SEG // 128
        assert s.DK == 256 and s.DV == 512


class Buf:
    __slots__ = ("name", "w", "r")

    def __init__(s, name):
        s.name = name; s.w = None; s.r = []


class Prog:
    def __init__(s, nc):
        s.nc = nc
        s.ops = {e: [] for e in ENGS}
        s.cnt = {e: 0 for e in ENGS}
        s.known = {e: {} for e in ENGS}
        s.pending = {e: {} for e in ENGS}
        s.dcnt = {}
        s.kinc = {}
        s.sems = {}
        s.nbuf = 0

    def buf(s, name):
        return Buf(name)

    def sem(s, key):
        if key not in s.sems:
            s.sems[key] = s.nc.alloc_semaphore("s%d" % len(s.sems))
        return s.sems[key]

    def _val(s, prod):
        if prod[0] == "c":
            return ("E", prod[1]), prod[2] + 1
        return ("D", prod[1]), s.dcnt[prod[1]] * prod[2]

    def add(s, eng, fn, r=(), w=(), dma=None, inc=16):
        need = dict(s.pending[eng]); s.pending[eng] = {}
        prods = []
        for b in r:
            if b.w is not None: prods.append(b.w)
        for b in w:
            if b.w is not None: prods.append(b.w)
            prods.extend(b.r)
        for p in prods:
            if p[0] == "c" and p[1] == "pe" and eng == "pe" and dma is None:
                continue
            k, v = s._val(p)
            if need.get(k, 0) < v: need[k] = v
        waits = []
        kn = s.known[eng]
        for k, v in need.items():
            if kn.get(k, 0) < v:
                kn[k] = v; waits.append((k, v))
        if dma is None:
            me = ("c", eng, s.cnt[eng]); s.cnt[eng] += 1
            sig = (("E", eng), 1)
        else:
            s.dcnt[dma] = s.dcnt.get(dma, 0) + 1
            s.kinc[dma] = inc
            me = ("d", dma, inc)
            sig = (("D", dma), inc)
        s.ops[eng].append((waits, fn, sig))
        for b in r: b.r.append(me)
        for b in w:
            b.w = me; b.r = []
        return me

    def barrier(s):
        snap = {}
        for e in ENGS:
            if s.cnt[e]: snap[("E", e)] = s.cnt[e]
        for k, c in s.dcnt.items():
            snap[("D", k)] = c * s.kinc.get(k, 16)
        for e in ENGS:
            for k, v in snap.items():
                if k == ("E", e) and e == "pe": continue
                if s.pending[e].get(k, 0) < v: s.pending[e][k] = v

    def emit(s):
        nc = s.nc
        s.barrier()
        final = s.pending
        sems = {k: s.sem(k) for k in set(k for e in ENGS for (w, _, sg) in s.ops[e] for k in [sg[0]] + [x[0] for x in w])}
        for e in ENGS:
            for k in final[e]: sems.setdefault(k, s.sem(k))

        def replay(eng, name):
            for (waits, fn, sig) in s.ops[name]:
                for (k, v) in waits: eng.wait_ge(sems[k], v)
                ins = fn(eng)
                ins.then_inc(sems[sig[0]], sig[1])
            kn = s.known[name]
            for k, v in final[name].items():
                if kn.get(k, 0) < v: eng.wait_ge(sems[k], v)

        with nc.Block() as block:
            @block.tensor
            def _(e): replay(e, "pe")

            @block.scalar
            def _(e): replay(e, "act")

            @block.vector
            def _(e): replay(e, "dve")

            @block.gpsimd
            def _(e): replay(e, "pool")

            @block.sync
            def _(e): replay(e, "sp")


class SB:
    BASE = 16512 + 96
    LIMIT = 16512 + 212000

    def __init__(s, nc):
        s.nc = nc; s.off = SB.BASE; s.n = 0

    def alloc(s, name, shape, dt):
        sz = DTSZ[dt]
        for d in shape[1:]: sz *= d
        sz = (sz + 63) // 64 * 64
        assert s.off + sz <= SB.LIMIT, ("SBUF overflow", name, s.off, sz)
        s.n += 1
        t = s.nc.alloc_sbuf_tensor_at("%s_%d" % (name, s.n), list(shape), dt, offset=s.off)
        s.off += sz
        return t

    def mark(s): return s.off

    def reset(s, m): s.off = m


def host_constants(cfg, core):
    c = cfg
    seg = core % c.CPS
    cons = {}
    cons["ident_f"] = np.eye(128, dtype=np.float32)
    ib = np.zeros((128, 4, 128), dtype=np.float32)
    ib[:, 0] = np.eye(128)
    ib[:, 1] = 1.0
    kk = np.arange(128)
    ib[:, 2] = -(kk[:, None] >= kk[None, :]).astype(np.float32)
    ib[:, 3] = -1.0
    cons["cb"] = ib.astype(ml_dtypes.bfloat16)
    half = 128
    inv = (c.ROPE_BASE ** (-np.arange(half, dtype=np.float32) / half)).astype(np.float32)
    pos = np.concatenate([seg * c.SEG + np.arange(c.SEG), c.PAST + (np.arange(c.TS) % c.DSEQ)]).astype(np.float32)
    ang = (pos[None, :] * inv[:, None]).astype(np.float32)
    cons["rope"] = np.stack([np.cos(ang), np.sin(ang)]).astype(np.float32)
    lg = np.log(1.0 - 2.0 ** (-5.0 - np.arange(c.RH, dtype=np.float64)))
    j = np.arange(128)
    sc = c.DK ** -0.5
    decT = np.zeros((128, c.RH, 128), np.float32)
    for h in range(c.RH):
        decT[:, h, :] = np.where(j[:, None] <= j[None, :], np.exp(-(j[:, None] + 1.0) * lg[h]) * sc, 0.0)
    cons["decT"] = decT
    js = np.arange(c.TS); jl = js % c.DSEQ; bs = js // c.DSEQ
    decTs = np.zeros((128, c.RH, c.TS), np.float32)
    for h in range(c.RH):
        m = (bs[:, None] == bs[None, :]) & (jl[:, None] <= jl[None, :])
        decTs[:c.TS, h, :] = np.where(m, np.exp(-(jl[:, None] + 1.0) * lg[h]) * sc, 0.0)
    cons["decTs"] = decTs
    RH = c.RH
    rt = np.zeros((128, 8 * RH + c.CPS * RH + c.SPC + 8), np.float32)
    o = 0
    cons_off = {}
    def put(name, arr):
        nonlocal o
        n = arr.shape[1]; rt[:, o:o + n] = arr; cons_off[name] = o; o += n
    put("ci", np.exp((j[:, None] + 1.0) * lg[None, :]))
    put("ci2", np.exp(2.0 * (j[:, None] + 1.0) * lg[None, :]))
    put("kdec", np.exp((127.0 - j[:, None]) * lg[None, :]) * sc)
    put("g128", np.tile(np.exp(128.0 * lg)[None, :], (128, 1)))
    jl128 = np.arange(128) % c.DSEQ
    put("cis", np.exp((jl128[:, None] + 1.0) * lg[None, :]))
    put("cis2", np.exp(2.0 * (jl128[:, None] + 1.0) * lg[None, :]))
    put("kdecs", np.exp((c.DSEQ - 1.0 - jl128[:, None]) * lg[None, :]) * sc)
    put("g4", np.tile(np.exp(float(c.DSEQ) * lg)[None, :], (128, 1)))
    coef = np.zeros((c.CPS, RH))
    for r in range(c.CPS):
        if r < seg:
            coef[r] = np.exp(float(c.SEG) * (seg - 1 - r) * lg)
    put("coef", np.tile(coef.reshape(1, -1), (128, 1)))
    rowm = np.zeros((128, c.SPC))
    for b in range(c.SPC):
        rowm[b * c.DSEQ:(b + 1) * c.DSEQ, b] = 1.0
    put("rowm", rowm)
    put("eps", np.full((128, 1), c.EPS))
    put("one", np.full((128, 1), 1.0))
    segm = np.zeros((128, c.CPS + 1))
    for si in range(c.CPS):
        r = c.CPS - 1 - si
        segm[:, 1 + si] = 0.0 if r < seg else NEG
    put("segm", segm)
    cons["rt"] = rt[:, :o].copy()
    cons["_off"] = cons_off
    nr = c.BLK // 128
    qi = np.arange(c.BLK)
    mn = np.zeros((128, nr, c.BLK), np.float32)
    for r in range(nr):
        mn[:, r, :] = np.where((r * 128 + j[:, None]) >= qi[None, :], NEG, 0.0)
    cons["mneg"] = mn.astype(ml_dtypes.bfloat16)
    sm = np.full((128, c.SPC, c.SBH, c.DSEQ), NEG, np.float32)
    for b in range(c.SPC):
        for jj in range(c.DSEQ):
            for ii in range(c.DSEQ):
                if jj < ii:
                    sm[b * c.DSEQ + jj, b, :, ii] = 0.0
    cons["smask"] = sm.reshape(128, c.SPC, c.SBH * c.DSEQ)
    colm = np.zeros((128, c.SPC, c.TS), np.float32)
    for b in range(c.SPC):
        colm[:, b, b * c.DSEQ:(b + 1) * c.DSEQ] = 1.0
    cons["colm"] = colm.astype(ml_dtypes.bfloat16)
    return cons


CONST_NAMES = ("ident_f", "cb", "rope", "decT", "decTs", "rt", "mneg", "smask", "colm")


def build(cfg, stop_after=None):
    c = cfg
    nc = bass.Bass("TRN2", target_bir_lowering=False)
    P = Prog(nc)
    sb = SB(nc)
    fl = c.__dict__.get("flags", {})
    KC, JC, PC, RH, VC, T, SEG, TS, D = c.KC, c.JC, c.PC, c.RH, c.VC, c.T, c.SEG, c.TS, c.D
    cons0 = host_constants(c, 0)
    OFF = cons0["_off"]

    def din(name, shape, dt=F32):
        return nc.dram_tensor(name, list(shape), dt, kind="ExternalInput")

    def dout(name, shape, dt=F32):
        return nc.dram_tensor(name, list(shape), dt, kind="ExternalOutput")

    def dscr(name, shape, dt=BF16):
        return nc.dram_tensor(name, list(shape), dt)

    x_in = din("x", [T, D])
    p_in = din("p", [c.DEPTH, T, c.PLE])
    st_in = din("state_ret", [c.NA, c.SPC, RH, c.DK, c.DV])
    ck_in = din("cache_k", [c.NPOOL * c.PAGE, D])
    cv_in = din("cache_v", [c.NPOOL * c.PAGE, D])
    pt_in = din("page_table", [c.SPC, c.NPG], I32)
    gv_in = din("gvec", [128, (c.DEPTH * 4 + 2) * KC])
    ggn_in = din("g_ret_gn", [c.NA, c.RV])
    bsb_in = din("b_sb", [c.DEPTH - c.NA, c.SBH])
    W = {}
    W["ff_gate"] = [din("w_ff1_gate", [c.DEPTH, D, c.DFF]), din("w_ff2_gate", [c.DEPTH, D, c.DFF])]
    W["ff_up"] = [din("w_ff1_up", [c.DEPTH, D, c.DFF]), din("w_ff2_up", [c.DEPTH, D, c.DFF])]
    W["ff_down"] = [din("w_ff1_down", [c.DEPTH, c.DFF, D]), din("w_ff2_down", [c.DEPTH, c.DFF, D])]
    W["ple_up"] = din("w_ple_up", [c.DEPTH, c.PLE, D])
    W["ple_gate"] = din("w_ple_gate", [c.DEPTH, D, D])
    W["ret_in"] = din("w_ret_in", [c.NA, D, 2 * c.RQK + 2 * c.RV])
    W["ret_out"] = din("w_ret_out", [c.NA, c.RV, D])
    W["kv"] = din("w_kv", [D, 2 * D])
    W["sb_q"] = din("w_sb_q", [c.DEPTH - c.NA, D, D])
    W["sb_out"] = din("w_sb_out", [c.DEPTH - c.NA, D, D])
    cin = {}
    for nm in CONST_NAMES:
        a = cons0[nm]
        cin[nm] = din("c_" + nm, a.shape, BF16 if a.dtype == ml_dtypes.bfloat16 else F32)

    y_out = dout("y", [T, D])
    sp_out = dout("s_prompt", [c.NA, RH, c.DK, c.DV])
    ss_out = dout("s_sample", [c.NA, c.SPC, RH, c.DK, c.DV])
    k_out = dout("k_new", [T, D])
    v_out = dout("v_new", [T, D])

    xT = sb.alloc("xT", [128, KC, T], F32)
    ident_f = sb.alloc("ident_f", [128, 128], F32)
    cb = sb.alloc("cb", [128, 4, 128], BF16)
    ident_b, ones_b, ntri_b, nones_b = cb[:, 0, :], cb[:, 1, :], cb[:, 2, :], cb[:, 3, :]
    rt = sb.alloc("rt", list(cons0["rt"].shape), F32)
    gv = sb.alloc("gv", [128, (c.DEPTH * 4 + 2) * KC], F32)
    NW = 2
    JW = 256 if JC % 2 == 0 else 128
    OW = 128
    WSLOT = max(KC * 512, 2 * KC * JW, JC * OW, VC * OW)
    wring = [sb.alloc("wr%d" % i, [128, WSLOT], BF16) for i in range(NW)]
    wbufs = [P.buf("wr%d" % i) for i in range(NW)]
    wstate = {"i": 0}
    kTs = sb.alloc("kTs", [128, KC, TS], BF16)
    vs_tok = sb.alloc("vs_tok", [128, D], BF16)
    B_kTs = P.buf("kTs"); B_vs = P.buf("vs")
    persist_mark = sb.mark()

    pbank = [nc.alloc_psum_tensor("pb%d" % i, [128, 512], F32) for i in range(7)]
    pbf = nc.alloc_psum_tensor("pbf", [128, 1024], BF16)
    Bp = [P.buf("pb%d" % i) for i in range(7)]
    Bpbf = P.buf("pbf")

    B_const = P.buf("const")
    B_x = [[P.buf("x%d_%d" % (b, fc)) for fc in range(KC)] for b in range(c.NB + 1)]

    def rtc(name, col, n=1, rows=128):
        o = OFF[name] + col
        return rt[0:rows, o:o + n]

    blocks = [(b * c.BLK, c.BLK) for b in range(c.NB)] + [(SEG, TS)]
    NBT = len(blocks)

    def xbufs(bi):
        return B_x[bi]

    class WS:
        pass

    wscr = {}

    wkey = {"k": "ws0"}

    def mk_w(name, src2d, K, N, CW):
        ws = WS()
        ws.K = K; ws.N = N; ws.CW = CW; ws.Kc = K // 128; ws.NG = N // CW
        ws.t = dscr("ws_" + name, [ws.NG, 128, ws.Kc, CW])
        ws.buf = P.buf("ws_" + name)
        srcv = src2d.rearrange("(kc p) n -> p kc n", p=128)
        for g in range(ws.NG):
            P.add("pool", (lambda e, g=g, ws=ws, srcv=srcv: e.dma_start(out=ws.t[g], in_=srcv[:, :, g * CW:(g + 1) * CW])),
                  r=(), w=(ws.buf,), dma=wkey["k"])
        wscr[name] = ws
        return ws

    def cast_layer_weights(l):
        wkey["k"] = "ws%d" % l
        for wh in range(2):
            mk_w("g%d_%d" % (l, wh), W["ff_gate"][wh][l], D, c.DFF, JW)
            mk_w("u%d_%d" % (l, wh), W["ff_up"][wh][l], D, c.DFF, JW)
            mk_w("d%d_%d" % (l, wh), W["ff_down"][wh][l], c.DFF, D, OW)
            if wh == 0:
                if l < c.NA:
                    mk_w("ri%d" % l, W["ret_in"][l], D, 2 * c.RQK + 2 * c.RV, 512)
                    mk_w("ro%d" % l, W["ret_out"][l], c.RV, D, OW)
                else:
                    mk_w("sq%d" % l, W["sb_q"][l - c.NA], D, D, min(512, D))
                    mk_w("so%d" % l, W["sb_out"][l - c.NA], D, D, OW)
        mk_w("pu%d" % l, W["ple_up"][l], c.PLE, D, min(512, D))
        mk_w("pg%d" % l, W["ple_gate"][l], D, D, min(512, D))
        if l == c.NA - 1:
            mk_w("kv", W["kv"], D, 2 * D, min(512, D))

    p_scr = dscr("p_scr", [c.DEPTH, T, c.PLE])
    B_pscr = P.buf("p_scr")

    def load_w(ws, g):
        i = wstate["i"] % NW; wstate["i"] += 1
        view = wring[i][:, 0:ws.Kc * ws.CW].rearrange("p (k c) -> p k c", c=ws.CW)
        P.add("sp", (lambda e, view=view, ws=ws, g=g: e.dma_start(out=view, in_=ws.t[g])),
              r=(ws.buf,), w=(wbufs[i],), dma="wr%d" % i)
        return view, wbufs[i]

    def load_w2(wa, wb_, g):
        i = wstate["i"] % NW; wstate["i"] += 1
        sz = wa.Kc * wa.CW
        va = wring[i][:, 0:sz].rearrange("p (k c) -> p k c", c=wa.CW)
        vb = wring[i][:, sz:2 * sz].rearrange("p (k c) -> p k c", c=wa.CW)
        P.add("sp", (lambda e: e.dma_start(out=va, in_=wa.t[g])), r=(wa.buf,), w=(wbufs[i],), dma="wr%d" % i)
        P.add("sp", (lambda e: e.dma_start(out=vb, in_=wb_.t[g])), r=(wb_.buf,), w=(wbufs[i],), dma="wr%d" % i)
        return va, vb, wbufs[i]

    def pe_group(fl, r, w):
        def fn(e, fl=fl):
            ins = None
            for f in fl: ins = f(e)
            return ins
        P.add("pe", fn, r=r, w=w)

    def MM(out, lhsT, rhs, start, stop):
        return lambda e: e.matmul(out=out, lhsT=lhsT, rhs=rhs, start=start, stop=stop)

    def TR(out, in_, ident):
        return lambda e: e.transpose(out=out, in_=in_, identity=ident)

    def ACT(out, in_, func, r, w, bias=None, scale=None):
        kw = {}
        if bias is not None: kw["bias"] = bias
        if scale is not None: kw["scale"] = scale
        P.add("act", (lambda e: e.activation(out=out, in_=in_, func=func, **kw)), r=r, w=w)

    def TT(eng, out, in0, in1, op, r, w):
        P.add(eng, (lambda e: e.tensor_tensor(out=out, in0=in0, in1=in1, op=op)), r=r, w=w)

    def TS_(eng, out, in0, s1, s2, op0, op1, r, w):
        if s2 is None:
            P.add(eng, (lambda e: e.tensor_scalar(out=out, in0=in0, scalar1=s1, scalar2=None, op0=op0)), r=r, w=w)
        else:
            P.add(eng, (lambda e: e.tensor_scalar(out=out, in0=in0, scalar1=s1, scalar2=s2, op0=op0, op1=op1)), r=r, w=w)

    def STT(eng, out, in0, scalar, in1, op0, op1, r, w):
        P.add(eng, (lambda e: e.scalar_tensor_tensor(out=out, in0=in0, scalar=scalar, in1=in1, op0=op0, op1=op1)), r=r, w=w)

    def CP(eng, out, in_, r, w):
        if eng == "act":
            P.add(eng, (lambda e: e.copy(out=out, in_=in_)), r=r, w=w)
        else:
            P.add(eng, (lambda e: e.tensor_copy(out=out, in_=in_)), r=r, w=w)

    def DMA(eng, out, in_, r, w, key):
        P.add(eng, (lambda e: e.dma_start(out=out, in_=in_)), r=r, w=w, dma=key)

    def DMAT(out, in_, r, w, key):
        P.add("sp", (lambda e: e.dma_start_transpose(out=out, in_=in_)), r=r, w=w, dma=key)

    pstate = {"i": 0}

    def nbank():
        i = pstate["i"] % 7; pstate["i"] += 1
        return pbank[i], Bp[i]

    cast_layer_weights(0)
    for l in range(c.DEPTH):
        P.add("pool", (lambda e, l=l: e.dma_start(out=p_scr[l], in_=p_in[l])), r=(), w=(B_pscr,), dma="p_scr")
    DMA("sp", ident_f[:, :], cin["ident_f"][:, :], (), (B_const,), "const")
    DMA("sp", cb[:, :, :], cin["cb"][:, :, :], (), (B_const,), "const")
    DMA("sp", rt[:, :], cin["rt"][:, :], (), (B_const,), "const")
    DMA("sp", gv[:, :], gv_in[:, :], (), (B_const,), "const")

    m0 = sb.mark()
    xst = [sb.alloc("xst%d" % i, [128, D], F32) for i in range(2)]
    Bxst = [P.buf("xst%d" % i) for i in range(2)]
    ti = 0
    for bi, (t0, n) in enumerate(blocks):
        for tt in range(0, n, 128):
            rows = min(128, n - tt)
            s = ti % 2; ti += 1
            DMA("sp", xst[s][0:rows, :], x_in[t0 + tt:t0 + tt + rows, :], (), (Bxst[s],), "xst%d" % s)
            for f0 in range(0, KC, 4):
                nf = min(4, KC - f0)
                bank, bb = nbank()
                pe_group([TR(bank[:, k * 128:k * 128 + rows], xst[s][0:rows, (f0 + k) * 128:(f0 + k + 1) * 128], ident_f[0:rows, 0:rows])
                          for k in range(nf)], r=(Bxst[s], B_const), w=(bb,))
                CP("dve" if (f0 // 4) % 2 == 0 else "act",
                   xT[:, f0:f0 + nf, t0 + tt:t0 + tt + rows],
                   bank[:, 0:nf * 128].rearrange("p (k t) -> p k t", t=128)[:, :, 0:rows],
                   (bb,), tuple(B_x[bi][f0:f0 + nf]))
    sb.reset(m0)
    P.barrier()

    def rmsnorm(bi, t0, n, gcol, hT, hbuf, work, out_f32=False):
        sq, Bsq, lnv, Blnv, rstd, Brstd = work
        bank, bb = nbank()
        for fc in range(KC):
            s = fc % 2
            if fc % 2 == 0:
                ACT(sq[s][:, 0:n], xT[:, fc, t0:t0 + n], AF.Square, (B_x[bi][fc],), (Bsq[s],))
            else:
                TT("pool", sq[s][:, 0:n], xT[:, fc, t0:t0 + n], xT[:, fc, t0:t0 + n], ALU.mult, (B_x[bi][fc],), (Bsq[s],))
            pe_group([MM(bank[:, 0:n], ones_b, sq[s][:, 0:n], fc == 0, fc == KC - 1)], r=(Bsq[s], B_const), w=(bb,))
        ACT(lnv[:, 0:n], bank[:, 0:n], AF.Ln, (bb, B_const), (Blnv,), bias=rtc("eps", 0), scale=1.0 / D)
        ACT(rstd[:, 0:n], lnv[:, 0:n], AF.Exp, (Blnv,), (Brstd,), scale=-0.5)
        for fc in range(KC):
            STT("dve", hT[:, fc, 0:n], xT[:, fc, t0:t0 + n], gv[:, gcol + fc:gcol + fc + 1], rstd[:, 0:n],
                ALU.mult, ALU.mult, (B_x[bi][fc], Brstd, B_const), (hbuf,))

    def norm_work():
        sq = [sb.alloc("sq%d" % i, [128, 512], BF16) for i in range(2)]
        Bsq = [P.buf("sq%d" % i) for i in range(2)]
        lnv = sb.alloc("lnv", [128, 512], F32); rstd = sb.alloc("rstd", [128, 512], F32)
        return (sq, Bsq, lnv, P.buf("lnv"), rstd, P.buf("rstd"))

    def gcol_norm(l, i): return (l * 3 + i) * KC
    def gcol_ple(l): return (c.DEPTH * 3 + l) * KC
    GCOL_KV = (c.DEPTH * 4) * KC
    GCOL_FIN = (c.DEPTH * 4 + 1) * KC

    def ffn_groups():
        groups = []; cur = []; tot = 0
        for bi, (t0, n) in enumerate(blocks):
            if tot + n > 1100 and cur:
                groups.append(cur); cur = []; tot = 0
            cur.append((bi, t0, n)); tot += n
        groups.append(cur)
        return groups

    def ffn(l, wh):
        m = sb.mark()
        wk = norm_work()
        groups = ffn_groups()
        gmax = max(sum(n for (_, _, n) in g) for g in groups)
        hT = sb.alloc("hT", [128, KC, gmax], BF16)
        act = sb.alloc("act", [128, JC, gmax], BF16)
        sg = [sb.alloc("sg%d" % i, [128, 512], F32) for i in range(2)]
        Bsg = [P.buf("sg%d" % i) for i in range(2)]
        wg, wu, wd = wscr["g%d_%d" % (l, wh)], wscr["u%d_%d" % (l, wh)], wscr["d%d_%d" % (l, wh)]
        for grp in groups:
            Bh = [P.buf("hT%d" % k) for k in range(len(grp))]
            Bact = [[P.buf("act") for _ in range(JC)] for _ in grp]
            offs = []; o = 0
            for k, (bi, t0, n) in enumerate(grp):
                offs.append(o)
                rmsnorm(bi, t0, n, gcol_norm(l, 0 if wh == 0 else 2), hT[:, :, o:o + n], Bh[k], wk)
                o += n
            si = 0
            for jg in range(wg.NG):
                gt, ut, gb = load_w2(wg, wu, jg)
                ub = gb
                for jj in range(JW // 128):
                    j = jg * (JW // 128) + jj
                    for k, (bi, t0, n) in enumerate(grp):
                        o = offs[k]
                        b1, bb1 = nbank(); b2, bb2 = nbank()
                        pe_group([MM(b1[:, 0:n], gt[:, kc, jj * 128:(jj + 1) * 128], hT[:, kc, o:o + n], kc == 0, kc == KC - 1) for kc in range(KC)],
                                 r=(gb, Bh[k]), w=(bb1,))
                        pe_group([MM(b2[:, 0:n], ut[:, kc, jj * 128:(jj + 1) * 128], hT[:, kc, o:o + n], kc == 0, kc == KC - 1) for kc in range(KC)],
                                 r=(ub, Bh[k]), w=(bb2,))
                        s = si % 2; si += 1
                        ACT(sg[s][:, 0:n], b1[:, 0:n], AF.Silu, (bb1,), (Bsg[s],))
                        TT("dve", act[:, j, o:o + n], sg[s][:, 0:n], b2[:, 0:n], ALU.mult, (Bsg[s], bb2), (Bact[k][j],))
            for og in range(wd.NG):
                dt_, db = load_w(wd, og)
                for oo in range(OW // 128):
                    oc = og * (OW // 128) + oo
                    for k, (bi, t0, n) in enumerate(grp):
                        o = offs[k]
                        b1, bb1 = nbank()
                        pe_group([MM(b1[:, 0:n], dt_[:, j, oo * 128:(oo + 1) * 128], act[:, j, o:o + n], j == 0, j == JC - 1) for j in range(JC)],
                                 r=(db,) + tuple(Bact[k]), w=(bb1,))
                        STT("dve", xT[:, oc, t0:t0 + n], b1[:, 0:n], 0.5, xT[:, oc, t0:t0 + n], ALU.mult, ALU.add,
                            (bb1, B_x[bi][oc]), (B_x[bi][oc],))
        sb.reset(m)
        P.barrier()

    def ple(l):
        m = sb.mark()
        wk = norm_work()
        hT = sb.alloc("hT", [128, KC, 512], BF16); Bh = P.buf("hT")
        pT = sb.alloc("pT", [128, PC, 512], BF16); BpT = P.buf("pT")
        sgt = sb.alloc("sgt", [128, 512], F32); Bsgt = P.buf("sgt")
        upt = sb.alloc("upt", [128, 512], F32); Bupt = P.buf("upt")
        wu, wg = wscr["pu%d" % l], wscr["pg%d" % l]
        for bi, (t0, n) in enumerate(blocks):
            rmsnorm(bi, t0, n, gcol_ple(l), hT, Bh, wk)
            for pc in range(PC):
                DMAT(pT[:, pc, 0:n], p_scr[l, t0:t0 + n, pc * 128:(pc + 1) * 128], (B_pscr,), (BpT,), "pT")
            CWp = wu.CW
            for g in range(wu.NG):
                ut, ub = load_w(wu, g)
                gt, gb = load_w(wg, g)
                for oo in range(CWp // 128):
                    oc = g * (CWp // 128) + oo
                    b1, bb1 = nbank(); b2, bb2 = nbank()
                    pe_group([MM(b1[:, 0:n], ut[:, pc, oo * 128:(oo + 1) * 128], pT[:, pc, 0:n], pc == 0, pc == PC - 1) for pc in range(PC)],
                             r=(ub, BpT), w=(bb1,))
                    pe_group([MM(b2[:, 0:n], gt[:, kc, oo * 128:(oo + 1) * 128], hT[:, kc, 0:n], kc == 0, kc == KC - 1) for kc in range(KC)],
                             r=(gb, Bh), w=(bb2,))
                    ACT(sgt[:, 0:n], b2[:, 0:n], AF.Sigmoid, (bb2,), (Bsgt,))
                    TT("dve", upt[:, 0:n], sgt[:, 0:n], b1[:, 0:n], ALU.mult, (Bsgt, bb1), (Bupt,))
                    TT("dve", xT[:, oc, t0:t0 + n], xT[:, oc, t0:t0 + n], upt[:, 0:n], ALU.add, (Bupt, B_x[bi][oc]), (B_x[bi][oc],))
        sb.reset(m)
        P.barrier()


    groups_rep = [[sq * c.CPS + r for r in range(c.CPS)] for sq in range(c.BATCH)]

    def retention(l):
        m = sb.mark()
        wk = norm_work()
        RB = c.RBLK
        wi, wo = wscr["ri%d" % l], wscr["ro%d" % l]
        NQG = c.RQK // 512
        H2 = 2 * RH
        hT = sb.alloc("r_hT", [128, KC, RB], BF16); Bh = P.buf("r_hT")
        qkT = sb.alloc("r_qkT", [128, 2 * H2, RB], BF16); Bqk = [P.buf("r_qk%d" % i) for i in range(2 * H2)]
        cos = sb.alloc("r_cos", [128, RB], F32); sin = sb.alloc("r_sin", [128, RB], F32)
        Bcos = P.buf("cos"); Bsin = P.buf("sin")
        tmp = [sb.alloc("r_t%d" % i, [128, RB], F32) for i in range(4)]; Bt = [P.buf("r_t%d" % i) for i in range(4)]
        NCH = RB // 128
        v_tok = sb.alloc("r_v", [128, NCH, c.RV], BF16); Bv = [[P.buf("r_v") for _ in range(RH)] for _ in range(NCH)]
        sg_tok = sb.alloc("r_sg", [128, NCH, c.RV], BF16); Bsgt = [[P.buf("r_sg") for _ in range(RH)] for _ in range(NCH)]
        k_tok = sb.alloc("r_kt", [128, c.RQK], BF16); Bkt = P.buf("r_kt")
        sT = sb.alloc("r_sT", [128, RH, 128], BF16); BsT = P.buf("r_sT")
        on = [sb.alloc("r_on%d" % i, [128, 512], F32) for i in range(2)]; Bon = [P.buf("r_on%d" % i) for i in range(2)]
        og = sb.alloc("r_og", [128, c.RV], BF16); Bog = P.buf("r_og")
        ogT = sb.alloc("r_ogT", [128, VC, RB], BF16); BogT = P.buf("r_ogT")
        Sx = [sb.alloc("r_S%d" % i, [128, H2, 512], F32) for i in range(2)]
        BSx = [[P.buf("r_S%d_%d" % (i, hd)) for hd in range(H2)] for i in range(2)]
        S_bf = sb.alloc("r_Sbf", [128, H2, 512], BF16); BSbf = [P.buf("r_Sbf%d" % hd) for hd in range(H2)]
        ggn = sb.alloc("r_ggn", [128, c.RV], F32); Bggn = P.buf("r_ggn")
        decT = sb.alloc("r_decT", [128, RH, 128], F32); decTs = sb.alloc("r_decTs", [128, RH, TS], F32); Bdec = P.buf("r_dec")
        colm = sb.alloc("r_colm", [128, c.SPC, TS], BF16)
        stats = sb.alloc("r_stats", [128, 6], F32); mv = sb.alloc("r_mv", [128, 2], F32); sm = sb.alloc("r_sm", [128, 8], F32)
        Bst = P.buf("r_stats"); Bmv = P.buf("r_mv"); Bsm = P.buf("r_sm")
        qb = [sb.alloc("r_qb%d" % i, [128, H2, TS], BF16) for i in range(2)]; Bqb = [P.buf("r_qb%d" % i) for i in range(2)]
        kb = sb.alloc("r_kb", [128, c.RQK], BF16); Bkb = P.buf("r_kb")

        DMA("sp", ggn[:, :], ggn_in[l:l + 1, :].broadcast_to([128, c.RV]), (), (Bggn,), "r_c")
        DMA("sp", decT[:, :, :], cin["decT"][:, :, :], (), (Bdec,), "r_c")
        DMA("sp", decTs[:, :, :], cin["decTs"][:, :, :], (), (Bdec,), "r_c")
        DMA("sp", colm[:, :, :], cin["colm"][:, :, :], (), (Bdec,), "r_c")
        S = Sx[0]; BS = BSx[0]
        bank_o = [pbank[h] for h in range(RH)]; Bbo = [Bp[h] for h in range(RH)]
        bank_s, Bbs = pbank[4], Bp[4]
        ubanks = [(pbank[5], Bp[5]), (pbank[6], Bp[6])]
        ust = {"i": 0}
        HH = H2 // 2
        agin = [dscr("ag_s_in%d_%d" % (l, ch), [HH * 128, 512], F32) for ch in range(2)]
        agout = [dscr("ag_s_out%d_%d" % (l, ch), [c.CPS * HH * 128, 512], F32) for ch in range(2)]
        Bagin = P.buf("agin"); Bagout = P.buf("agout")

        def ret_block(t0, n, C, sample, p2):
            bi = c.NB if sample else t0 // c.BLK
            nch = n // C
            rmsnorm(bi, t0, n, gcol_norm(l, 1), hT, Bh, wk)
            DMA("sp", cos[:, 0:n], cin["rope"][0, :, t0:t0 + n], (), (Bcos,), "r_cos")
            DMA("sp", sin[:, 0:n], cin["rope"][1, :, t0:t0 + n], (), (Bsin,), "r_sin")
            glist = ([("q", g) for g in range(NQG)] if p2 else []) + [("k", NQG + g) for g in range(NQG)]
            for (kind, g) in glist:
                wt, wb = load_w(wi, g)
                for hh in range(2):
                    h = (g % NQG) * 2 + hh
                    bA, bbA = nbank(); bB, bbB = nbank()
                    pe_group([MM(bA[:, 0:n], wt[:, kc, (2 * hh) * 128:(2 * hh + 1) * 128], hT[:, kc, 0:n], kc == 0, kc == KC - 1) for kc in range(KC)],
                             r=(wb, Bh), w=(bbA,))
                    pe_group([MM(bB[:, 0:n], wt[:, kc, (2 * hh + 1) * 128:(2 * hh + 2) * 128], hT[:, kc, 0:n], kc == 0, kc == KC - 1) for kc in range(KC)],
                             r=(wb, Bh), w=(bbB,))
                    base = (0 if kind == "q" else H2) + 2 * h
                    TT("dve", tmp[0][:, 0:n], bA[:, 0:n], cos[:, 0:n], ALU.mult, (bbA, Bcos), (Bt[0],))
                    TT("dve", tmp[1][:, 0:n], bB[:, 0:n], sin[:, 0:n], ALU.mult, (bbB, Bsin), (Bt[1],))
                    TT("pool", qkT[:, base, 0:n], tmp[0][:, 0:n], tmp[1][:, 0:n], ALU.subtract, (Bt[0], Bt[1]), (Bqk[base],))
                    TT("dve", tmp[2][:, 0:n], bB[:, 0:n], cos[:, 0:n], ALU.mult, (bbB, Bcos), (Bt[2],))
                    TT("dve", tmp[3][:, 0:n], bA[:, 0:n], sin[:, 0:n], ALU.mult, (bbA, Bsin), (Bt[3],))
                    TT("pool", qkT[:, base + 1, 0:n], tmp[2][:, 0:n], tmp[3][:, 0:n], ALU.add, (Bt[2], Bt[3]), (Bqk[base + 1],))
            for kind in (("v", "g") if p2 else ("v",)):
                for h in range(RH):
                    wt, wb = load_w(wi, 2 * NQG + (0 if kind == "v" else RH) + h)
                    for ci in range(nch):
                        c0 = ci * C
                        bk, bbk = nbank()
                        pe_group([MM(bk[0:C, :], hT[:, kc, c0:c0 + C], wt[:, kc, :], kc == 0, kc == KC - 1) for kc in range(KC)],
                                 r=(wb, Bh), w=(bbk,))
                        if kind == "v":
                            CP("act", v_tok[0:C, ci, h * 512:(h + 1) * 512], bk[0:C, :], (bbk,), (Bv[ci][h],))
                        else:
                            ACT(sg_tok[0:C, ci, h * 512:(h + 1) * 512], bk[0:C, :], AF.Silu, (bbk,), (Bsgt[ci][h],))
            dT = decTs if sample else decT
            tci, tci2, tkd = ("cis", "cis2", "kdecs") if sample else ("ci", "ci2", "kdec")
            for ci in range(nch):
                c0 = ci * C
                pe_group([TR(pbf[0:C, hc * 128:(hc + 1) * 128], qkT[:, H2 + hc, c0:c0 + C], ident_b) for hc in range(H2)],
                         r=tuple(Bqk[H2:2 * H2]) + (B_const,), w=(Bpbf,))
                for h in range(RH):
                    TS_("dve", k_tok[0:C, h * 256:(h + 1) * 256], pbf[0:C, h * 256:(h + 1) * 256], rtc(tkd, h, 1, C), None, ALU.mult, None,
                        (Bpbf, B_const), (Bkt,))
                if p2:
                    pe_group([MM(bank_s[0:C, h * C:(h + 1) * C], qkT[:, H2 + 2 * h + hf, c0:c0 + C], qkT[:, 2 * h + hf, c0:c0 + C], hf == 0, hf == 1)
                              for h in range(RH) for hf in range(2)], r=tuple(Bqk), w=(Bbs,))
                    TT("dve", sT[0:C, :, 0:C], bank_s[0:C, 0:RH * C].rearrange("p (h i) -> p h i", i=C), dT[0:C, :, 0:C], ALU.mult,
                       (Bbs, Bdec), (BsT,))
                    for h in range(RH):
                        fl_ = [MM(bank_o[h][0:C, :], sT[0:C, h, 0:C], v_tok[0:C, ci, h * 512:(h + 1) * 512], True, False)]
                        if not sample:
                            fl_ += [MM(bank_o[h][0:C, :], qkT[:, 2 * h + dc, c0:c0 + C], S_bf[:, 2 * h + dc, :], False, dc == 1) for dc in range(2)]
                        pe_group(fl_, r=(BsT, Bv[ci][h], Bqk[2 * h], Bqk[2 * h + 1], BSbf[2 * h], BSbf[2 * h + 1]), w=(Bbo[h],))
                if sample:
                    for b in range(c.SPC):
                        sx = Sx[b % 2]; bsx = BSx[b % 2]
                        DMA("sp", sx[:, :, :], st_in[l, b].rearrange("h (dc p) v -> p (h dc) v", p=128), (), tuple(bsx), "r_S%d" % (b % 2))
                        for hd in range(H2):
                            CP("act" if hd % 2 == 0 else "pool", S_bf[:, hd, :], sx[:, hd, :], (bsx[hd],), (BSbf[hd],))
                        TT("dve", qb[b % 2][:, :, :], qkT[:, 0:H2, 0:TS], colm[:, b:b + 1, :].broadcast_to([128, H2, TS]), ALU.mult,
                           tuple(Bqk[0:H2]) + (Bdec,), (Bqb[b % 2],))
                        for h in range(RH):
                            pe_group([MM(bank_o[h][0:C, :], qb[b % 2][:, 2 * h + dc, :], S_bf[:, 2 * h + dc, :], False, (b == c.SPC - 1 and dc == 1)) for dc in range(2)],
                                     r=(Bqb[b % 2], BSbf[2 * h], BSbf[2 * h + 1]), w=(Bbo[h],))
                        TS_("dve", kb[0:C, :], k_tok[0:C, :], rtc("rowm", b, 1, C), None, ALU.mult, None, (Bkt, B_const), (Bkb,))
                        for h in range(RH):
                            for dc in range(2):
                                hd = 2 * h + dc
                                bu, bbu = ubanks[ust["i"] % 2]; ust["i"] += 1
                                pe_group([MM(bu[:, :], kb[0:C, hd * 128:(hd + 1) * 128], v_tok[0:C, ci, h * 512:(h + 1) * 512], True, True)],
                                         r=(Bkb, Bv[ci][h]), w=(bbu,))
                                STT("dve", sx[:, hd, :], sx[:, hd, :], rtc("g4", h), bu[:, :], ALU.mult, ALU.add, (bbu, bsx[hd], B_const, BSbf[hd]), (bsx[hd],))
                        DMA("sp", ss_out[l, b].rearrange("h (dc p) v -> p (h dc) v", p=128), sx[:, :, :], tuple(bsx), (), "r_So%d" % (b % 2))
                if p2:
                    for h in range(RH):
                        P.add("dve", (lambda e, h=h: e.bn_stats(out=stats[0:C, :], in_=bank_o[h][0:C, :])), r=(Bbo[h],), w=(Bst,))
                        P.add("dve", (lambda e: e.bn_aggr(out=mv[0:C, :], in_=stats[0:C, :])), r=(Bst,), w=(Bmv,))
                        TS_("dve", sm[0:C, 0:1], mv[0:C, 1:2], rtc(tci2, h, 1, C), rtc("eps", 0, 1, C), ALU.mult, ALU.add, (Bmv, B_const), (Bsm,))
                        ACT(sm[0:C, 1:2], sm[0:C, 0:1], AF.Ln, (Bsm,), (Bsm,))
                        ACT(sm[0:C, 2:3], sm[0:C, 1:2], AF.Exp, (Bsm,), (Bsm,), scale=-0.5)
                        TT("dve", sm[0:C, 3:4], sm[0:C, 2:3], rtc(tci, h, 1, C), ALU.mult, (Bsm, B_const), (Bsm,))
                        STT("dve", sm[0:C, 4:5], mv[0:C, 0:1], -1.0, sm[0:C, 3:4], ALU.mult, ALU.mult, (Bmv, Bsm), (Bsm,))
                        ACT(on[0][0:C, :], bank_o[h][0:C, :], AF.Identity, (Bbo[h], Bsm), (Bon[0],), bias=sm[0:C, 4:5], scale=sm[0:C, 3:4])
                        TT("pool", on[1][0:C, :], on[0][0:C, :], ggn[0:C, h * 512:(h + 1) * 512], ALU.mult, (Bon[0], Bggn), (Bon[1],))
                        TT("dve", og[0:C, h * 512:(h + 1) * 512], on[1][0:C, :], sg_tok[0:C, ci, h * 512:(h + 1) * 512], ALU.mult,
                           (Bon[1], Bsgt[ci][h]), (Bog,))
                    for v0 in range(0, VC, 8):
                        nv = min(8, VC - v0)
                        pe_group([TR(pbf[:, k * C:(k + 1) * C], og[0:C, (v0 + k) * 128:(v0 + k + 1) * 128], ident_b[0:C, 0:C]) for k in range(nv)],
                                 r=(Bog, B_const), w=(Bpbf,))
                        CP("act", ogT[:, v0:v0 + nv, c0:c0 + C], pbf[:, 0:nv * C].rearrange("p (k t) -> p k t", t=C), (Bpbf,), (BogT,))
                if not sample:
                    for h in range(RH):
                        for dc in range(2):
                            hd = 2 * h + dc
                            bu, bbu = ubanks[ust["i"] % 2]; ust["i"] += 1
                            pe_group([MM(bu[:, :], k_tok[0:C, hd * 128:(hd + 1) * 128], v_tok[0:C, ci, h * 512:(h + 1) * 512], True, True)],
                                     r=(Bkt, Bv[ci][h]), w=(bbu,))
                            STT("dve", S[:, hd, :], S[:, hd, :], rtc("g128", h), bu[:, :], ALU.mult, ALU.add, (bbu, BS[hd], B_const, BSbf[hd]), (BS[hd],))
                            if p2:
                                CP("act" if dc == 0 else "pool", S_bf[:, hd, :], S[:, hd, :], (BS[hd],), (BSbf[hd],))
            if p2:
                for g in range(wo.NG):
                    wt, wb = load_w(wo, g)
                    for oo in range(OW // 128):
                        oc = g * (OW // 128) + oo
                        bk, bbk = nbank()
                        pe_group([MM(bk[:, 0:n], wt[:, vc, oo * 128:(oo + 1) * 128], ogT[:, vc, 0:n], vc == 0, vc == VC - 1) for vc in range(VC)],
                                 r=(wb, BogT), w=(bbk,))
                        TT("dve", xT[:, oc, t0:t0 + n], xT[:, oc, t0:t0 + n], bk[:, 0:n], ALU.add, (bbk, B_x[bi][oc]), (B_x[bi][oc],))

        for hd in range(H2):
            P.add("dve", (lambda e, hd=hd: e.memset(S[:, hd, :], 0.0)), r=(), w=(BS[hd],))
        for rb in range(c.NRB):
            ret_block(rb * RB, RB, 128, False, False)
        for ch in range(2):
            DMA("sp", agin[ch].ap().rearrange("(hd p) v -> p hd v", p=128), S[:, ch * HH:(ch + 1) * HH, :], tuple(BS), (Bagin,), "r_ag")
        for ch in range(2):
            P.add("pool", (lambda e, ch=ch: e.collective_compute("AllGather", ALU.bypass, replica_groups=groups_rep,
                                                                  ins=[agin[ch].ap().opt()], outs=[agout[ch].ap().opt()])),
                  r=(Bagin,), w=(Bagout,), dma="cc_s%d_%d" % (l, ch), inc=1)
        for hd in range(H2):
            P.add("dve", (lambda e, hd=hd: e.memset(S[:, hd, :], 0.0)), r=(), w=(BS[hd],))
        for r_ in range(c.CPS):
            for ch in range(2):
                DMA("sp", Sx[1][:, ch * HH:(ch + 1) * HH, :], agout[ch].ap()[r_ * HH * 128:(r_ + 1) * HH * 128, :].rearrange("(hd p) v -> p hd v", p=128),
                    (Bagout,), tuple(BSx[1]), "r_S1")
            for hd in range(H2):
                STT("dve", S[:, hd, :], Sx[1][:, hd, :], rtc("coef", r_ * RH + hd // 2), S[:, hd, :], ALU.mult, ALU.add,
                    (BSx[1][hd], BS[hd], B_const), (BS[hd],))
        for hd in range(H2):
            CP("act" if hd % 2 == 0 else "pool", S_bf[:, hd, :], S[:, hd, :], (BS[hd],), (BSbf[hd],))
        for rb in range(c.NRB):
            ret_block(rb * RB, RB, 128, False, True)
        DMA("sp", sp_out[l].rearrange("h (dc p) v -> p (h dc) v", p=128), S[:, :, :], tuple(BS), (), "r_So0")
        ret_block(SEG, TS, TS, True, True)
        sb.reset(m)
        P.barrier()


    NCH = D // 256
    ag_kT_in = [dscr("ag_kT_in%d" % ch, [256, SEG]) for ch in range(NCH)]
    ag_kT_out = [dscr("ag_kT_out%d" % ch, [c.CPS * 256, SEG]) for ch in range(NCH)]
    agvi = [dscr("agvi%d" % ch, [SEG, 256]) for ch in range(NCH)]
    agvo = [dscr("agvo%d" % ch, [c.CPS * SEG, 256]) for ch in range(NCH)]
    B_agk_in = P.buf("agk_in"); B_agk_out = P.buf("agk_out"); B_agv_in = P.buf("agv_in"); B_agv_out = P.buf("agv_out")

    def kv_phase():
        m = sb.mark()
        wk = norm_work()
        wkv = wscr["kv"]; CW = wkv.CW; NGK = D // CW
        hT = sb.alloc("kv_hT", [128, KC, 512], BF16); Bh = P.buf("kv_hT")
        kst = [sb.alloc("kv_kst%d" % i, [128, 512], BF16) for i in range(2)]; Bkst = [P.buf("kst%d" % i) for i in range(2)]
        st32 = [sb.alloc("kv_st%d" % i, [128, 512], F32) for i in range(2)]; Bst32 = [P.buf("st32%d" % i) for i in range(2)]
        vst = [sb.alloc("kv_vst%d" % i, [128, 512], BF16) for i in range(2)]; Bvst = [P.buf("vst%d" % i) for i in range(2)]
        i1 = 0; i2 = 0
        for bi, (t0, n) in enumerate(blocks):
            sample = bi == c.NB
            rmsnorm(bi, t0, n, GCOL_KV, hT, Bh, wk)
            for g in range(NGK if fl.get("kva", True) else 0):
                wt, wb = load_w(wkv, g)
                for oo in range(CW // 128):
                    fc = g * (CW // 128) + oo
                    bk, bbk = nbank()
                    pe_group([MM(bk[:, 0:n], wt[:, kc, oo * 128:(oo + 1) * 128], hT[:, kc, 0:n], kc == 0, kc == KC - 1) for kc in range(KC)],
                             r=(wb, Bh), w=(bbk,))
                    if sample:
                        CP("act", kTs[:, fc, 0:n], bk[:, 0:n], (bbk,), (B_kTs,))
                    else:
                        s_ = i1 % 2; i1 += 1
                        CP("act", kst[s_][:, 0:n], bk[:, 0:n], (bbk,), (Bkst[s_],))
                        DMA("sp", ag_kT_in[fc // 2][(fc % 2) * 128:(fc % 2 + 1) * 128, t0:t0 + n], kst[s_][:, 0:n], (Bkst[s_],), (B_agk_in,), "kst%d" % s_)
            for g in range(2 * NGK if fl.get("kvb", True) else 0):
                wt, wb = load_w(wkv, g)
                isv = g >= NGK
                col = (g % NGK) * CW
                for tt in range(0, n, 128):
                    rows = min(128, n - tt)
                    bk, bbk = nbank()
                    pe_group([MM(bk[0:rows, 0:CW], hT[:, kc, tt:tt + rows], wt[:, kc, :], kc == 0, kc == KC - 1) for kc in range(KC)],
                             r=(wb, Bh), w=(bbk,))
                    s_ = i2 % 2; i2 += 1
                    CP("dve", st32[s_][0:rows, 0:CW], bk[0:rows, 0:CW], (bbk,), (Bst32[s_],))
                    dst = v_out if isv else k_out
                    if fl.get("kvo", True):
                        DMA("sp", dst[t0 + tt:t0 + tt + rows, col:col + CW], st32[s_][0:rows, 0:CW], (Bst32[s_],), (), "st32%d" % s_)
                    if isv and fl.get("kvv", True):
                        if sample:
                            CP("act", vs_tok[0:rows, col:col + CW], st32[s_][0:rows, 0:CW], (Bst32[s_],), (B_vs,))
                        else:
                            CP("act", vst[s_][0:rows, 0:CW], st32[s_][0:rows, 0:CW], (Bst32[s_],), (Bvst[s_],))
                            for cc_ in range(0, CW, 256):
                                DMA("sp", agvi[(col + cc_) // 256][t0 + tt:t0 + tt + rows, :], vst[s_][0:rows, cc_:cc_ + 256], (Bvst[s_],), (B_agv_in,), "vst%d" % s_)
        if c.__dict__.get("flags", {}).get("kvcc", True):
            for ch in range(NCH):
                P.add("pool", (lambda e, ch=ch: e.collective_compute("AllGather", ALU.bypass, replica_groups=groups_rep,
                                                                      ins=[ag_kT_in[ch].ap().opt()], outs=[ag_kT_out[ch].ap().opt()])),
                      r=(B_agk_in,), w=(B_agk_out,), dma="cc_k%d" % ch, inc=1)
                P.add("pool", (lambda e, ch=ch: e.collective_compute("AllGather", ALU.bypass, replica_groups=groups_rep,
                                                                      ins=[agvi[ch].ap().opt()], outs=[agvo[ch].ap().opt()])),
                      r=(B_agv_in,), w=(B_agv_out,), dma="cc_v%d" % ch, inc=1)
        sb.reset(m)
        P.barrier()

    def sb_layer(l):
        li = l - c.NA
        m = sb.mark()
        wq, wo = wscr["sq%d" % l], wscr["so%d" % l]
        SBH, DSEQ, NPG, SPC, NKB, CPS = c.SBH, c.DSEQ, c.NPG, c.SPC, c.NKB, c.CPS
        HQ = SBH * DSEQ
        QT = sb.alloc("QT", [128, KC, T], BF16)
        BQ = [[[P.buf("Q") for _ in range(2)] for _ in range(KC)] for _ in range(NBT)]
        BQs = [P.buf("Qs%d" % b) for b in range(SPC)]
        m1 = sb.mark()
        wk = norm_work()
        hT = sb.alloc("sb_hT", [128, KC, 512], BF16); Bh = P.buf("sb_hT")
        for bi, (t0, n) in enumerate(blocks):
            rmsnorm(bi, t0, n, gcol_norm(l, 1), hT, Bh, wk)
            for g in range(wq.NG):
                wt, wb = load_w(wq, g)
                for oo in range(wq.CW // 128):
                    fc = g * (wq.CW // 128) + oo
                    bk, bbk = nbank()
                    pe_group([MM(bk[:, 0:n], wt[:, kc, oo * 128:(oo + 1) * 128], hT[:, kc, 0:n], kc == 0, kc == KC - 1) for kc in range(KC)],
                             r=(wb, Bh), w=(bbk,))
                    wl = tuple(BQ[bi][fc]) + (tuple(BQs) if bi == c.NB else ())
                    ACT(QT[:, fc, t0:t0 + n], bk[:, 0:n], AF.Identity, (bbk,), wl, scale=float(c.HD) ** -0.5)
        sb.reset(m1)
        P.barrier()
        NSL = CPS + 1
        KT = sb.alloc("KT", [128, NSL, SEG], BF16); BKT = P.buf("KT")
        Vt = sb.alloc("Vt", [128, NSL * NKB, 128], BF16); BVt = P.buf("Vt")
        mneg = sb.alloc("mneg", [128, c.BLK // 128, c.BLK], BF16); Bmn = P.buf("mneg")
        e32 = [sb.alloc("e32_%d" % i, [128, 512], F32) for i in range(2)]; Be32 = [P.buf("e32") for i in range(2)]
        spt = [sb.alloc("sp_%d" % i, [128, 512], BF16) for i in range(2)]; Bspt = [P.buf("sp") for i in range(2)]
        sps32 = sb.alloc("sps32", [128, 512], F32); Bsps32 = P.buf("sps32")
        spsb = [sb.alloc("spsb_%d" % i, [128, 512], BF16) for i in range(2)]; Bspsb = [P.buf("spsb") for i in range(2)]
        aex = [sb.alloc("aex_%d" % i, [128, 512], BF16) for i in range(2)]; Baex = [P.buf("aex") for i in range(2)]
        bsb = sb.alloc("bsb", [128, SBH], F32); Bbsb = P.buf("bsb")
        biasP = sb.alloc("biasP", [128, NSL, SBH], F32)
        biasS = sb.alloc("biasS", [128, SBH, DSEQ], F32)
        newb = sb.alloc("newb", [128, SPC, HQ], F32); Bnewb = P.buf("newb")
        Ksb = [sb.alloc("Ksb%d" % i, [128, D], BF16) for i in range(2)]; BKsb = [P.buf("Ksb") for i in range(2)]
        Vsb = [sb.alloc("Vsb%d" % i, [128, D], BF16) for i in range(2)]; BVsb = [P.buf("Vsb") for i in range(2)]
        KTp = [sb.alloc("KTp%d" % i, [128, KC, 128], BF16) for i in range(2)]; BKTp = [P.buf("KTp") for i in range(2)]
        zs = sb.alloc("zs", [128, HQ], F32); Bzs = P.buf("zs")
        es = sb.alloc("es", [128, HQ], F32); Bes = P.buf("es")
        sps = sb.alloc("sps", [128, HQ], BF16); Bsps = P.buf("sps")
        ss32 = sb.alloc("ss32", [128, HQ], F32); Bss32 = P.buf("ss32")
        ssb = [sb.alloc("ssb%d" % i, [128, HQ], BF16) for i in range(2)]; Bssb = [P.buf("ssb") for i in range(2)]
        ts_ = sb.alloc("ts", [128, HQ], F32); Bts = P.buf("ts")
        axs = sb.alloc("axs", [128, HQ], BF16); Baxs = P.buf("axs")
        zer = sb.alloc("zer", [128, 2 * KC * DSEQ], BF16); Bzer = P.buf("zer")
        P.add("dve", (lambda e: e.memset(zer[:, :], 0.0)), r=(), w=(Bzer,))
        ptile = sb.alloc("ptile", [128, SPC * NPG], I32); iot = sb.alloc("iot", [128, 1], I32); idx = sb.alloc("idx", [128, SPC * NPG], I32)
        Bidx = P.buf("idx"); Bpt = P.buf("ptile"); Biot = P.buf("iot")

        DMA("sp", mneg[:, :, :], cin["mneg"][:, :, :], (), (Bmn,), "sb_c")
        DMA("sp", bsb[:, :], bsb_in[li:li + 1, :].broadcast_to([128, SBH]), (), (Bbsb,), "sb_c")
        DMA("sp", newb[:, :, :], cin["smask"][:, :, :], (), (Bnewb,), "sb_c")
        DMA("sp", ptile[:, :], pt_in.ap().rearrange("b j -> (b j)").unsqueeze(0).broadcast_to([128, SPC * NPG]), (), (Bpt,), "sb_c")
        P.add("pool", (lambda e: e.iota(iot[:, :], pattern=[[0, 1]], base=0, channel_multiplier=1)), r=(), w=(Biot,))
        TS_("dve", idx[:, :], ptile[:, :], 128, iot[:, 0:1], ALU.mult, ALU.add, (Bpt, Biot), (Bidx,))
        for sl in range(NSL):
            TS_("dve", biasP[:, sl, :], bsb[:, :], rtc("segm", sl), None, ALU.add, None, (Bbsb, B_const), (Bbsb,))
        for q_ in range(DSEQ):
            CP("dve", biasS[:, :, q_], bsb[:, :], (Bbsb,), (Bbsb,))
        for b in range(SPC):
            TT("dve", newb[:, b, :], newb[:, b, :], biasS[:, :, :].rearrange("p h q -> p (h q)"), ALU.add, (Bnewb, Bbsb), (Bnewb,))

        QTz = sb.alloc("QTz", [128, KC, 2, TS], BF16); BQTz = P.buf("QTz")
        P.add("dve", (lambda e: e.memset(QTz[:, :, :, :], 0.0)), r=(), w=(BQTz,))
        CP("dve", QTz[0:64, :, 0, :], QT[0:64, :, SEG:SEG + TS], tuple(BQs), (BQTz,))
        CP("dve", QTz[64:128, :, 1, :], QT[64:128, :, SEG:SEG + TS], tuple(BQs), (BQTz,))
        zbanks = [(pbank[0], Bp[0]), (pbank[1], Bp[1])]
        abanks = [(pbank[2], Bp[2]), (pbank[3], Bp[3])]
        obanks = [(pbank[4], Bp[4]), (pbank[5], Bp[5])]
        cnt = {"t": 0, "o": 0, "g": 0}
        for fc in range(KC if fl.get("sbp", True) else 0):
            fo_ = (fc % 2) * 128
            DMA("sp", KT[:, 0, :], ag_kT_in[fc // 2][fo_:fo_ + 128, :], (B_agk_in,), (BKT,), "KT")
            DMA("sp", Vt[:, 0:NKB, :], agvi[fc // 2][:, fo_:fo_ + 128].rearrange("(kb p) c -> p kb c", p=128), (B_agv_in,), (BVt,), "Vt")
            for si in range(CPS):
                r_ = CPS - 1 - si
                DMA("sp", KT[:, 1 + si, :], ag_kT_out[fc // 2][r_ * 256 + fo_:r_ * 256 + fo_ + 128, :], (B_agk_out,), (BKT,), "KT")
                DMA("sp", Vt[:, (1 + si) * NKB:(2 + si) * NKB, :],
                    agvo[fc // 2][r_ * SEG:(r_ + 1) * SEG, fo_:fo_ + 128].rearrange("(kb p) c -> p kb c", p=128), (B_agv_out,), (BVt,), "Vt")
            jobs = []
            for e_ in range(2):
                h = 2 * fc + e_; pb = 64 * e_
                for bi in range(c.NB):
                    t0, n = blocks[bi]
                    tiles = [(0, kb, (kb - t0 // 128) if kb * 128 >= t0 else None) for kb in reversed(range((t0 + n) // 128))]
                    tiles += [(1 + si, kb, None) for si in range(CPS) for kb in reversed(range(NKB))]
                    obo = obanks[cnt["o"] % 2]; cnt["o"] += 1
                    for ti_, (slot, kb, diag) in enumerate(tiles):
                        jobs.append((h, pb, e_, bi, t0, n, slot, kb, diag, ti_ == 0, ti_ == len(tiles) - 1, cnt["t"] % 2, obo))
                        cnt["t"] += 1

            def job_ctx(J):
                h, pb, e_, bi, t0, n, slot, kb, diag, first, last, j, obo = J
                Qh = QT[pb:pb + 64, fc, t0:t0 + n]
                Kh = KT[pb:pb + 64, slot, kb * 128:(kb + 1) * 128]
                bias = biasP[:, slot, h:h + 1]
                rd = (BKT, BQ[bi][fc][e_]) + ((Bmn, B_const) if diag is not None else ())
                return Qh, Kh, bias, rd

            def stage_a(J):
                h, pb, e_, bi, t0, n, slot, kb, diag, first, last, j, obo = J
                Qh, Kh, bias, rd = job_ctx(J)
                zb, bzb = zbanks[j]
                fz = [MM(zb[:, 0:n], Kh, Qh, True, diag is None)]
                if diag is not None: fz.append(MM(zb[:, 0:n], ident_b, mneg[:, diag, 0:n], False, True))
                pe_group(fz, r=rd, w=(bzb,))
                ACT(e32[j][:, 0:n], zb[:, 0:n], AF.Exp, (bzb, Bbsb), (Be32[j],), bias=bias)
                ACT(spt[j][:, 0:n], e32[j][:, 0:n], AF.Ln, (Be32[j], B_const), (Bspt[j],), bias=rtc("one", 0))

            def stage_b(J):
                h, pb, e_, bi, t0, n, slot, kb, diag, first, last, j, obo = J
                Qh, Kh, bias, rd = job_ctx(J)
                ab, bab = abanks[j]
                fa = [MM(ab[:, 0:n], Kh, Qh, True, False)]
                if diag is not None: fa.append(MM(ab[:, 0:n], ident_b, mneg[:, diag, 0:n], False, False))
                fa.append(MM(ab[:, 0:n], ntri_b, spt[j][:, 0:n], False, first))
                rr = rd + (Bspt[j], B_const)
                if not first:
                    fa.append(MM(ab[:, 0:n], nones_b, spsb[(j + 1) % 2][:, 0:n], False, True))
                    rr = rr + (Bspsb[(j + 1) % 2],)
                pe_group(fa, r=rr, w=(bab,))
                if not last:
                    if first:
                        CP("dve", sps32[:, 0:n], spt[j][:, 0:n], (Bspt[j],), (Bsps32,))
                    else:
                        TT("dve", sps32[:, 0:n], sps32[:, 0:n], spt[j][:, 0:n], ALU.add, (Bspt[j], Bsps32), (Bsps32,))
                    CP("dve", spsb[j][:, 0:n], sps32[:, 0:n], (Bsps32,), (Bspsb[j],))
                ACT(aex[j][:, 0:n], ab[:, 0:n], AF.Exp, (bab, Bbsb), (Baex[j],), bias=bias)

            def stage_c(J):
                h, pb, e_, bi, t0, n, slot, kb, diag, first, last, j, obo = J
                ob, bob = obo
                pe_group([MM(ob[:, 0:n], Vt[:, slot * NKB + kb, :], aex[j][:, 0:n], first, last)], r=(BVt, Baex[j]), w=(bob,))
                if last:
                    CP("dve", QT[pb:pb + 64, fc, t0:t0 + n], ob[pb:pb + 64, 0:n], (bob,), (BQ[bi][fc][e_],))

            NJ = len(jobs)
            if not fl.get("swp", True):
                for J in jobs:
                    stage_a(J); stage_b(J); stage_c(J)
            else:
                for i_ in range(NJ + 2):
                    if i_ < NJ: stage_a(jobs[i_])
                    if 0 <= i_ - 1 < NJ: stage_b(jobs[i_ - 1])
                    if 0 <= i_ - 2 < NJ: stage_c(jobs[i_ - 2])
        ts0 = SEG
        for b in range(SPC if fl.get("sbs", True) else 0):
            qc0 = ts0 + b * DSEQ
            ob, bob = obanks[cnt["o"] % 2]; cnt["o"] += 1
            P.add("dve", (lambda e: e.memset(ss32[:, :], 0.0)), r=(), w=(Bss32,))
            ntile = NPG + 1 if fl.get("sbs1", True) else 1
            for ti_ in range(ntile):
                first = ti_ == 0; last = ti_ == ntile - 1
                j = cnt["t"] % 2; cnt["t"] += 1
                zb, bzb = zbanks[j]; ab, bab = abanks[j]
                if first:
                    rows = TS
                    def Kof(fc_): return kTs[:, fc_, 0:TS]
                    Vsrc = vs_tok; rk = (B_kTs,); rv = (B_vs,)
                    btile = newb[0:rows, b, :]
                else:
                    pg = NPG - ti_
                    g_ = cnt["g"] % 2; cnt["g"] += 1
                    rows = 128
                    col = b * NPG + pg
                    P.add("pool", (lambda e, g_=g_, col=col: e.indirect_dma_start(
                        out=Ksb[g_][:, :], out_offset=None, in_=ck_in[:, :],
                        in_offset=bass.IndirectOffsetOnAxis(ap=idx[:, col:col + 1], axis=0))), r=(Bidx,), w=(BKsb[g_],), dma="Ksb%d" % g_)
                    P.add("pool", (lambda e, g_=g_, col=col: e.indirect_dma_start(
                        out=Vsb[g_][:, :], out_offset=None, in_=cv_in[:, :],
                        in_offset=bass.IndirectOffsetOnAxis(ap=idx[:, col:col + 1], axis=0))), r=(Bidx,), w=(BVsb[g_],), dma="Vsb%d" % g_)
                    pe_group([TR(pbf[:, k * 128:(k + 1) * 128], Ksb[g_][:, k * 128:(k + 1) * 128], ident_b) for k in range(KC)],
                             r=(BKsb[g_], B_const), w=(Bpbf,))
                    CP("dve", KTp[g_][:, :, :], pbf[:, 0:KC * 128].rearrange("p (k t) -> p k t", t=128), (Bpbf,), (BKTp[g_],))
                    def Kof(fc_, g_=g_): return KTp[g_][:, fc_, :]
                    Vsrc = Vsb[g_]; rk = (BKTp[g_],); rv = (BVsb[g_],)
                    btile = biasS[:, :, :].rearrange("p h q -> p (h q)")
                pe_group([MM(zb[0:rows, fc_ * 2 * DSEQ:(fc_ + 1) * 2 * DSEQ], Kof(fc_), QTz[:, fc_, :, b * DSEQ:(b + 1) * DSEQ], True, True)
                          for fc_ in range(KC)], r=rk + (BQTz,), w=(bzb,))
                TT("dve", zs[0:rows, :], zb[0:rows, 0:HQ], btile, ALU.add, (bzb, Bbsb, Bnewb), (Bzs,))
                ACT(es[0:rows, :], zs[0:rows, :], AF.Exp, (Bzs,), (Bes,))
                ACT(sps[0:rows, :], es[0:rows, :], AF.Ln, (Bes, B_const), (Bsps,), bias=rtc("one", 0, 1, rows))
                if not fl.get("sba", True): continue
                fa = [MM(ab[0:rows, 0:HQ], ntri_b[0:rows, 0:rows], sps[0:rows, :], True, False)]
                rr = rk + (BQTz, Bsps, B_const)
                if not first:
                    fa.append(MM(ab[0:rows, 0:HQ], nones_b[:, 0:rows], ssb[(j + 1) % 2][:, :], False, False))
                    rr = rr + (Bssb[(j + 1) % 2],)
                fa += [MM(ab[0:rows, fc_ * 2 * DSEQ:(fc_ + 1) * 2 * DSEQ], Kof(fc_), QTz[:, fc_, :, b * DSEQ:(b + 1) * DSEQ], False, fc_ == KC - 1)
                       for fc_ in range(KC)]
                pe_group(fa, r=rr, w=(bab,))
                if not last:
                    TT("dve", ss32[0:rows, :], ss32[0:rows, :], sps[0:rows, :], ALU.add, (Bsps, Bss32), (Bss32,))
                    CP("dve", ssb[j][:, :], ss32[:, :], (Bss32,), (Bssb[j],))
                TT("dve", ts_[0:rows, :], ab[0:rows, 0:HQ], btile, ALU.add, (bab, Bbsb, Bnewb), (Bts,))
                ACT(axs[0:rows, :], ts_[0:rows, :], AF.Exp, (Bts,), (Baxs,))
                if not fl.get("sbo", True): continue
                fo = [MM(ob[:, 0:2 * KC * DSEQ], ident_b, zer[:, 0:2 * KC * DSEQ], True, False)] if first else []
                fo += [MM(ob[:, (hh % 2) * KC * DSEQ + (hh // 2) * DSEQ:(hh % 2) * KC * DSEQ + (hh // 2 + 1) * DSEQ],
                          Vsrc[0:rows, (hh // 2) * 128:(hh // 2 + 1) * 128], axs[0:rows, hh * DSEQ:(hh + 1) * DSEQ], False, last and hh == SBH - 1) for hh in range(SBH)]
                pe_group(fo, r=rv + (Baxs, Bzer, B_const), w=(bob,))
            for e_ in range(2 if (fl.get("sbo", True) and fl.get("sba", True)) else 0):
                CP("dve", QT[64 * e_:64 * e_ + 64, :, qc0:qc0 + DSEQ],
                   ob[64 * e_:64 * e_ + 64, e_ * KC * DSEQ:(e_ + 1) * KC * DSEQ].rearrange("p (k q) -> p k q", q=DSEQ), (bob,), (BQs[b],))
        for bi, (t0, n) in enumerate(blocks):
            for g in range(wo.NG):
                wt, wb = load_w(wo, g)
                for oo in range(wo.CW // 128):
                    oc = g * (wo.CW // 128) + oo
                    bk, bbk = nbank()
                    rq = tuple(BQ[bi][kc][e_] for kc in range(KC) for e_ in range(2)) + (tuple(BQs) if bi == c.NB else ())
                    pe_group([MM(bk[:, 0:n], wt[:, kc, oo * 128:(oo + 1) * 128], QT[:, kc, t0:t0 + n], kc == 0, kc == KC - 1) for kc in range(KC)],
                             r=(wb,) + rq, w=(bbk,))
                    TT("dve", xT[:, oc, t0:t0 + n], xT[:, oc, t0:t0 + n], bk[:, 0:n], ALU.add, (bbk, B_x[bi][oc]), (B_x[bi][oc],))
        sb.reset(m)
        P.barrier()

    def final():
        m = sb.mark()
        wk = norm_work()
        yT = sb.alloc("yT", [128, KC, 512], F32); ByT = P.buf("yT")
        yst = [sb.alloc("yst%d" % i, [128, D], F32) for i in range(2)]
        Byst = [P.buf("yst%d" % i) for i in range(2)]
        ti = 0
        for bi, (t0, n) in enumerate(blocks):
            rmsnorm(bi, t0, n, GCOL_FIN, yT, ByT, wk)
            for tt in range(0, n, 128):
                rows = min(128, n - tt)
                s = ti % 2; ti += 1
                for f0 in range(0, KC, 4):
                    nf = min(4, KC - f0)
                    bank, bb = nbank()
                    pe_group([TR(bank[0:rows, k * 128:(k + 1) * 128], yT[:, f0 + k, tt:tt + rows], ident_f[:, :]) for k in range(nf)],
                             r=(ByT, B_const), w=(bb,))
                    CP("dve" if (f0 // 4) % 2 == 0 else "act", yst[s][0:rows, f0 * 128:(f0 + nf) * 128], bank[0:rows, 0:nf * 128], (bb,), (Byst[s],))
                DMA("sp", y_out[t0 + tt:t0 + tt + rows, :], yst[s][0:rows, :], (Byst[s],), (), "yst%d" % s)
        sb.reset(m)

    for l in range(c.DEPTH):
        if l + 1 < c.DEPTH:
            cast_layer_weights(l + 1)
        ffn(l, 0)
        if l < c.NA:
            if fl.get("ret", True): retention(l)
        else:
            if fl.get("sb", True): sb_layer(l)
        ffn(l, 1)
        ple(l)
        if l == c.NA - 1 and fl.get("kv", True):
            kv_phase()
    final()
    P.emit()
    return nc


def prep_inputs(cfg, inputs):
    c = cfg
    g_norm = np.asarray(inputs["g_norm"], np.float32)
    vecs = [g_norm[l, i] for l in range(c.DEPTH) for i in range(3)] + [np.asarray(inputs["g_ple"], np.float32)[l] for l in range(c.DEPTH)]
    vecs += [np.asarray(inputs["g_kv"], np.float32), np.asarray(inputs["g_final"], np.float32)]
    gvec = np.concatenate([v.reshape(c.KC, 128).T for v in vecs], axis=1).astype(np.float32)
    gvec = np.ascontiguousarray(gvec)
    shared = {"gvec": gvec}
    for k in ("w_ff1_gate", "w_ff1_up", "w_ff1_down", "w_ff2_gate", "w_ff2_up", "w_ff2_down", "w_ple_up", "w_ple_gate",
              "w_ret_in", "w_ret_out", "g_ret_gn", "w_kv", "w_sb_q", "w_sb_out", "b_sb"):
        shared[k] = np.ascontiguousarray(np.asarray(inputs[k], np.float32))
    shared["cache_k"] = np.asarray(inputs["cache_k"], np.float32).reshape(c.NPOOL * c.PAGE, c.D)
    shared["cache_v"] = np.asarray(inputs["cache_v"], np.float32).reshape(c.NPOOL * c.PAGE, c.D)
    xp = np.asarray(inputs["x_prompt"], np.float32); xs = np.asarray(inputs["x_sample"], np.float32)
    pp = np.asarray(inputs["p_prompt"], np.float32); ps = np.asarray(inputs["p_sample"], np.float32)
    st = np.asarray(inputs["state_ret"], np.float32)
    pt = np.asarray(inputs["page_table"], np.int32)
    maps = []
    for core in range(c.NC):
        seq, seg = core // c.CPS, core % c.CPS
        m = dict(shared)
        sl = slice(seg * c.SEG, (seg + 1) * c.SEG); bs = slice(core * c.SPC, (core + 1) * c.SPC)
        m["x"] = np.concatenate([xp[seq, sl], xs[bs].reshape(c.TS, c.D)], axis=0)
        m["p"] = np.concatenate([pp[:, seq, sl], ps[:, bs].reshape(c.DEPTH, c.TS, c.PLE)], axis=1)
        m["state_ret"] = np.ascontiguousarray(st[:, bs])
        m["page_table"] = np.ascontiguousarray(pt[bs])
        cons = host_constants(c, core)
        for nm in CONST_NAMES:
            m["c_" + nm] = cons[nm]
        maps.append(m)
    return maps


def assemble(cfg, results):
    c = cfg
    y_p = np.zeros((c.BATCH, c.CPS * c.SEG, c.D), np.float32)
    y_s = np.zeros((c.NC * c.SPC, c.DSEQ, c.D), np.float32)
    S_p = np.zeros((c.NA, c.BATCH, c.RH, c.DK, c.DV), np.float32)
    S_s = np.zeros((c.NA, c.NC * c.SPC, c.RH, c.DK, c.DV), np.float32)
    k_p = np.zeros((c.BATCH, c.CPS * c.SEG, c.SBH, c.HD), np.float32); v_p = np.zeros_like(k_p)
    k_s = np.zeros((c.NC * c.SPC, c.DSEQ, c.SBH, c.HD), np.float32); v_s = np.zeros_like(k_s)
    for core in range(c.NC):
        r = results[core]
        seq, seg = core // c.CPS, core % c.CPS
        sl = slice(seg * c.SEG, (seg + 1) * c.SEG); bs = slice(core * c.SPC, (core + 1) * c.SPC)
        y = np.asarray(r["y"]); kn = np.asarray(r["k_new"]); vn = np.asarray(r["v_new"])
        y_p[seq, sl] = y[:c.SEG]; y_s[bs] = y[c.SEG:].reshape(c.SPC, c.DSEQ, c.D)
        k_p[seq, sl] = kn[:c.SEG].reshape(c.SEG, c.SBH, c.HD); v_p[seq, sl] = vn[:c.SEG].reshape(c.SEG, c.SBH, c.HD)
        k_s[bs] = kn[c.SEG:].reshape(c.SPC, c.DSEQ, c.SBH, c.HD); v_s[bs] = vn[c.SEG:].reshape(c.SPC, c.DSEQ, c.SBH, c.HD)
        S_s[:, bs] = np.asarray(r["s_sample"])
        if seg == c.CPS - 1:
            S_p[:, seq] = np.asarray(r["s_prompt"])
    return (y_p, y_s, S_p, S_s, k_p, v_p, k_s, v_s)


_NC_CACHE = {}


def kernel(**inputs):
    cfg = Cfg()
    if "nc" not in _NC_CACHE:
        _NC_CACHE["nc"] = build(cfg)
    nc = _NC_CACHE["nc"]
    maps = prep_inputs(cfg, inputs)
    res = run_bass_kernel_spmd(nc, maps, core_ids=list(range(cfg.NC)))
    return assemble(cfg, res.results)
```

```python
import math
import numpy as np
import ml_dtypes
import concourse.bass as bass
import concourse.mybir as mybir
from concourse.bass_utils import run_bass_kernel_spmd

F32 = mybir.dt.float32
BF16 = mybir.dt.bfloat16
I32 = mybir.dt.int32
AF = mybir.ActivationFunctionType
ALU = mybir.AluOpType
NEG = -30000.0
ENGS = ("pe", "act", "dve", "pool", "sp")
DTSZ = {F32: 4, BF16: 2, I32: 4}


class Cfg:
    def __init__(s, **kw):
        s.NC = 8; s.CPS = 4; s.D = 1024; s.DFF = 2816; s.PLE = 256; s.SEG = 2048
        s.SPC = 16; s.DSEQ = 4; s.NPG = 16; s.PAGE = 128; s.NPOOL = 2560
        s.DEPTH = 4; s.NA = 2; s.HD = 64; s.DK = 256; s.EPS = 1e-6; s.ROPE_BASE = 10000.0
        s.SB_BIAS = -8.0
        s.__dict__.update(kw)
        s.KC = s.D // 128; s.JC = s.DFF // 128; s.PC = s.PLE // 128
        s.RH = s.D // s.DK; s.DV = 2 * s.DK; s.RQK = s.RH * s.DK; s.RV = s.RH * s.DV; s.VC = s.RV // 128
        s.SBH = s.D // s.HD; s.TS = s.SPC * s.DSEQ; s.T = s.SEG + s.TS
        s.BLK = min(512, s.SEG); s.NB = s.SEG // s.BLK
        s.RBLK = min(256, s.SEG); s.NRB = s.SEG // s.RBLK
        s.BATCH = s.NC // s.CPS; s.PAST = s.NPG * s.PAGE
        s.NKB = s.SEG // 128
        assert s.DK == 256 and s.DV == 512


class Buf:
    __slots__ = ("name", "w", "r")

    def __init__(s, name):
        s.name = name; s.w = None; s.r = []


class Prog:
    def __init__(s, nc):
        s.nc = nc
        s.ops = {e: [] for e in ENGS}
        s.cnt = {e: 0 for e in ENGS}
        s.known = {e: {} for e in ENGS}
        s.pending = {e: {} for e in ENGS}
        s.dcnt = {}
        s.kinc = {}
        s.sems = {}
        s.nbuf = 0

    def buf(s, name):
        return Buf(name)

    def sem(s, key):
        if key not in s.sems:
            s.sems[key] = s.nc.alloc_semaphore("s%d" % len(s.sems))
        return s.sems[key]

    def _val(s, prod):
        if prod[0] == "c":
            return ("E", prod[1]), prod[2] + 1
        return ("D", prod[1]), s.dcnt[prod[1]] * prod[2]

    def add(s, eng, fn, r=(), w=(), dma=None, inc=16):
        need = dict(s.pending[eng]); s.pending[eng] = {}
        prods = []
        for b in r:
            if b.w is not None: prods.append(b.w)
        for b in w:
            if b.w is not None: prods.append(b.w)
            prods.extend(b.r)
        for p in prods:
            if p[0] == "c" and p[1] == "pe" and eng == "pe" and dma is None:
                continue
            k, v = s._val(p)
            if need.get(k, 0) < v: need[k] = v
        waits = []
        kn = s.known[eng]
        for k, v in need.items():
            if kn.get(k, 0) < v:
                kn[k] = v; waits.append((k, v))
        if dma is None:
            me = ("c", eng, s.cnt[eng]); s.cnt[eng] += 1
            sig = (("E", eng), 1)
        else:
            s.dcnt[dma] = s.dcnt.get(dma, 0) + 1
            s.kinc[dma] = inc
            me = ("d", dma, inc)
            sig = (("D", dma), inc)
        s.ops[eng].append((waits, fn, sig))
        for b in r: b.r.append(me)
        for b in w:
            b.w = me; b.r = []
        return me

    def barrier(s):
        snap = {}
        for e in ENGS:
            if s.cnt[e]: snap[("E", e)] = s.cnt[e]
        for k, c in s.dcnt.items():
            snap[("D", k)] = c * s.kinc.get(k, 16)
        for e in ENGS:
            for k, v in snap.items():
                if k == ("E", e) and e == "pe": continue
                if s.pending[e].get(k, 0) < v: s.pending[e][k] = v

    def emit(s):
        nc = s.nc
        s.barrier()
        final = s.pending
        sems = {k: s.sem(k) for k in set(k for e in ENGS for (w, _, sg) in s.ops[e] for k in [sg[0]] + [x[0] for x in w])}
        for e in ENGS:
            for k in final[e]: sems.setdefault(k, s.sem(k))

        def replay(eng, name):
            for (waits, fn, sig) in s.ops[name]:
                for (k, v) in waits: eng.wait_ge(sems[k], v)
                ins = fn(eng)
                ins.then_inc(sems[sig[0]], sig[1])
            kn = s.known[name]
            for k, v in final[name].items():
                if kn.get(k, 0) < v: eng.wait_ge(sems[k], v)

        with nc.Block() as block:
            @block.tensor
            def _(e): replay(e, "pe")

            @block.scalar
            def _(e): replay(e, "act")

            @block.vector
            def _(e): replay(e, "dve")

            @block.gpsimd
            def _(e): replay(e, "pool")

            @block.sync
            def _(e): replay(e, "sp")


class SB:
    BASE = 16512 + 96
    LIMIT = 16512 + 212000

    def __init__(s, nc):
        s.nc = nc; s.off = SB.BASE; s.n = 0

    def alloc(s, name, shape, dt):
        sz = DTSZ[dt]
        for d in shape[1:]: sz *= d
        sz = (sz + 63) // 64 * 64
        assert s.off + sz <= SB.LIMIT, ("SBUF overflow", name, s.off, sz)
        s.n += 1
        t = s.nc.alloc_sbuf_tensor_at("%s_%d" % (name, s.n), list(shape), dt, offset=s.off)
        s.off += sz
        return t

    def mark(s): return s.off

    def reset(s, m): s.off = m


def host_constants(cfg, core):
    c = cfg
    seg = core % c.CPS
    cons = {}
    cons["ident_f"] = np.eye(128, dtype=np.float32)
    ib = np.zeros((128, 4, 128), dtype=np.float32)
    ib[:, 0] = np.eye(128)
    ib[:, 1] = 1.0
    kk = np.arange(128)
    ib[:, 2] = -(kk[:, None] >= kk[None, :]).astype(np.float32)
    ib[:, 3] = -1.0
    cons["cb"] = ib.astype(ml_dtypes.bfloat16)
    half = 128
    inv = (c.ROPE_BASE ** (-np.arange(half, dtype=np.float32) / half)).astype(np.float32)
    pos = np.concatenate([seg * c.SEG + np.arange(c.SEG), c.PAST + (np.arange(c.TS) % c.DSEQ)]).astype(np.float32)
    ang = (pos[None, :] * inv[:, None]).astype(np.float32)
    cons["rope"] = np.stack([np.cos(ang), np.sin(ang)]).astype(np.float32)
    lg = np.log(1.0 - 2.0 ** (-5.0 - np.arange(c.RH, dtype=np.float64)))
    j = np.arange(128)
    sc = c.DK ** -0.5
    decT = np.zeros((128, c.RH, 128), np.float32)
    for h in range(c.RH):
        decT[:, h, :] = np.where(j[:, None] <= j[None, :], np.exp(-(j[:, None] + 1.0) * lg[h]) * sc, 0.0)
    cons["decT"] = decT
    js = np.arange(c.TS); jl = js % c.DSEQ; bs = js // c.DSEQ
    decTs = np.zeros((128, c.RH, c.TS), np.float32)
    for h in range(c.RH):
        m = (bs[:, None] == bs[None, :]) & (jl[:, None] <= jl[None, :])
        decTs[:c.TS, h, :] = np.where(m, np.exp(-(jl[:, None] + 1.0) * lg[h]) * sc, 0.0)
    cons["decTs"] = decTs
    RH = c.RH
    rt = np.zeros((128, 8 * RH + c.CPS * RH + c.SPC + 8), np.float32)
    o = 0
    cons_off = {}
    def put(name, arr):
        nonlocal o
        n = arr.shape[1]; rt[:, o:o + n] = arr; cons_off[name] = o; o += n
    put("ci", np.exp((j[:, None] + 1.0) * lg[None, :]))
    put("ci2", np.exp(2.0 * (j[:, None] + 1.0) * lg[None, :]))
    put("kdec", np.exp((127.0 - j[:, None]) * lg[None, :]) * sc)
    put("g128", np.tile(np.exp(128.0 * lg)[None, :], (128, 1)))
    jl128 = np.arange(128) % c.DSEQ
    put("cis", np.exp((jl128[:, None] + 1.0) * lg[None, :]))
    put("cis2", np.exp(2.0 * (jl128[:, None] + 1.0) * lg[None, :]))
    put("kdecs", np.exp((c.DSEQ - 1.0 - jl128[:, None]) * lg[None, :]) * sc)
    put("g4", np.tile(np.exp(float(c.DSEQ) * lg)[None, :], (128, 1)))
    coef = np.zeros((c.CPS, RH))
    for r in range(c.CPS):
        if r < seg:
            coef[r] = np.exp(float(c.SEG) * (seg - 1 - r) * lg)
    put("coef", np.tile(coef.reshape(1, -1), (128, 1)))
    rowm = np.zeros((128, c.SPC))
    for b in range(c.SPC):
        rowm[b * c.DSEQ:(b + 1) * c.DSEQ, b] = 1.0
    put("rowm", rowm)
    put("eps", np.full((128, 1), c.EPS))
    put("one", np.full((128, 1), 1.0))
    segm = np.zeros((128, c.CPS + 1))
    for si in range(c.CPS):
        r = c.CPS - 1 - si
        segm[:, 1 + si] = 0.0 if r < seg else NEG
    put("segm", segm)
    cons["rt"] = rt[:, :o].copy()
    cons["_off"] = cons_off
    nr = c.BLK // 128
    qi = np.arange(c.BLK)
    mn = np.zeros((128, nr, c.BLK), np.float32)
    for r in range(nr):
        mn[:, r, :] = np.where((r * 128 + j[:, None]) >= qi[None, :], NEG, 0.0)
    cons["mneg"] = mn.astype(ml_dtypes.bfloat16)
    sm = np.full((128, c.SPC, c.SBH, c.DSEQ), NEG, np.float32)
    for b in range(c.SPC):
        for jj in range(c.DSEQ):
            for ii in range(c.DSEQ):
                if jj < ii:
                    sm[b * c.DSEQ + jj, b, :, ii] = 0.0
    cons["smask"] = sm.reshape(128, c.SPC, c.SBH * c.DSEQ)
    colm = np.zeros((128, c.SPC, c.TS), np.float32)
    for b in range(c.SPC):
        colm[:, b, b * c.DSEQ:(b + 1) * c.DSEQ] = 1.0
    cons["colm"] = colm.astype(ml_dtypes.bfloat16)
    return cons


CONST_NAMES = ("ident_f", "cb", "rope", "decT", "decTs", "rt", "mneg", "smask", "colm")


def build(cfg, stop_after=None):
    c = cfg
    nc = bass.Bass("TRN2", target_bir_lowering=False)
    P = Prog(nc)
    sb = SB(nc)
    fl = c.__dict__.get("flags", {})
    KC, JC, PC, RH, VC, T, SEG, TS, D = c.KC, c.JC, c.PC, c.RH, c.VC, c.T, c.SEG, c.TS, c.D
    cons0 = host_constants(c, 0)
    OFF = cons0["_off"]

    def din(name, shape, dt=F32):
        return nc.dram_tensor(name, list(shape), dt, kind="ExternalInput")

    def dout(name, shape, dt=F32):
        return nc.dram_tensor(name, list(shape), dt, kind="ExternalOutput")

    def dscr(name, shape, dt=BF16):
        return nc.dram_tensor(name, list(shape), dt)

    x_in = din("x", [T, D])
    p_in = din("p", [c.DEPTH, T, c.PLE])
    st_in = din("state_ret", [c.NA, c.SPC, RH, c.DK, c.DV])
    ck_in = din("cache_k", [c.NPOOL * c.PAGE, D])
    cv_in = din("cache_v", [c.NPOOL * c.PAGE, D])
    pt_in = din("page_table", [c.SPC, c.NPG], I32)
    gv_in = din("gvec", [128, (c.DEPTH * 4 + 2) * KC])
    ggn_in = din("g_ret_gn", [c.NA, c.RV])
    bsb_in = din("b_sb", [c.DEPTH - c.NA, c.SBH])
    W = {}
    W["ff_gate"] = [din("w_ff1_gate", [c.DEPTH, D, c.DFF]), din("w_ff2_gate", [c.DEPTH, D, c.DFF])]
    W["ff_up"] = [din("w_ff1_up", [c.DEPTH, D, c.DFF]), din("w_ff2_up", [c.DEPTH, D, c.DFF])]
    W["ff_down"] = [din("w_ff1_down", [c.DEPTH, c.DFF, D]), din("w_ff2_down", [c.DEPTH, c.DFF, D])]
    W["ple_up"] = din("w_ple_up", [c.DEPTH, c.PLE, D])
    W["ple_gate"] = din("w_ple_gate", [c.DEPTH, D, D])
    W["ret_in"] = din("w_ret_in", [c.NA, D, 2 * c.RQK + 2 * c.RV])
    W["ret_out"] = din("w_ret_out", [c.NA, c.RV, D])
    W["kv"] = din("w_kv", [D, 2 * D])
    W["sb_q"] = din("w_sb_q", [c.DEPTH - c.NA, D, D])
    W["sb_out"] = din("w_sb_out", [c.DEPTH - c.NA, D, D])
    cin = {}
    for nm in CONST_NAMES:
        a = cons0[nm]
        cin[nm] = din("c_" + nm, a.shape, BF16 if a.dtype == ml_dtypes.bfloat16 else F32)

    y_out = dout("y", [T, D])
    sp_out = dout("s_prompt", [c.NA, RH, c.DK, c.DV])
    ss_out = dout("s_sample", [c.NA, c.SPC, RH, c.DK, c.DV])
    k_out = dout("k_new", [T, D])
    v_out = dout("v_new", [T, D])

    xT = sb.alloc("xT", [128, KC, T], F32)
    ident_f = sb.alloc("ident_f", [128, 128], F32)
    cb = sb.alloc("cb", [128, 4, 128], BF16)
    ident_b, ones_b, ntri_b, nones_b = cb[:, 0, :], cb[:, 1, :], cb[:, 2, :], cb[:, 3, :]
    rt = sb.alloc("rt", list(cons0["rt"].shape), F32)
    gv = sb.alloc("gv", [128, (c.DEPTH * 4 + 2) * KC], F32)
    NW = 2
    JW = 256 if JC % 2 == 0 else 128
    OW = 128
    WSLOT = max(KC * 512, 2 * KC * JW, JC * OW, VC * OW)
    wring = [sb.alloc("wr%d" % i, [128, WSLOT], BF16) for i in range(NW)]
    wbufs = [P.buf("wr%d" % i) for i in range(NW)]
    wstate = {"i": 0}
    kTs = sb.alloc("kTs", [128, KC, TS], BF16)
    vs_tok = sb.alloc("vs_tok", [128, D], BF16)
    B_kTs = P.buf("kTs"); B_vs = P.buf("vs")
    persist_mark = sb.mark()

    pbank = [nc.alloc_psum_tensor("pb%d" % i, [128, 512], F32) for i in range(7)]
    pbf = nc.alloc_psum_tensor("pbf", [128, 1024], BF16)
    Bp = [P.buf("pb%d" % i) for i in range(7)]
    Bpbf = P.buf("pbf")

    B_const = P.buf("const")
    B_x = [[P.buf("x%d_%d" % (b, fc)) for fc in range(KC)] for b in range(c.NB + 1)]

    def rtc(name, col, n=1, rows=128):
        o = OFF[name] + col
        return rt[0:rows, o:o + n]

    blocks = [(b * c.BLK, c.BLK) for b in range(c.NB)] + [(SEG, TS)]
    NBT = len(blocks)

    def xbufs(bi):
        return B_x[bi]

    class WS:
        pass

    wscr = {}

    wkey = {"k": "ws0"}

    def mk_w(name, src2d, K, N, CW):
        ws = WS()
        ws.K = K; ws.N = N; ws.CW = CW; ws.Kc = K // 128; ws.NG = N // CW
        ws.t = dscr("ws_" + name, [ws.NG, 128, ws.Kc, CW])
        ws.buf = P.buf("ws_" + name)
        srcv = src2d.rearrange("(kc p) n -> p kc n", p=128)
        for g in range(ws.NG):
            P.add("pool", (lambda e, g=g, ws=ws, srcv=srcv: e.dma_start(out=ws.t[g], in_=srcv[:, :, g * CW:(g + 1) * CW])),
                  r=(), w=(ws.buf,), dma=wkey["k"])
        wscr[name] = ws
        return ws

    def cast_layer_weights(l):
        wkey["k"] = "ws%d" % l
        for wh in range(2):
            mk_w("g%d_%d" % (l, wh), W["ff_gate"][wh][l], D, c.DFF, JW)
            mk_w("u%d_%d" % (l, wh), W["ff_up"][wh][l], D, c.DFF, JW)
            mk_w("d%d_%d" % (l, wh), W["ff_down"][wh][l], c.DFF, D, OW)
            if wh == 0:
                if l < c.NA:
                    mk_w("ri%d" % l, W["ret_in"][l], D, 2 * c.RQK + 2 * c.RV, 512)
                    mk_w("ro%d" % l, W["ret_out"][l], c.RV, D, OW)
                else:
                    mk_w("sq%d" % l, W["sb_q"][l - c.NA], D, D, min(512, D))
                    mk_w("so%d" % l, W["sb_out"][l - c.NA], D, D, OW)
        mk_w("pu%d" % l, W["ple_up"][l], c.PLE, D, min(512, D))
        mk_w("pg%d" % l, W["ple_gate"][l], D, D, min(512, D))
        if l == c.NA - 1:
            mk_w("kv", W["kv"], D, 2 * D, min(512, D))

    p_scr = dscr("p_scr", [c.DEPTH, T, c.PLE])
    B_pscr = P.buf("p_scr")

    def load_w(ws, g):
        i = wstate["i"] % NW; wstate["i"] += 1
        view = wring[i][:, 0:ws.Kc * ws.CW].rearrange("p (k c) -> p k c", c=ws.CW)
        P.add("sp", (lambda e, view=view, ws=ws, g=g: e.dma_start(out=view, in_=ws.t[g])),
              r=(ws.buf,), w=(wbufs[i],), dma="wr%d" % i)
        return view, wbufs[i]

    def load_w2(wa, wb_, g):
        i = wstate["i"] % NW; wstate["i"] += 1
        sz = wa.Kc * wa.CW
        va = wring[i][:, 0:sz].rearrange("p (k c) -> p k c", c=wa.CW)
        vb = wring[i][:, sz:2 * sz].rearrange("p (k c) -> p k c", c=wa.CW)
        P.add("sp", (lambda e: e.dma_start(out=va, in_=wa.t[g])), r=(wa.buf,), w=(wbufs[i],), dma="wr%d" % i)
        P.add("sp", (lambda e: e.dma_start(out=vb, in_=wb_.t[g])), r=(wb_.buf,), w=(wbufs[i],), dma="wr%d" % i)
        return va, vb, wbufs[i]

    def pe_group(fl, r, w):
        def fn(e, fl=fl):
            ins = None
            for f in fl: ins = f(e)
            return ins
        P.add("pe", fn, r=r, w=w)

    def MM(out, lhsT, rhs, start, stop):
        return lambda e: e.matmul(out=out, lhsT=lhsT, rhs=rhs, start=start, stop=stop)

    def TR(out, in_, ident):
        return lambda e: e.transpose(out=out, in_=in_, identity=ident)

    def ACT(out, in_, func, r, w, bias=None, scale=None):
        kw = {}
        if bias is not None: kw["bias"] = bias
        if scale is not None: kw["scale"] = scale
        P.add("act", (lambda e: e.activation(out=out, in_=in_, func=func, **kw)), r=r, w=w)

    def TT(eng, out, in0, in1, op, r, w):
        P.add(eng, (lambda e: e.tensor_tensor(out=out, in0=in0, in1=in1, op=op)), r=r, w=w)

    def TS_(eng, out, in0, s1, s2, op0, op1, r, w):
        if s2 is None:
            P.add(eng, (lambda e: e.tensor_scalar(out=out, in0=in0, scalar1=s1, scalar2=None, op0=op0)), r=r, w=w)
        else:
            P.add(eng, (lambda e: e.tensor_scalar(out=out, in0=in0, scalar1=s1, scalar2=s2, op0=op0, op1=op1)), r=r, w=w)

    def STT(eng, out, in0, scalar, in1, op0, op1, r, w):
        P.add(eng, (lambda e: e.scalar_tensor_tensor(out=out, in0=in0, scalar=scalar, in1=in1, op0=op0, op1=op1)), r=r, w=w)

    def CP(eng, out, in_, r, w):
        if eng == "act":
            P.add(eng, (lambda e: e.copy(out=out, in_=in_)), r=r, w=w)
        else:
            P.add(eng, (lambda e: e.tensor_copy(out=out, in_=in_)), r=r, w=w)

    def DMA(eng, out, in_, r, w, key):
        P.add(eng, (lambda e: e.dma_start(out=out, in_=in_)), r=r, w=w, dma=key)

    def DMAT(out, in_, r, w, key):
        P.add("sp", (lambda e: e.dma_start_transpose(out=out, in_=in_)), r=r, w=w, dma=key)

    pstate = {"i": 0}

    def nbank():
        i = pstate["i"] % 7; pstate["i"] += 1
        return pbank[i], Bp[i]

    cast_layer_weights(0)
    for l in range(c.DEPTH):
        P.add("pool", (lambda e, l=l: e.dma_start(out=p_scr[l], in_=p_in[l])), r=(), w=(B_pscr,), dma="p_scr")
    DMA("sp", ident_f[:, :], cin["ident_f"][:, :], (), (B_const,), "const")
    DMA("sp", cb[:, :, :], cin["cb"][:, :, :], (), (B_const,), "const")
    DMA("sp", rt[:, :], cin["rt"][:, :], (), (B_const,), "const")
    DMA("sp", gv[:, :], gv_in[:, :], (), (B_const,), "const")

    m0 = sb.mark()
    xst = [sb.alloc("xst%d" % i, [128, D], F32) for i in range(2)]
    Bxst = [P.buf("xst%d" % i) for i in range(2)]
    ti = 0
    for bi, (t0, n) in enumerate(blocks):
        for tt in range(0, n, 128):
            rows = min(128, n - tt)
            s = ti % 2; ti += 1
            DMA("sp", xst[s][0:rows, :], x_in[t0 + tt:t0 + tt + rows, :], (), (Bxst[s],), "xst%d" % s)
            for f0 in range(0, KC, 4):
                nf = min(4, KC - f0)
                bank, bb = nbank()
                pe_group([TR(bank[:, k * 128:k * 128 + rows], xst[s][0:rows, (f0 + k) * 128:(f0 + k + 1) * 128], ident_f[0:rows, 0:rows])
                          for k in range(nf)], r=(Bxst[s], B_const), w=(bb,))
                CP("dve" if (f0 // 4) % 2 == 0 else "act",
                   xT[:, f0:f0 + nf, t0 + tt:t0 + tt + rows],
                   bank[:, 0:nf * 128].rearrange("p (k t) -> p k t", t=128)[:, :, 0:rows],
                   (bb,), tuple(B_x[bi][f0:f0 + nf]))
    sb.reset(m0)
    P.barrier()

    def rmsnorm(bi, t0, n, gcol, hT, hbuf, work, out_f32=False):
        sq, Bsq, lnv, Blnv, rstd, Brstd = work
        bank, bb = nbank()
        for fc in range(KC):
            s = fc % 2
            if fc % 2 == 0:
                ACT(sq[s][:, 0:n], xT[:, fc, t0:t0 + n], AF.Square, (B_x[bi][fc],), (Bsq[s],))
            else:
                TT("pool", sq[s][:, 0:n], xT[:, fc, t0:t0 + n], xT[:, fc, t0:t0 + n], ALU.mult, (B_x[bi][fc],), (Bsq[s],))
            pe_group([MM(bank[:, 0:n], ones_b, sq[s][:, 0:n], fc == 0, fc == KC - 1)], r=(Bsq[s], B_const), w=(bb,))
        ACT(lnv[:, 0:n], bank[:, 0:n], AF.Ln, (bb, B_const), (Blnv,), bias=rtc("eps", 0), scale=1.0 / D)
        ACT(rstd[:, 0:n], lnv[:, 0:n], AF.Exp, (Blnv,), (Brstd,), scale=-0.5)
        for fc in range(KC):
            STT("dve", hT[:, fc, 0:n], xT[:, fc, t0:t0 + n], gv[:, gcol + fc:gcol + fc + 1], rstd[:, 0:n],
                ALU.mult, ALU.mult, (B_x[bi][fc], Brstd, B_const), (hbuf,))

    def norm_work():
        sq = [sb.alloc("sq%d" % i, [128, 512], BF16) for i in range(2)]
        Bsq = [P.buf("sq%d" % i) for i in range(2)]
        lnv = sb.alloc("lnv", [128, 512], F32); rstd = sb.alloc("rstd", [128, 512], F32)
        return (sq, Bsq, lnv, P.buf("lnv"), rstd, P.buf("rstd"))

    def gcol_norm(l, i): return (l * 3 + i) * KC
    def gcol_ple(l): return (c.DEPTH * 3 + l) * KC
    GCOL_KV = (c.DEPTH * 4) * KC
    GCOL_FIN = (c.DEPTH * 4 + 1) * KC

    def ffn_groups():
        groups = []; cur = []; tot = 0
        for bi, (t0, n) in enumerate(blocks):
            if tot + n > 1100 and cur:
                groups.append(cur); cur = []; tot = 0
            cur.append((bi, t0, n)); tot += n
        groups.append(cur)
        return groups

    def ffn(l, wh):
        m = sb.mark()
        wk = norm_work()
        groups = ffn_groups()
        gmax = max(sum(n for (_, _, n) in g) for g in groups)
        hT = sb.alloc("hT", [128, KC, gmax], BF16)
        act = sb.alloc("act", [128, JC, gmax], BF16)
        sg = [sb.alloc("sg%d" % i, [128, 512], F32) for i in range(2)]
        Bsg = [P.buf("sg%d" % i) for i in range(2)]
        wg, wu, wd = wscr["g%d_%d" % (l, wh)], wscr["u%d_%d" % (l, wh)], wscr["d%d_%d" % (l, wh)]
        for grp in groups:
            Bh = [P.buf("hT%d" % k) for k in range(len(grp))]
            Bact = [[P.buf("act") for _ in range(JC)] for _ in grp]
            offs = []; o = 0
            for k, (bi, t0, n) in enumerate(grp):
                offs.append(o)
                rmsnorm(bi, t0, n, gcol_norm(l, 0 if wh == 0 else 2), hT[:, :, o:o + n], Bh[k], wk)
                o += n
            si = 0
            for jg in range(wg.NG):
                gt, ut, gb = load_w2(wg, wu, jg)
                ub = gb
                for jj in range(JW // 128):
                    j = jg * (JW // 128) + jj
                    for k, (bi, t0, n) in enumerate(grp):
                        o = offs[k]
                        b1, bb1 = nbank(); b2, bb2 = nbank()
                        pe_group([MM(b1[:, 0:n], gt[:, kc, jj * 128:(jj + 1) * 128], hT[:, kc, o:o + n], kc == 0, kc == KC - 1) for kc in range(KC)],
                                 r=(gb, Bh[k]), w=(bb1,))
                        pe_group([MM(b2[:, 0:n], ut[:, kc, jj * 128:(jj + 1) * 128], hT[:, kc, o:o + n], kc == 0, kc == KC - 1) for kc in range(KC)],
                                 r=(ub, Bh[k]), w=(bb2,))
                        s = si % 2; si += 1
                        ACT(sg[s][:, 0:n], b1[:, 0:n], AF.Silu, (bb1,), (Bsg[s],))
                        TT("dve", act[:, j, o:o + n], sg[s][:, 0:n], b2[:, 0:n], ALU.mult, (Bsg[s], bb2), (Bact[k][j],))
            for og in range(wd.NG):
                dt_, db = load_w(wd, og)
                for oo in range(OW // 128):
                    oc = og * (OW // 128) + oo
                    for k, (bi, t0, n) in enumerate(grp):
                        o = offs[k]
                        b1, bb1 = nbank()
                        pe_group([MM(b1[:, 0:n], dt_[:, j, oo * 128:(oo + 1) * 128], act[:, j, o:o + n], j == 0, j == JC - 1) for j in range(JC)],
                                 r=(db,) + tuple(Bact[k]), w=(bb1,))
                        STT("dve", xT[:, oc, t0:t0 + n], b1[:, 0:n], 0.5, xT[:, oc, t0:t0 + n], ALU.mult, ALU.add,
                            (bb1, B_x[bi][oc]), (B_x[bi][oc],))
        sb.reset(m)
        P.barrier()

    def ple(l):
        m = sb.mark()
        wk = norm_work()
        hT = sb.alloc("hT", [128, KC, 512], BF16); Bh = P.buf("hT")
        pT = sb.alloc("pT", [128, PC, 512], BF16); BpT = P.buf("pT")
        sgt = sb.alloc("sgt", [128, 512], F32); Bsgt = P.buf("sgt")
        upt = sb.alloc("upt", [128, 512], F32); Bupt = P.buf("upt")
        wu, wg = wscr["pu%d" % l], wscr["pg%d" % l]
        for bi, (t0, n) in enumerate(blocks):
            rmsnorm(bi, t0, n, gcol_ple(l), hT, Bh, wk)
            for pc in range(PC):
                DMAT(pT[:, pc, 0:n], p_scr[l, t0:t0 + n, pc * 128:(pc + 1) * 128], (B_pscr,), (BpT,), "pT")
            CWp = wu.CW
            for g in range(wu.NG):
                ut, ub = load_w(wu, g)
                gt, gb = load_w(wg, g)
                for oo in range(CWp // 128):
                    oc = g * (CWp // 128) + oo
                    b1, bb1 = nbank(); b2, bb2 = nbank()
                    pe_group([MM(b1[:, 0:n], ut[:, pc, oo * 128:(oo + 1) * 128], pT[:, pc, 0:n], pc == 0, pc == PC - 1) for pc in range(PC)],
                             r=(ub, BpT), w=(bb1,))
                    pe_group([MM(b2[:, 0:n], gt[:, kc, oo * 128:(oo + 1) * 128], hT[:, kc, 0:n], kc == 0, kc == KC - 1) for kc in range(KC)],
                             r=(gb, Bh), w=(bb2,))
                    ACT(sgt[:, 0:n], b2[:, 0:n], AF.Sigmoid, (bb2,), (Bsgt,))
                    TT("dve", upt[:, 0:n], sgt[:, 0:n], b1[:, 0:n], ALU.mult, (Bsgt, bb1), (Bupt,))
                    TT("dve", xT[:, oc, t0:t0 + n], xT[:, oc, t0:t0 + n], upt[:, 0:n], ALU.add, (Bupt, B_x[bi][oc]), (B_x[bi][oc],))
        sb.reset(m)
        P.barrier()


    groups_rep = [[sq * c.CPS + r for r in range(c.CPS)] for sq in range(c.BATCH)]

    def retention(l):
        m = sb.mark()
        wk = norm_work()
        RB = c.RBLK
        wi, wo = wscr["ri%d" % l], wscr["ro%d" % l]
        NQG = c.RQK // 512
        H2 = 2 * RH
        hT = sb.alloc("r_hT", [128, KC, RB], BF16); Bh = P.buf("r_hT")
        qkT = sb.alloc("r_qkT", [128, 2 * H2, RB], BF16); Bqk = [P.buf("r_qk%d" % i) for i in range(2 * H2)]
        cos = sb.alloc("r_cos", [128, RB], F32); sin = sb.alloc("r_sin", [128, RB], F32)
        Bcos = P.buf("cos"); Bsin = P.buf("sin")
        tmp = [sb.alloc("r_t%d" % i, [128, RB], F32) for i in range(4)]; Bt = [P.buf("r_t%d" % i) for i in range(4)]
        NCH = RB // 128
        v_tok = sb.alloc("r_v", [128, NCH, c.RV], BF16); Bv = [[P.buf("r_v") for _ in range(RH)] for _ in range(NCH)]
        sg_tok = sb.alloc("r_sg", [128, NCH, c.RV], BF16); Bsgt = [[P.buf("r_sg") for _ in range(RH)] for _ in range(NCH)]
        k_tok = sb.alloc("r_kt", [128, c.RQK], BF16); Bkt = P.buf("r_kt")
        sT = sb.alloc("r_sT", [128, RH, 128], BF16); BsT = P.buf("r_sT")
        on = [sb.alloc("r_on%d" % i, [128, 512], F32) for i in range(2)]; Bon = [P.buf("r_on%d" % i) for i in range(2)]
        og = sb.alloc("r_og", [128, c.RV], BF16); Bog = P.buf("r_og")
        ogT = sb.alloc("r_ogT", [128, VC, RB], BF16); BogT = P.buf("r_ogT")
        Sx = [sb.alloc("r_S%d" % i, [128, H2, 512], F32) for i in range(2)]
        BSx = [[P.buf("r_S%d_%d" % (i, hd)) for hd in range(H2)] for i in range(2)]
        S_bf = sb.alloc("r_Sbf", [128, H2, 512], BF16); BSbf = [P.buf("r_Sbf%d" % hd) for hd in range(H2)]
        ggn = sb.alloc("r_ggn", [128, c.RV], F32); Bggn = P.buf("r_ggn")
        decT = sb.alloc("r_decT", [128, RH, 128], F32); decTs = sb.alloc("r_decTs", [128, RH, TS], F32); Bdec = P.buf("r_dec")
        colm = sb.alloc("r_colm", [128, c.SPC, TS], BF16)
        stats = sb.alloc("r_stats", [128, 6], F32); mv = sb.alloc("r_mv", [128, 2], F32); sm = sb.alloc("r_sm", [128, 8], F32)
        Bst = P.buf("r_stats"); Bmv = P.buf("r_mv"); Bsm = P.buf("r_sm")
        qb = [sb.alloc("r_qb%d" % i, [128, H2, TS], BF16) for i in range(2)]; Bqb = [P.buf("r_qb%d" % i) for i in range(2)]
        kb = sb.alloc("r_kb", [128, c.RQK], BF16); Bkb = P.buf("r_kb")

        DMA("sp", ggn[:, :], ggn_in[l:l + 1, :].broadcast_to([128, c.RV]), (), (Bggn,), "r_c")
        DMA("sp", decT[:, :, :], cin["decT"][:, :, :], (), (Bdec,), "r_c")
        DMA("sp", decTs[:, :, :], cin["decTs"][:, :, :], (), (Bdec,), "r_c")
        DMA("sp", colm[:, :, :], cin["colm"][:, :, :], (), (Bdec,), "r_c")
        S = Sx[0]; BS = BSx[0]
        bank_o = [pbank[h] for h in range(RH)]; Bbo = [Bp[h] for h in range(RH)]
        bank_s, Bbs = pbank[4], Bp[4]
        ubanks = [(pbank[5], Bp[5]), (pbank[6], Bp[6])]
        ust = {"i": 0}
        HH = H2 // 2
        agin = [dscr("ag_s_in%d_%d" % (l, ch), [HH * 128, 512], F32) for ch in range(2)]
        agout = [dscr("ag_s_out%d_%d" % (l, ch), [c.CPS * HH * 128, 512], F32) for ch in range(2)]
        Bagin = P.buf("agin"); Bagout = P.buf("agout")

        def ret_block(t0, n, C, sample, p2):
            bi = c.NB if sample else t0 // c.BLK
            nch = n // C
            rmsnorm(bi, t0, n, gcol_norm(l, 1), hT, Bh, wk)
            DMA("sp", cos[:, 0:n], cin["rope"][0, :, t0:t0 + n], (), (Bcos,), "r_cos")
            DMA("sp", sin[:, 0:n], cin["rope"][1, :, t0:t0 + n], (), (Bsin,), "r_sin")
            glist = ([("q", g) for g in range(NQG)] if p2 else []) + [("k", NQG + g) for g in range(NQG)]
            for (kind, g) in glist:
                wt, wb = load_w(wi, g)
                for hh in range(2):
                    h = (g % NQG) * 2 + hh
                    bA, bbA = nbank(); bB, bbB = nbank()
                    pe_group([MM(bA[:, 0:n], wt[:, kc, (2 * hh) * 128:(2 * hh + 1) * 128], hT[:, kc, 0:n], kc == 0, kc == KC - 1) for kc in range(KC)],
                             r=(wb, Bh), w=(bbA,))
                    pe_group([MM(bB[:, 0:n], wt[:, kc, (2 * hh + 1) * 128:(2 * hh + 2) * 128], hT[:, kc, 0:n], kc == 0, kc == KC - 1) for kc in range(KC)],
                             r=(wb, Bh), w=(bbB,))
                    base = (0 if kind == "q" else H2) + 2 * h
                    TT("dve", tmp[0][:, 0:n], bA[:, 0:n], cos[:, 0:n], ALU.mult, (bbA, Bcos), (Bt[0],))
                    TT("dve", tmp[1][:, 0:n], bB[:, 0:n], sin[:, 0:n], ALU.mult, (bbB, Bsin), (Bt[1],))
                    TT("pool", qkT[:, base, 0:n], tmp[0][:, 0:n], tmp[1][:, 0:n], ALU.subtract, (Bt[0], Bt[1]), (Bqk[base],))
                    TT("dve", tmp[2][:, 0:n], bB[:, 0:n], cos[:, 0:n], ALU.mult, (bbB, Bcos), (Bt[2],))
                    TT("dve", tmp[3][:, 0:n], bA[:, 0:n], sin[:, 0:n], ALU.mult, (bbA, Bsin), (Bt[3],))
                    TT("pool", qkT[:, base + 1, 0:n], tmp[2][:, 0:n], tmp[3][:, 0:n], ALU.add, (Bt[2], Bt[3]), (Bqk[base + 1],))
            for kind in (("v", "g") if p2 else ("v",)):
                for h in range(RH):
                    wt, wb = load_w(wi, 2 * NQG + (0 if kind == "v" else RH) + h)
                    for ci in range(nch):
                        c0 = ci * C
                        bk, bbk = nbank()
                        pe_group([MM(bk[0:C, :], hT[:, kc, c0:c0 + C], wt[:, kc, :], kc == 0, kc == KC - 1) for kc in range(KC)],
                                 r=(wb, Bh), w=(bbk,))
                        if kind == "v":
                            CP("act", v_tok[0:C, ci, h * 512:(h + 1) * 512], bk[0:C, :], (bbk,), (Bv[ci][h],))
                        else:
                            ACT(sg_tok[0:C, ci, h * 512:(h + 1) * 512], bk[0:C, :], AF.Silu, (bbk,), (Bsgt[ci][h],))
            dT = decTs if sample else decT
            tci, tci2, tkd = ("cis", "cis2", "kdecs") if sample else ("ci", "ci2", "kdec")
            for ci in range(nch):
                c0 = ci * C
                pe_group([TR(pbf[0:C, hc * 128:(hc + 1) * 128], qkT[:, H2 + hc, c0:c0 + C], ident_b) for hc in range(H2)],
                         r=tuple(Bqk[H2:2 * H2]) + (B_const,), w=(Bpbf,))
                for h in range(RH):
                    TS_("dve", k_tok[0:C, h * 256:(h + 1) * 256], pbf[0:C, h * 256:(h + 1) * 256], rtc(tkd, h, 1, C), None, ALU.mult, None,
                        (Bpbf, B_const), (Bkt,))
                if p2:
                    pe_group([MM(bank_s[0:C, h * C:(h + 1) * C], qkT[:, H2 + 2 * h + hf, c0:c0 + C], qkT[:, 2 * h + hf, c0:c0 + C], hf == 0, hf == 1)
                              for h in range(RH) for hf in range(2)], r=tuple(Bqk), w=(Bbs,))
                    TT("dve", sT[0:C, :, 0:C], bank_s[0:C, 0:RH * C].rearrange("p (h i) -> p h i", i=C), dT[0:C, :, 0:C], ALU.mult,
                       (Bbs, Bdec), (BsT,))
                    for h in range(RH):
                        fl_ = [MM(bank_o[h][0:C, :], sT[0:C, h, 0:C], v_tok[0:C, ci, h * 512:(h + 1) * 512], True, False)]
                        if not sample:
                            fl_ += [MM(bank_o[h][0:C, :], qkT[:, 2 * h + dc, c0:c0 + C], S_bf[:, 2 * h + dc, :], False, dc == 1) for dc in range(2)]
                        pe_group(fl_, r=(BsT, Bv[ci][h], Bqk[2 * h], Bqk[2 * h + 1], BSbf[2 * h], BSbf[2 * h + 1]), w=(Bbo[h],))
                if sample:
                    for b in range(c.SPC):
                        sx = Sx[b % 2]; bsx = BSx[b % 2]
                        DMA("sp", sx[:, :, :], st_in[l, b].rearrange("h (dc p) v -> p (h dc) v", p=128), (), tuple(bsx), "r_S%d" % (b % 2))
                        for hd in range(H2):
                            CP("act" if hd % 2 == 0 else "pool", S_bf[:, hd, :], sx[:, hd, :], (bsx[hd],), (BSbf[hd],))
                        TT("dve", qb[b % 2][:, :, :], qkT[:, 0:H2, 0:TS], colm[:, b:b + 1, :].broadcast_to([128, H2, TS]), ALU.mult,
                           tuple(Bqk[0:H2]) + (Bdec,), (Bqb[b % 2],))
                        for h in range(RH):
                            pe_group([MM(bank_o[h][0:C, :], qb[b % 2][:, 2 * h + dc, :], S_bf[:, 2 * h + dc, :], False, (b == c.SPC - 1 and dc == 1)) for dc in range(2)],
                                     r=(Bqb[b % 2], BSbf[2 * h], BSbf[2 * h + 1]), w=(Bbo[h],))
                        TS_("dve", kb[0:C, :], k_tok[0:C, :], rtc("rowm", b, 1, C), None, ALU.mult, None, (Bkt, B_const), (Bkb,))
                        for h in range(RH):
                            for dc in range(2):
                                hd = 2 * h + dc
                                bu, bbu = ubanks[ust["i"] % 2]; ust["i"] += 1
                                pe_group([MM(bu[:, :], kb[0:C, hd * 128:(hd + 1) * 128], v_tok[0:C, ci, h * 512:(h + 1) * 512], True, True)],
                                         r=(Bkb, Bv[ci][h]), w=(bbu,))
                                STT("dve", sx[:, hd, :], sx[:, hd, :], rtc("g4", h), bu[:, :], ALU.mult, ALU.add, (bbu, bsx[hd], B_const, BSbf[hd]), (bsx[hd],))
                        DMA("sp", ss_out[l, b].rearrange("h (dc p) v -> p (h dc) v", p=128), sx[:, :, :], tuple(bsx), (), "r_So%d" % (b % 2))
                if p2:
                    for h in range(RH):
                        P.add("dve", (lambda e, h=h: e.bn_stats(out=stats[0:C, :], in_=bank_o[h][0:C, :])), r=(Bbo[h],), w=(Bst,))
                        P.add("dve", (lambda e: e.bn_aggr(out=mv[0:C, :], in_=stats[0:C, :])), r=(Bst,), w=(Bmv,))
                        TS_("dve", sm[0:C, 0:1], mv[0:C, 1:2], rtc(tci2, h, 1, C), rtc("eps", 0, 1, C), ALU.mult, ALU.add, (Bmv, B_const), (Bsm,))
                        ACT(sm[0:C, 1:2], sm[0:C, 0:1], AF.Ln, (Bsm,), (Bsm,))
                        ACT(sm[0:C, 2:3], sm[0:C, 1:2], AF.Exp, (Bsm,), (Bsm,), scale=-0.5)
                        TT("dve", sm[0:C, 3:4], sm[0:C, 2:3], rtc(tci, h, 1, C), ALU.mult, (Bsm, B_const), (Bsm,))
                        STT("dve", sm[0:C, 4:5], mv[0:C, 0:1], -1.0, sm[0:C, 3:4], ALU.mult, ALU.mult, (Bmv, Bsm), (Bsm,))
                        ACT(on[0][0:C, :], bank_o[h][0:C, :], AF.Identity, (Bbo[h], Bsm), (Bon[0],), bias=sm[0:C, 4:5], scale=sm[0:C, 3:4])
                        TT("pool", on[1][0:C, :], on[0][0:C, :], ggn[0:C, h * 512:(h + 1) * 512], ALU.mult, (Bon[0], Bggn), (Bon[1],))
                        TT("dve", og[0:C, h * 512:(h + 1) * 512], on[1][0:C, :], sg_tok[0:C, ci, h * 512:(h + 1) * 512], ALU.mult,
                           (Bon[1], Bsgt[ci][h]), (Bog,))
                    for v0 in range(0, VC, 8):
                        nv = min(8, VC - v0)
                        pe_group([TR(pbf[:, k * C:(k + 1) * C], og[0:C, (v0 + k) * 128:(v0 + k + 1) * 128], ident_b[0:C, 0:C]) for k in range(nv)],
                                 r=(Bog, B_const), w=(Bpbf,))
                        CP("act", ogT[:, v0:v0 + nv, c0:c0 + C], pbf[:, 0:nv * C].rearrange("p (k t) -> p k t", t=C), (Bpbf,), (BogT,))
                if not sample:
                    for h in range(RH):
                        for dc in range(2):
                            hd = 2 * h + dc
                            bu, bbu = ubanks[ust["i"] % 2]; ust["i"] += 1
                            pe_group([MM(bu[:, :], k_tok[0:C, hd * 128:(hd + 1) * 128], v_tok[0:C, ci, h * 512:(h + 1) * 512], True, True)],
                                     r=(Bkt, Bv[ci][h]), w=(bbu,))
                            STT("dve", S[:, hd, :], S[:, hd, :], rtc("g128", h), bu[:, :], ALU.mult, ALU.add, (bbu, BS[hd], B_const, BSbf[hd]), (BS[hd],))
                            if p2:
                                CP("act" if dc == 0 else "pool", S_bf[:, hd, :], S[:, hd, :], (BS[hd],), (BSbf[hd],))
            if p2:
                for g in range(wo.NG):
                    wt, wb = load_w(wo, g)
                    for oo in range(OW // 128):
                        oc = g * (OW // 128) + oo
                        bk, bbk = nbank()
                        pe_group([MM(bk[:, 0:n], wt[:, vc, oo * 128:(oo + 1) * 128], ogT[:, vc, 0:n], vc == 0, vc == VC - 1) for vc in range(VC)],
                                 r=(wb, BogT), w=(bbk,))
                        TT("dve", xT[:, oc, t0:t0 + n], xT[:, oc, t0:t0 + n], bk[:, 0:n], ALU.add, (bbk, B_x[bi][oc]), (B_x[bi][oc],))

        for hd in range(H2):
            P.add("dve", (lambda e, hd=hd: e.memset(S[:, hd, :], 0.0)), r=(), w=(BS[hd],))
        for rb in range(c.NRB):
            ret_block(rb * RB, RB, 128, False, False)
        for ch in range(2):
            DMA("sp", agin[ch].ap().rearrange("(hd p) v -> p hd v", p=128), S[:, ch * HH:(ch + 1) * HH, :], tuple(BS), (Bagin,), "r_ag")
        for ch in range(2):
            P.add("pool", (lambda e, ch=ch: e.collective_compute("AllGather", ALU.bypass, replica_groups=groups_rep,
                                                                  ins=[agin[ch].ap().opt()], outs=[agout[ch].ap().opt()])),
                  r=(Bagin,), w=(Bagout,), dma="cc_s%d_%d" % (l, ch), inc=1)
        for hd in range(H2):
            P.add("dve", (lambda e, hd=hd: e.memset(S[:, hd, :], 0.0)), r=(), w=(BS[hd],))
        for r_ in range(c.CPS):
            for ch in range(2):
                DMA("sp", Sx[1][:, ch * HH:(ch + 1) * HH, :], agout[ch].ap()[r_ * HH * 128:(r_ + 1) * HH * 128, :].rearrange("(hd p) v -> p hd v", p=128),
                    (Bagout,), tuple(BSx[1]), "r_S1")
            for hd in range(H2):
                STT("dve", S[:, hd, :], Sx[1][:, hd, :], rtc("coef", r_ * RH + hd // 2), S[:, hd, :], ALU.mult, ALU.add,
                    (BSx[1][hd], BS[hd], B_const), (BS[hd],))
        for hd in range(H2):
            CP("act" if hd % 2 == 0 else "pool", S_bf[:, hd, :], S[:, hd, :], (BS[hd],), (BSbf[hd],))
        for rb in range(c.NRB):
            ret_block(rb * RB, RB, 128, False, True)
        DMA("sp", sp_out[l].rearrange("h (dc p) v -> p (h dc) v", p=128), S[:, :, :], tuple(BS), (), "r_So0")
        ret_block(SEG, TS, TS, True, True)
        sb.reset(m)
        P.barrier()


    NCH = D // 256
    ag_kT_in = [dscr("ag_kT_in%d" % ch, [256, SEG]) for ch in range(NCH)]
    ag_kT_out = [dscr("ag_kT_out%d" % ch, [c.CPS * 256, SEG]) for ch in range(NCH)]
    agvi = [dscr("agvi%d" % ch, [SEG, 256]) for ch in range(NCH)]
    agvo = [dscr("agvo%d" % ch, [c.CPS * SEG, 256]) for ch in range(NCH)]
    B_agk_in = P.buf("agk_in"); B_agk_out = P.buf("agk_out"); B_agv_in = P.buf("agv_in"); B_agv_out = P.buf("agv_out")

    def kv_phase():
        m = sb.mark()
        wk = norm_work()
        wkv = wscr["kv"]; CW = wkv.CW; NGK = D // CW
        hT = sb.alloc("kv_hT", [128, KC, 512], BF16); Bh = P.buf("kv_hT")
        kst = [sb.alloc("kv_kst%d" % i, [128, 512], BF16) for i in range(2)]; Bkst = [P.buf("kst%d" % i) for i in range(2)]
        st32 = [sb.alloc("kv_st%d" % i, [128, 512], F32) for i in range(2)]; Bst32 = [P.buf("st32%d" % i) for i in range(2)]
        vst = [sb.alloc("kv_vst%d" % i, [128, 512], BF16) for i in range(2)]; Bvst = [P.buf("vst%d" % i) for i in range(2)]
        i1 = 0; i2 = 0
        for bi, (t0, n) in enumerate(blocks):
            sample = bi == c.NB
            rmsnorm(bi, t0, n, GCOL_KV, hT, Bh, wk)
            for g in range(NGK if fl.get("kva", True) else 0):
                wt, wb = load_w(wkv, g)
                for oo in range(CW // 128):
                    fc = g * (CW // 128) + oo
                    bk, bbk = nbank()
                    pe_group([MM(bk[:, 0:n], wt[:, kc, oo * 128:(oo + 1) * 128], hT[:, kc, 0:n], kc == 0, kc == KC - 1) for kc in range(KC)],
                             r=(wb, Bh), w=(bbk,))
                    if sample:
                        CP("act", kTs[:, fc, 0:n], bk[:, 0:n], (bbk,), (B_kTs,))
                    else:
                        s_ = i1 % 2; i1 += 1
                        CP("act", kst[s_][:, 0:n], bk[:, 0:n], (bbk,), (Bkst[s_],))
                        DMA("sp", ag_kT_in[fc // 2][(fc % 2) * 128:(fc % 2 + 1) * 128, t0:t0 + n], kst[s_][:, 0:n], (Bkst[s_],), (B_agk_in,), "kst%d" % s_)
            for g in range(2 * NGK if fl.get("kvb", True) else 0):
                wt, wb = load_w(wkv, g)
                isv = g >= NGK
                col = (g % NGK) * CW
                for tt in range(0, n, 128):
                    rows = min(128, n - tt)
                    bk, bbk = nbank()
                    pe_group([MM(bk[0:rows, 0:CW], hT[:, kc, tt:tt + rows], wt[:, kc, :], kc == 0, kc == KC - 1) for kc in range(KC)],
                             r=(wb, Bh), w=(bbk,))
                    s_ = i2 % 2; i2 += 1
                    CP("dve", st32[s_][0:rows, 0:CW], bk[0:rows, 0:CW], (bbk,), (Bst32[s_],))
                    dst = v_out if isv else k_out
                    if fl.get("kvo", True):
                        DMA("sp", dst[t0 + tt:t0 + tt + rows, col:col + CW], st32[s_][0:rows, 0:CW], (Bst32[s_],), (), "st32%d" % s_)
                    if isv and fl.get("kvv", True):
                        if sample:
                            CP("act", vs_tok[0:rows, col:col + CW], st32[s_][0:rows, 0:CW], (Bst32[s_],), (B_vs,))
                        else:
                            CP("act", vst[s_][0:rows, 0:CW], st32[s_][0:rows, 0:CW], (Bst32[s_],), (Bvst[s_],))
                            for cc_ in range(0, CW, 256):
                                DMA("sp", agvi[(col + cc_) // 256][t0 + tt:t0 + tt + rows, :], vst[s_][0:rows, cc_:cc_ + 256], (Bvst[s_],), (B_agv_in,), "vst%d" % s_)
        if c.__dict__.get("flags", {}).get("kvcc", True):
            for ch in range(NCH):
                P.add("pool", (lambda e, ch=ch: e.collective_compute("AllGather", ALU.bypass, replica_groups=groups_rep,
                                                                      ins=[ag_kT_in[ch].ap().opt()], outs=[ag_kT_out[ch].ap().opt()])),
                      r=(B_agk_in,), w=(B_agk_out,), dma="cc_k%d" % ch, inc=1)
                P.add("pool", (lambda e, ch=ch: e.collective_compute("AllGather", ALU.bypass, replica_groups=groups_rep,
                                                                      ins=[agvi[ch].ap().opt()], outs=[agvo[ch].ap().opt()])),
                      r=(B_agv_in,), w=(B_agv_out,), dma="cc_v%d" % ch, inc=1)
        sb.reset(m)
        P.barrier()

    def sb_layer(l):
        li = l - c.NA
        m = sb.mark()
        wq, wo = wscr["sq%d" % l], wscr["so%d" % l]
        SBH, DSEQ, NPG, SPC, NKB, CPS = c.SBH, c.DSEQ, c.NPG, c.SPC, c.NKB, c.CPS
        HQ = SBH * DSEQ
        QT = sb.alloc("QT", [128, KC, T], BF16)
        BQ = [[[P.buf("Q") for _ in range(2)] for _ in range(KC)] for _ in range(NBT)]
        BQs = [P.buf("Qs%d" % b) for b in range(SPC)]
        m1 = sb.mark()
        wk = norm_work()
        hT = sb.alloc("sb_hT", [128, KC, 512], BF16); Bh = P.buf("sb_hT")
        for bi, (t0, n) in enumerate(blocks):
            rmsnorm(bi, t0, n, gcol_norm(l, 1), hT, Bh, wk)
            for g in range(wq.NG):
                wt, wb = load_w(wq, g)
                for oo in range(wq.CW // 128):
                    fc = g * (wq.CW // 128) + oo
                    bk, bbk = nbank()
                    pe_group([MM(bk[:, 0:n], wt[:, kc, oo * 128:(oo + 1) * 128], hT[:, kc, 0:n], kc == 0, kc == KC - 1) for kc in range(KC)],
                             r=(wb, Bh), w=(bbk,))
                    wl = tuple(BQ[bi][fc]) + (tuple(BQs) if bi == c.NB else ())
                    ACT(QT[:, fc, t0:t0 + n], bk[:, 0:n], AF.Identity, (bbk,), wl, scale=float(c.HD) ** -0.5)
        sb.reset(m1)
        P.barrier()
        NSL = CPS + 1
        KT = sb.alloc("KT", [128, NSL, SEG], BF16); BKT = P.buf("KT")
        Vt = sb.alloc("Vt", [128, NSL * NKB, 128], BF16); BVt = P.buf("Vt")
        mneg = sb.alloc("mneg", [128, c.BLK // 128, c.BLK], BF16); Bmn = P.buf("mneg")
        e32 = [sb.alloc("e32_%d" % i, [128, 512], F32) for i in range(2)]; Be32 = [P.buf("e32") for i in range(2)]
        spt = [sb.alloc("sp_%d" % i, [128, 512], BF16) for i in range(2)]; Bspt = [P.buf("sp") for i in range(2)]
        sps32 = sb.alloc("sps32", [128, 512], F32); Bsps32 = P.buf("sps32")
        spsb = [sb.alloc("spsb_%d" % i, [128, 512], BF16) for i in range(2)]; Bspsb = [P.buf("spsb") for i in range(2)]
        aex = [sb.alloc("aex_%d" % i, [128, 512], BF16) for i in range(2)]; Baex = [P.buf("aex") for i in range(2)]
        bsb = sb.alloc("bsb", [128, SBH], F32); Bbsb = P.buf("bsb")
        biasP = sb.alloc("biasP", [128, NSL, SBH], F32)
        biasS = sb.alloc("biasS", [128, SBH, DSEQ], F32)
        newb = sb.alloc("newb", [128, SPC, HQ], F32); Bnewb = P.buf("newb")
        Ksb = [sb.alloc("Ksb%d" % i, [128, D], BF16) for i in range(2)]; BKsb = [P.buf("Ksb") for i in range(2)]
        Vsb = [sb.alloc("Vsb%d" % i, [128, D], BF16) for i in range(2)]; BVsb = [P.buf("Vsb") for i in range(2)]
        KTp = [sb.alloc("KTp%d" % i, [128, KC, 128], BF16) for i in range(2)]; BKTp = [P.buf("KTp") for i in range(2)]
        zs = sb.alloc("zs", [128, HQ], F32); Bzs = P.buf("zs")
        es = sb.alloc("es", [128, HQ], F32); Bes = P.buf("es")
        sps = sb.alloc("sps", [128, HQ], BF16); Bsps = P.buf("sps")
        ss32 = sb.alloc("ss32", [128, HQ], F32); Bss32 = P.buf("ss32")
        ssb = [sb.alloc("ssb%d" % i, [128, HQ], BF16) for i in range(2)]; Bssb = [P.buf("ssb") for i in range(2)]
        ts_ = sb.alloc("ts", [128, HQ], F32); Bts = P.buf("ts")
        axs = sb.alloc("axs", [128, HQ], BF16); Baxs = P.buf("axs")
        zer = sb.alloc("zer", [128, 2 * KC * DSEQ], BF16); Bzer = P.buf("zer")
        P.add("dve", (lambda e: e.memset(zer[:, :], 0.0)), r=(), w=(Bzer,))
        ptile = sb.alloc("ptile", [128, SPC * NPG], I32); iot = sb.alloc("iot", [128, 1], I32); idx = sb.alloc("idx", [128, SPC * NPG], I32)
        Bidx = P.buf("idx"); Bpt = P.buf("ptile"); Biot = P.buf("iot")

        DMA("sp", mneg[:, :, :], cin["mneg"][:, :, :], (), (Bmn,), "sb_c")
        DMA("sp", bsb[:, :], bsb_in[li:li + 1, :].broadcast_to([128, SBH]), (), (Bbsb,), "sb_c")
        DMA("sp", newb[:, :, :], cin["smask"][:, :, :], (), (Bnewb,), "sb_c")
        DMA("sp", ptile[:, :], pt_in.ap().rearrange("b j -> (b j)").unsqueeze(0).broadcast_to([128, SPC * NPG]), (), (Bpt,), "sb_c")
        P.add("pool", (lambda e: e.iota(iot[:, :], pattern=[[0, 1]], base=0, channel_multiplier=1)), r=(), w=(Biot,))
        TS_("dve", idx[:, :], ptile[:, :], 128, iot[:, 0:1], ALU.mult, ALU.add, (Bpt, Biot), (Bidx,))
        for sl in range(NSL):
            TS_("dve", biasP[:, sl, :], bsb[:, :], rtc("segm", sl), None, ALU.add, None, (Bbsb, B_const), (Bbsb,))
        for q_ in range(DSEQ):
            CP("dve", biasS[:, :, q_], bsb[:, :], (Bbsb,), (Bbsb,))
        for b in range(SPC):
            TT("dve", newb[:, b, :], newb[:, b, :], biasS[:, :, :].rearrange("p h q -> p (h q)"), ALU.add, (Bnewb, Bbsb), (Bnewb,))

        QTz = sb.alloc("QTz", [128, KC, 2, TS], BF16); BQTz = P.buf("QTz")
        P.add("dve", (lambda e: e.memset(QTz[:, :, :, :], 0.0)), r=(), w=(BQTz,))
        CP("dve", QTz[0:64, :, 0, :], QT[0:64, :, SEG:SEG + TS], tuple(BQs), (BQTz,))
        CP("dve", QTz[64:128, :, 1, :], QT[64:128, :, SEG:SEG + TS], tuple(BQs), (BQTz,))
        zbanks = [(pbank[0], Bp[0]), (pbank[1], Bp[1])]
        abanks = [(pbank[2], Bp[2]), (pbank[3], Bp[3])]
        obanks = [(pbank[4], Bp[4]), (pbank[5], Bp[5])]
        cnt = {"t": 0, "o": 0, "g": 0}
        for fc in range(KC if fl.get("sbp", True) else 0):
            fo_ = (fc % 2) * 128
            DMA("sp", KT[:, 0, :], ag_kT_in[fc // 2][fo_:fo_ + 128, :], (B_agk_in,), (BKT,), "KT")
            DMA("sp", Vt[:, 0:NKB, :], agvi[fc // 2][:, fo_:fo_ + 128].rearrange("(kb p) c -> p kb c", p=128), (B_agv_in,), (BVt,), "Vt")
            for si in range(1, CPS):
                r_ = CPS - 1 - si
                DMA("sp", KT[:, 1 + si, :], ag_kT_out[fc // 2][r_ * 256 + fo_:r_ * 256 + fo_ + 128, :], (B_agk_out,), (BKT,), "KT")
                DMA("sp", Vt[:, (1 + si) * NKB:(2 + si) * NKB, :],
                    agvo[fc // 2][r_ * SEG:(r_ + 1) * SEG, fo_:fo_ + 128].rearrange("(kb p) c -> p kb c", p=128), (B_agv_out,), (BVt,), "Vt")
            jobs = []
            for e_ in range(2):
                h = 2 * fc + e_; pb = 64 * e_
                for bi in range(c.NB):
                    t0, n = blocks[bi]
                    tiles = [(0, kb, (kb - t0 // 128) if kb * 128 >= t0 else None) for kb in reversed(range((t0 + n) // 128))]
                    tiles += [(1 + si, kb, None) for si in range(1, CPS) for kb in reversed(range(NKB))]
                    obo = obanks[cnt["o"] % 2]; cnt["o"] += 1
                    for ti_, (slot, kb, diag) in enumerate(tiles):
                        jobs.append((h, pb, e_, bi, t0, n, slot, kb, diag, ti_ == 0, ti_ == len(tiles) - 1, cnt["t"] % 2, obo))
                        cnt["t"] += 1

            def job_ctx(J):
                h, pb, e_, bi, t0, n, slot, kb, diag, first, last, j, obo = J
                Qh = QT[pb:pb + 64, fc, t0:t0 + n]
                Kh = KT[pb:pb + 64, slot, kb * 128:(kb + 1) * 128]
                bias = biasP[:, slot, h:h + 1]
                rd = (BKT, BQ[bi][fc][e_]) + ((Bmn, B_const) if diag is not None else ())
                return Qh, Kh, bias, rd

            def stage_a(J):
                h, pb, e_, bi, t0, n, slot, kb, diag, first, last, j, obo = J
                Qh, Kh, bias, rd = job_ctx(J)
                zb, bzb = zbanks[j]
                fz = [MM(zb[:, 0:n], Kh, Qh, True, diag is None)]
                if diag is not None: fz.append(MM(zb[:, 0:n], ident_b, mneg[:, diag, 0:n], False, True))
                pe_group(fz, r=rd, w=(bzb,))
                ACT(e32[j][:, 0:n], zb[:, 0:n], AF.Exp, (bzb, Bbsb), (Be32[j],), bias=bias)
                ACT(spt[j][:, 0:n], e32[j][:, 0:n], AF.Ln, (Be32[j], B_const), (Bspt[j],), bias=rtc("one", 0))

            def stage_b(J):
                h, pb, e_, bi, t0, n, slot, kb, diag, first, last, j, obo = J
                Qh, Kh, bias, rd = job_ctx(J)
                ab, bab = abanks[j]
                fa = [MM(ab[:, 0:n], Kh, Qh, True, False)]
                if diag is not None: fa.append(MM(ab[:, 0:n], ident_b, mneg[:, diag, 0:n], False, False))
                fa.append(MM(ab[:, 0:n], ntri_b, spt[j][:, 0:n], False, first))
                rr = rd + (Bspt[j], B_const)
                if not first:
                    fa.append(MM(ab[:, 0:n], nones_b, spsb[(j + 1) % 2][:, 0:n], False, True))
                    rr = rr + (Bspsb[(j + 1) % 2],)
                pe_group(fa, r=rr, w=(bab,))
                if not last:
                    if first:
                        CP("dve", sps32[:, 0:n], spt[j][:, 0:n], (Bspt[j],), (Bsps32,))
                    else:
                        TT("dve", sps32[:, 0:n], sps32[:, 0:n], spt[j][:, 0:n], ALU.add, (Bspt[j], Bsps32), (Bsps32,))
                    CP("dve", spsb[j][:, 0:n], sps32[:, 0:n], (Bsps32,), (Bspsb[j],))
                ACT(aex[j][:, 0:n], ab[:, 0:n], AF.Exp, (bab, Bbsb), (Baex[j],), bias=bias)

            def stage_c(J):
                h, pb, e_, bi, t0, n, slot, kb, diag, first, last, j, obo = J
                ob, bob = obo
                pe_group([MM(ob[:, 0:n], Vt[:, slot * NKB + kb, :], aex[j][:, 0:n], first, last)], r=(BVt, Baex[j]), w=(bob,))
                if last:
                    CP("dve", QT[pb:pb + 64, fc, t0:t0 + n], ob[pb:pb + 64, 0:n], (bob,), (BQ[bi][fc][e_],))

            NJ = len(jobs)
            if not fl.get("swp", True):
                for J in jobs:
                    stage_a(J); stage_b(J); stage_c(J)
            else:
                for i_ in range(NJ + 2):
                    if i_ < NJ: stage_a(jobs[i_])
                    if 0 <= i_ - 1 < NJ: stage_b(jobs[i_ - 1])
                    if 0 <= i_ - 2 < NJ: stage_c(jobs[i_ - 2])
        ts0 = SEG
        for b in range(SPC if fl.get("sbs", True) else 0):
            qc0 = ts0 + b * DSEQ
            ob, bob = obanks[cnt["o"] % 2]; cnt["o"] += 1
            P.add("dve", (lambda e: e.memset(ss32[:, :], 0.0)), r=(), w=(Bss32,))
            ntile = NPG + 1 if fl.get("sbs1", True) else 1
            for ti_ in range(ntile):
                first = ti_ == 0; last = ti_ == ntile - 1
                j = cnt["t"] % 2; cnt["t"] += 1
                zb, bzb = zbanks[j]; ab, bab = abanks[j]
                if first:
                    rows = TS
                    def Kof(fc_): return kTs[:, fc_, 0:TS]
                    Vsrc = vs_tok; rk = (B_kTs,); rv = (B_vs,)
                    btile = newb[0:rows, b, :]
                else:
                    pg = NPG - ti_
                    g_ = cnt["g"] % 2; cnt["g"] += 1
                    rows = 128
                    col = b * NPG + pg
                    P.add("pool", (lambda e, g_=g_, col=col: e.indirect_dma_start(
                        out=Ksb[g_][:, :], out_offset=None, in_=ck_in[:, :],
                        in_offset=bass.IndirectOffsetOnAxis(ap=idx[:, col:col + 1], axis=0))), r=(Bidx,), w=(BKsb[g_],), dma="Ksb%d" % g_)
                    P.add("pool", (lambda e, g_=g_, col=col: e.indirect_dma_start(
                        out=Vsb[g_][:, :], out_offset=None, in_=cv_in[:, :],
                        in_offset=bass.IndirectOffsetOnAxis(ap=idx[:, col:col + 1], axis=0))), r=(Bidx,), w=(BVsb[g_],), dma="Vsb%d" % g_)
                    pe_group([TR(pbf[:, k * 128:(k + 1) * 128], Ksb[g_][:, k * 128:(k + 1) * 128], ident_b) for k in range(KC)],
                             r=(BKsb[g_], B_const), w=(Bpbf,))
                    CP("dve", KTp[g_][:, :, :], pbf[:, 0:KC * 128].rearrange("p (k t) -> p k t", t=128), (Bpbf,), (BKTp[g_],))
                    def Kof(fc_, g_=g_): return KTp[g_][:, fc_, :]
                    Vsrc = Vsb[g_]; rk = (BKTp[g_],); rv = (BVsb[g_],)
                    btile = biasS[:, :, :].rearrange("p h q -> p (h q)")
                pe_group([MM(zb[0:rows, fc_ * 2 * DSEQ:(fc_ + 1) * 2 * DSEQ], Kof(fc_), QTz[:, fc_, :, b * DSEQ:(b + 1) * DSEQ], True, True)
                          for fc_ in range(KC)], r=rk + (BQTz,), w=(bzb,))
                TT("dve", zs[0:rows, :], zb[0:rows, 0:HQ], btile, ALU.add, (bzb, Bbsb, Bnewb), (Bzs,))
                ACT(es[0:rows, :], zs[0:rows, :], AF.Exp, (Bzs,), (Bes,))
                ACT(sps[0:rows, :], es[0:rows, :], AF.Ln, (Bes, B_const), (Bsps,), bias=rtc("one", 0, 1, rows))
                if not fl.get("sba", True): continue
                fa = [MM(ab[0:rows, 0:HQ], ntri_b[0:rows, 0:rows], sps[0:rows, :], True, False)]
                rr = rk + (BQTz, Bsps, B_const)
                if not first:
                    fa.append(MM(ab[0:rows, 0:HQ], nones_b[:, 0:rows], ssb[(j + 1) % 2][:, :], False, False))
                    rr = rr + (Bssb[(j + 1) % 2],)
                fa += [MM(ab[0:rows, fc_ * 2 * DSEQ:(fc_ + 1) * 2 * DSEQ], Kof(fc_), QTz[:, fc_, :, b * DSEQ:(b + 1) * DSEQ], False, fc_ == KC - 1)
                       for fc_ in range(KC)]
                pe_group(fa, r=rr, w=(bab,))
                if not last:
                    TT("dve", ss32[0:rows, :], ss32[0:rows, :], sps[0:rows, :], ALU.add, (Bsps, Bss32), (Bss32,))
                    CP("dve", ssb[j][:, :], ss32[:, :], (Bss32,), (Bssb[j],))
                TT("dve", ts_[0:rows, :], ab[0:rows, 0:HQ], btile, ALU.add, (bab, Bbsb, Bnewb), (Bts,))
                ACT(axs[0:rows, :], ts_[0:rows, :], AF.Exp, (Bts,), (Baxs,))
                if not fl.get("sbo", True): continue
                fo = [MM(ob[:, 0:2 * KC * DSEQ], ident_b, zer[:, 0:2 * KC * DSEQ], True, False)] if first else []
                fo += [MM(ob[:, (hh % 2) * KC * DSEQ + (hh // 2) * DSEQ:(hh % 2) * KC * DSEQ + (hh // 2 + 1) * DSEQ],
                          Vsrc[0:rows, (hh // 2) * 128:(hh // 2 + 1) * 128], axs[0:rows, hh * DSEQ:(hh + 1) * DSEQ], False, last and hh == SBH - 1) for hh in range(SBH)]
                pe_group(fo, r=rv + (Baxs, Bzer, B_const), w=(bob,))
            for e_ in range(2 if (fl.get("sbo", True) and fl.get("sba", True)) else 0):
                CP("dve", QT[64 * e_:64 * e_ + 64, :, qc0:qc0 + DSEQ],
                   ob[64 * e_:64 * e_ + 64, e_ * KC * DSEQ:(e_ + 1) * KC * DSEQ].rearrange("p (k q) -> p k q", q=DSEQ), (bob,), (BQs[b],))
        for bi, (t0, n) in enumerate(blocks):
            for g in range(wo.NG):
                wt, wb = load_w(wo, g)
                for oo in range(wo.CW // 128):
                    oc = g * (wo.CW // 128) + oo
                    bk, bbk = nbank()
                    rq = tuple(BQ[bi][kc][e_] for kc in range(KC) for e_ in range(2)) + (tuple(BQs) if bi == c.NB else ())
                    pe_group([MM(bk[:, 0:n], wt[:, kc, oo * 128:(oo + 1) * 128], QT[:, kc, t0:t0 + n], kc == 0, kc == KC - 1) for kc in range(KC)],
                             r=(wb,) + rq, w=(bbk,))
                    TT("dve", xT[:, oc, t0:t0 + n], xT[:, oc, t0:t0 + n], bk[:, 0:n], ALU.add, (bbk, B_x[bi][oc]), (B_x[bi][oc],))
        sb.reset(m)
        P.barrier()

    def final():
        m = sb.mark()
        wk = norm_work()
        yT = sb.alloc("yT", [128, KC, 512], F32); ByT = P.buf("yT")
        yst = [sb.alloc("yst%d" % i, [128, D], F32) for i in range(2)]
        Byst = [P.buf("yst%d" % i) for i in range(2)]
        ti = 0
        for bi, (t0, n) in enumerate(blocks):
            rmsnorm(bi, t0, n, GCOL_FIN, yT, ByT, wk)
            for tt in range(0, n, 128):
                rows = min(128, n - tt)
                s = ti % 2; ti += 1
                for f0 in range(0, KC, 4):
                    nf = min(4, KC - f0)
                    bank, bb = nbank()
                    pe_group([TR(bank[0:rows, k * 128:(k + 1) * 128], yT[:, f0 + k, tt:tt + rows], ident_f[:, :]) for k in range(nf)],
                             r=(ByT, B_const), w=(bb,))
                    CP("dve" if (f0 // 4) % 2 == 0 else "act", yst[s][0:rows, f0 * 128:(f0 + nf) * 128], bank[0:rows, 0:nf * 128], (bb,), (Byst[s],))
                DMA("sp", y_out[t0 + tt:t0 + tt + rows, :], yst[s][0:rows, :], (Byst[s],), (), "yst%d" % s)
        sb.reset(m)

    for l in range(c.DEPTH):
        if l + 1 < c.DEPTH:
            cast_layer_weights(l + 1)
        ffn(l, 0)
        if l < c.NA:
            if fl.get("ret", True): retention(l)
        else:
            if fl.get("sb", True): sb_layer(l)
        ffn(l, 1)
        ple(l)
        if l == c.NA - 1 and fl.get("kv", True):
            kv_phase()
    final()
    P.emit()
    return nc


def prep_inputs(cfg, inputs):
    c = cfg
    g_norm = np.asarray(inputs["g_norm"], np.float32)
    vecs = [g_norm[l, i] for l in range(c.DEPTH) for i in range(3)] + [np.asarray(inputs["g_ple"], np.float32)[l] for l in range(c.DEPTH)]
    vecs += [np.asarray(inputs["g_kv"], np.float32), np.asarray(inputs["g_final"], np.float32)]
    gvec = np.concatenate([v.reshape(c.KC, 128).T for v in vecs], axis=1).astype(np.float32)
    gvec = np.ascontiguousarray(gvec)
    shared = {"gvec": gvec}
    for k in ("w_ff1_gate", "w_ff1_up", "w_ff1_down", "w_ff2_gate", "w_ff2_up", "w_ff2_down", "w_ple_up", "w_ple_gate",
              "w_ret_in", "w_ret_out", "g_ret_gn", "w_kv", "w_sb_q", "w_sb_out", "b_sb"):
        shared[k] = np.ascontiguousarray(np.asarray(inputs[k], np.float32))
    shared["cache_k"] = np.asarray(inputs["cache_k"], np.float32).reshape(c.NPOOL * c.PAGE, c.D)
    shared["cache_v"] = np.asarray(inputs["cache_v"], np.float32).reshape(c.NPOOL * c.PAGE, c.D)
    xp = np.asarray(inputs["x_prompt"], np.float32); xs = np.asarray(inputs["x_sample"], np.float32)
    pp = np.asarray(inputs["p_prompt"], np.float32); ps = np.asarray(inputs["p_sample"], np.float32)
    st = np.asarray(inputs["state_ret"], np.float32)
    pt = np.asarray(inputs["page_table"], np.int32)
    maps = []
    for core in range(c.NC):
        seq, seg = core // c.CPS, core % c.CPS
        m = dict(shared)
        sl = slice(seg * c.SEG, (seg + 1) * c.SEG); bs = slice(core * c.SPC, (core + 1) * c.SPC)
        m["x"] = np.concatenate([xp[seq, sl], xs[bs].reshape(c.TS, c.D)], axis=0)
        m["p"] = np.concatenate([pp[:, seq, sl], ps[:, bs].reshape(c.DEPTH, c.TS, c.PLE)], axis=1)
        m["state_ret"] = np.ascontiguousarray(st[:, bs])
        m["page_table"] = np.ascontiguousarray(pt[bs])
        cons = host_constants(c, core)
        for nm in CONST_NAMES:
            m["c_" + nm] = cons[nm]
        maps.append(m)
    return maps


def assemble(cfg, results):
    c = cfg
    y_p = np.zeros((c.BATCH, c.CPS * c.SEG, c.D), np.float32)
    y_s = np.zeros((c.NC * c.SPC, c.DSEQ, c.D), np.float32)
    S_p = np.zeros((c.NA, c.BATCH, c.RH, c.DK, c.DV), np.float32)
    S_s = np.zeros((c.NA, c.NC * c.SPC, c.RH, c.DK, c.DV), np.float32)
    k_p = np.zeros((c.BATCH, c.CPS * c.SEG, c.SBH, c.HD), np.float32); v_p = np.zeros_like(k_p)
    k_s = np.zeros((c.NC * c.SPC, c.DSEQ, c.SBH, c.HD), np.float32); v_s = np.zeros_like(k_s)
    for core in range(c.NC):
        r = results[core]
        seq, seg = core // c.CPS, core % c.CPS
        sl = slice(seg * c.SEG, (seg + 1) * c.SEG); bs = slice(core * c.SPC, (core + 1) * c.SPC)
        y = np.asarray(r["y"]); kn = np.asarray(r["k_new"]); vn = np.asarray(r["v_new"])
        y_p[seq, sl] = y[:c.SEG]; y_s[bs] = y[c.SEG:].reshape(c.SPC, c.DSEQ, c.D)
        k_p[seq, sl] = kn[:c.SEG].reshape(c.SEG, c.SBH, c.HD); v_p[seq, sl] = vn[:c.SEG].reshape(c.SEG, c.SBH, c.HD)
        k_s[bs] = kn[c.SEG:].reshape(c.SPC, c.DSEQ, c.SBH, c.HD); v_s[bs] = vn[c.SEG:].reshape(c.SPC, c.DSEQ, c.SBH, c.HD)
        S_s[:, bs] = np.asarray(r["s_sample"])
        if seg == c.CPS - 1:
            S_p[:, seq] = np.asarray(r["s_prompt"])
    return (y_p, y_s, S_p, S_s, k_p, v_p, k_s, v_s)


_NC_CACHE = {}


def kernel(**inputs):
    cfg = Cfg()
    if "nc" not in _NC_CACHE:
        _NC_CACHE["nc"] = build(cfg)
    nc = _NC_CACHE["nc"]
    maps = prep_inputs(cfg, inputs)
    res = run_bass_kernel_spmd(nc, maps, core_ids=list(range(cfg.NC)))
    return assemble(cfg, res.results)
```
